# Optimizing a Trainium2 kernel written in Bass

```python
import jax, jax.numpy as jnp
from jax import lax
import numpy as np

D_MODEL = 1024
BATCH = 8
SEQ = 4096
DEPTH = 4

PLE_DIM = 256
N_BRANCHES = 3
POOL_WINDOWS = (2, 4, 8, 16)
POOL_GROUPS = 4
POOL_WIDTH = D_MODEL
POOL_GROUP_DIM = POOL_WIDTH // POOL_GROUPS
ATTN_GROUPS = ((128, 1), (512, 4), (2048, 16))
ATTN_HEADS_PER_GROUP = 4
ATTN_HEADS = ATTN_HEADS_PER_GROUP * len(ATTN_GROUPS)
ATTN_HEAD_DIM = 128
ATTN_WIDTH = ATTN_HEADS * ATTN_HEAD_DIM
ATTN_OUT_WIDTH = ATTN_HEADS_PER_GROUP * ATTN_HEAD_DIM
ALIBI_MAX_BIAS = 8.0
NEG_INF = -1e30
HGRN_HEADS = 8
HGRN_HEAD_DIM = 128
HGRN_WIDTH = HGRN_HEADS * HGRN_HEAD_DIM
HGRN_CHUNK = 32
FFN_HIDDEN = ((8 * D_MODEL + 3 * 256 - 1) // (3 * 256)) * 256
IN_SIZES = (POOL_WIDTH, ATTN_WIDTH, ATTN_WIDTH, ATTN_WIDTH,
            HGRN_WIDTH, HGRN_WIDTH, HGRN_WIDTH, HGRN_WIDTH, HGRN_WIDTH,
            N_BRANCHES * D_MODEL)
IN_WIDTH = sum(IN_SIZES)
DEEPNORM_ALPHA = (2 * DEPTH) ** 0.25
DEEPNORM_BETA = (8 * DEPTH) ** -0.25
LN_EPS = 1e-5
RMS_EPS = 1e-6

kernel_name = "hybrid_pool_dilattn_hgrn2_encoder"


def layer_norm(x, g, b):
    xf = x.astype(jnp.float32)
    mu = jnp.mean(xf, axis=-1, keepdims=True)
    var = jnp.mean(jnp.square(xf - mu), axis=-1, keepdims=True)
    y = (xf - mu) * lax.rsqrt(var + LN_EPS)
    return (y * g.astype(jnp.float32) + b.astype(jnp.float32)).astype(x.dtype)


def multiscale_pool(a, pool_w, pool_scale):
    B, S, _ = a.shape
    af = a.astype(jnp.float32).reshape(B, S, POOL_GROUPS, POOL_GROUP_DIM)
    csum = jnp.concatenate([jnp.zeros_like(af[:, :1]), jnp.cumsum(af, axis=1)], axis=1)
    pos = jnp.arange(S, dtype=jnp.int32)[:, None]
    half = jnp.asarray(POOL_WINDOWS, jnp.int32)[None, :] // 2
    lo = jnp.clip(pos - half, 0, S)
    hi = jnp.clip(pos + half, 0, S)
    grp = jnp.arange(POOL_GROUPS, dtype=jnp.int32)[None, :]
    wsum = csum[:, hi, grp] - csum[:, lo, grp]
    count = (hi - lo).astype(jnp.float32)[None, :, :, None]
    mixed = wsum / count - af
    y = jnp.einsum('bsgc,gcd->bsgd', mixed, pool_w.astype(jnp.float32))
    return (y.reshape(B, S, POOL_WIDTH) * pool_scale.astype(jnp.float32)).astype(a.dtype)


def dilated_band_attention(q, k, v, dilation, side, slopes):
    B, S, H, Dh = q.shape
    L = S // dilation
    nb = -(-L // side)
    Lp = nb * side

    def to_sub(t):
        return t.astype(jnp.float32).reshape(B, L, dilation, H, Dh).transpose(0, 2, 3, 1, 4)

    qs = jnp.pad(to_sub(q), ((0, 0), (0, 0), (0, 0), (0, Lp - L), (0, 0)))
    kv_pad = ((0, 0), (0, 0), (0, 0), (side, Lp - L + side), (0, 0))
    ks = jnp.pad(to_sub(k), kv_pad)
    vs = jnp.pad(to_sub(v), kv_pad)
    qb = qs.reshape(B, dilation, H, nb, side, Dh)

    def windows(t):
        tb = t.reshape(B, dilation, H, nb + 2, side, Dh)
        return jnp.concatenate([tb[:, :, :, :-2], tb[:, :, :, 1:-1], tb[:, :, :, 2:]], axis=4)

    kw, vw = windows(ks), windows(vs)
    scores = jnp.einsum('brhnqe,brhnke->brhnqk', qb, kw) * (Dh ** -0.5)
    r = jnp.arange(side, dtype=jnp.int32)[:, None]
    c = jnp.arange(3 * side, dtype=jnp.int32)[None, :]
    off = c - side - r
    kj = jnp.arange(nb, dtype=jnp.int32)[:, None, None] * side - side + c[None]
    valid = (jnp.abs(off)[None] <= side) & (kj >= 0) & (kj < L)
    dist = (jnp.abs(off) * dilation).astype(jnp.float32)
    bias = -slopes.astype(jnp.float32)[:, None, None, None] * dist[None, None]
    scores = jnp.where(valid, scores + bias, NEG_INF)
    lse = jax.nn.logsumexp(scores, axis=-1)
    probs = jnp.exp(scores - lse[..., None])
    o = jnp.einsum('brhnqk,brhnke->brhnqe', probs, vw)
    o = o.reshape(B, dilation, H, Lp, Dh)[:, :, :, :L].transpose(0, 3, 1, 2, 4).reshape(B, S, H, Dh)
    lse = lse.reshape(B, dilation, H, Lp)[..., :L].transpose(0, 3, 1, 2).reshape(B, S, H)
    return o, lse


def dilated_attention_mixer(q_raw, k_raw, v_raw):
    B, S, _ = q_raw.shape
    shp = (B, S, ATTN_HEADS, ATTN_HEAD_DIM)
    q, k, v = q_raw.reshape(shp), k_raw.reshape(shp), v_raw.reshape(shp)
    slopes = 2.0 ** (-ALIBI_MAX_BIAS * jnp.arange(1, ATTN_HEADS + 1, dtype=jnp.float32) / ATTN_HEADS)
    outs, lses = [], []
    for g, (window, dilation) in enumerate(ATTN_GROUPS):
        sl = slice(g * ATTN_HEADS_PER_GROUP, (g + 1) * ATTN_HEADS_PER_GROUP)
        o, l = dilated_band_attention(q[:, :, sl], k[:, :, sl], v[:, :, sl],
                                      dilation, window // (2 * dilation), slopes[sl])
        outs.append(o)
        lses.append(l)
    w = jax.nn.softmax(jnp.stack(lses, axis=0), axis=0)
    o = jnp.sum(w[..., None] * jnp.stack(outs, axis=0), axis=0)
    return o.reshape(B, S, ATTN_OUT_WIDTH).astype(q_raw.dtype)


def hgrn2_scan(q, k, v, log_f):
    B, S, H, Dk = q.shape
    Dv = v.shape[-1]
    C = HGRN_CHUNK
    N = S // C

    def chunks(t):
        return t.reshape(B, N, C, H, t.shape[-1]).transpose(1, 0, 3, 2, 4)

    qc, kc, vc, fc = chunks(q), chunks(k), chunks(v), chunks(log_f)
    b = jnp.cumsum(fc, axis=3)
    b_end = b[:, :, :, -1:]
    qe = qc * jnp.exp(b)
    kd = kc * jnp.exp(b_end - b)
    att = jnp.einsum('nbhte,nbhse->nbhts', qe, kc * jnp.exp(-b))
    lower = jnp.tril(jnp.ones((C, C), dtype=bool))
    o_intra = jnp.einsum('nbhts,nbhsv->nbhtv', jnp.where(lower, att, 0.0), vc)
    decay = jnp.exp(b_end[:, :, :, 0])

    def step(state, xs):
        qe_n, kd_n, v_n, decay_n = xs
        o_n = jnp.einsum('bhte,bhev->bhtv', qe_n, state)
        state = decay_n[..., None] * state + jnp.einsum('bhse,bhsv->bhev', kd_n, v_n)
        return state, o_n

    init = jnp.zeros((B, H, Dk, Dv), jnp.float32)
    _, o_inter = lax.scan(step, init, (qe, kd, vc, decay))
    o = o_intra + o_inter
    return o.transpose(1, 0, 3, 2, 4).reshape(B, S, H, Dv)


def hgrn2_bidirectional(q_raw, i_raw, f_fwd_raw, f_bwd_raw, g_raw, lb_fwd, lb_bwd, norm_w):
    B, S, _ = q_raw.shape

    def heads(t):
        return t.astype(jnp.float32).reshape(B, S, HGRN_HEADS, HGRN_HEAD_DIM)

    q = jax.nn.silu(heads(q_raw))
    v = heads(i_raw)

    def forget(raw, lb):
        lb = lb.reshape(HGRN_HEADS, HGRN_HEAD_DIM)
        z = heads(raw)
        f = lb + (1.0 - lb) * jax.nn.sigmoid(z)
        return (1.0 - lb) * jax.nn.sigmoid(-z), jnp.log(f)

    k_f, logf_f = forget(f_fwd_raw, lb_fwd)
    k_b, logf_b = forget(f_bwd_raw, lb_bwd)
    flip = lambda t: jnp.flip(t, axis=1)
    o = hgrn2_scan(q, k_f, v, logf_f) + flip(hgrn2_scan(flip(q), flip(k_b), flip(v), flip(logf_b)))
    o = o * lax.rsqrt(jnp.mean(jnp.square(o), axis=-1, keepdims=True) + RMS_EPS)
    o = o * norm_w.astype(jnp.float32).reshape(HGRN_HEADS, HGRN_HEAD_DIM) * jax.nn.silu(heads(g_raw))
    return o.reshape(B, S, HGRN_WIDTH).astype(q_raw.dtype)


def setup_inputs(seed: int = 0) -> dict:
    key = jax.random.key(seed)
    ks = jax.random.split(key, 20)
    f32 = jnp.float32

    def dense(k, shape, fan_in, scale=1.0):
        return jax.random.normal(k, shape, f32) * (scale * fan_in ** -0.5)

    def gain(k, shape):
        return 1.0 + 0.02 * jax.random.normal(k, shape, f32)

    def small(k, shape):
        return 0.02 * jax.random.normal(k, shape, f32)

    return {
        "x": jax.random.normal(ks[0], (BATCH, SEQ, D_MODEL), f32),
        "p": jax.random.normal(ks[1], (DEPTH, BATCH, SEQ, PLE_DIM), f32),
        "w_in": dense(ks[2], (DEPTH, D_MODEL, IN_WIDTH), D_MODEL),
        "pool_w": dense(ks[3], (DEPTH, POOL_GROUPS, POOL_GROUP_DIM, POOL_GROUP_DIM), POOL_GROUP_DIM),
        "pool_scale": gain(ks[4], (DEPTH, POOL_WIDTH)),
        "w_branch_a": dense(ks[5], (DEPTH, POOL_WIDTH, D_MODEL), POOL_WIDTH),
        "w_branch_b": dense(ks[6], (DEPTH, ATTN_OUT_WIDTH, D_MODEL), ATTN_OUT_WIDTH),
        "w_branch_c": dense(ks[7], (DEPTH, HGRN_WIDTH, D_MODEL), HGRN_WIDTH),
        "hgrn_lb_logits": 0.5 * jax.random.normal(ks[8], (DEPTH, 2 * HGRN_WIDTH), f32),
        "hgrn_norm_w": gain(ks[9], (DEPTH, HGRN_WIDTH)),
        "w_out": dense(ks[10], (DEPTH, D_MODEL, D_MODEL), D_MODEL, DEEPNORM_BETA),
        "ln1_g": gain(ks[11], (DEPTH, D_MODEL)),
        "ln1_b": small(ks[12], (DEPTH, D_MODEL)),
        "w_ffn_gate": dense(ks[13], (DEPTH, D_MODEL, FFN_HIDDEN), D_MODEL),
        "w_ffn_up": dense(ks[14], (DEPTH, D_MODEL, FFN_HIDDEN), D_MODEL),
        "w_ffn_down": dense(ks[15], (DEPTH, FFN_HIDDEN, D_MODEL), FFN_HIDDEN, DEEPNORM_BETA),
        "w_ple_proj": dense(ks[16], (DEPTH, PLE_DIM, D_MODEL), PLE_DIM, DEEPNORM_BETA),
        "w_ple_gate": dense(ks[17], (DEPTH, D_MODEL, D_MODEL), D_MODEL),
        "ln2_g": gain(ks[18], (DEPTH, D_MODEL)),
        "ln2_b": small(ks[19], (DEPTH, D_MODEL)),
    }


def reference(x, p, w_in, pool_w, pool_scale, w_branch_a, w_branch_b, w_branch_c,
              hgrn_lb_logits, hgrn_norm_w, w_out, ln1_g, ln1_b, w_ffn_gate, w_ffn_up,
              w_ffn_down, w_ple_proj, w_ple_gate, ln2_g, ln2_b):
    B, S, _ = x.shape
    lb = jnp.cumsum(jax.nn.softmax(hgrn_lb_logits.astype(jnp.float32), axis=0), axis=0)
    lb = lb - lb[:1]
    splits = np.cumsum(IN_SIZES)[:-1].tolist()
    for i in range(DEPTH):
        proj = jnp.einsum('bsd,de->bse', x, w_in[i])
        (a_in, q_att, k_att, v_att, q_h, i_h, ff_h, fb_h, g_h, gate_raw) = jnp.split(proj, splits, axis=-1)
        y_a = jnp.einsum('bsc,cd->bsd', multiscale_pool(a_in, pool_w[i], pool_scale[i]), w_branch_a[i])
        y_b = jnp.einsum('bsc,cd->bsd', dilated_attention_mixer(q_att, k_att, v_att), w_branch_b[i])
        y_c = jnp.einsum('bsc,cd->bsd',
                         hgrn2_bidirectional(q_h, i_h, ff_h, fb_h, g_h,
                                             lb[i, :HGRN_WIDTH], lb[i, HGRN_WIDTH:], hgrn_norm_w[i]),
                         w_branch_c[i])
        gates = jax.nn.sigmoid(gate_raw.astype(jnp.float32)).reshape(B, S, N_BRANCHES, D_MODEL)
        merged = (gates[:, :, 0] * y_a + gates[:, :, 1] * y_b + gates[:, :, 2] * y_c).astype(x.dtype)
        mix = jnp.einsum('bsd,de->bse', merged, w_out[i])
        x1 = layer_norm(DEEPNORM_ALPHA * x + mix, ln1_g[i], ln1_b[i])
        hidden = jax.nn.silu(jnp.einsum('bsd,df->bsf', x1, w_ffn_gate[i])) * jnp.einsum('bsd,df->bsf', x1, w_ffn_up[i])
        ffn = jnp.einsum('bsf,fd->bsd', hidden, w_ffn_down[i])
        ple = jnp.einsum('bsc,cd->bsd', p[i], w_ple_proj[i]) * jax.nn.sigmoid(jnp.einsum('bsd,de->bse', x1, w_ple_gate[i]))
        x = layer_norm(DEEPNORM_ALPHA * x1 + ffn + ple, ln2_g[i], ln2_b[i])
    return x
```

```python
import contextlib
import os
import numpy as np
import ml_dtypes
import concourse.bass as bass
import concourse.mybir as mybir
from concourse.bass_utils import run_bass_kernel_spmd

F32 = mybir.dt.float32
BF16 = mybir.dt.bfloat16
AF = mybir.ActivationFunctionType
ALU = mybir.AluOpType
PE, ACT, DVE, POOL, SP = "pe", "act", "dve", "pool", "sp"
ENGS = (PE, ACT, DVE, POOL, SP)
SEM_LIMIT = 24000
RELAX_N = int(os.environ.get('KRELAX', '256'))

S = 4096
D = 1024
DEPTH = 4
NCORES = 8
FF = 2816
NF = 22
INW = 13824
ALPHA = float((2 * DEPTH) ** 0.25)
LN_EPS = 1e-5
RMS_EPS = 1e-6
OFF_POOL, OFF_AQ, OFF_AK, OFF_AV = 0, 1024, 2560, 4096
OFF_HQ, OFF_HI, OFF_HFF, OFF_HFB, OFF_HG, OFF_GATE = 5632, 6656, 7680, 8704, 9728, 10752
ATT_D = (1, 4, 16)
CH = 64


def fsz(ap):
    n = 1
    for d in ap.shape[1:]:
        n *= int(d)
    return n


class Prog:
    def __init__(self, nc, same_engine_sync=True):
        self.nc = nc
        self.ops = []
        self.by_eng = {e: [] for e in ENGS}
        self.last_w = {}
        self.readers = {}
        self.same_engine_sync = same_engine_sync
        self.pending = {e: set() for e in ENGS}

    def op(self, eng, emit, reads=(), writes=(), dma=None, n=0):
        oid = len(self.ops)
        deps = set()
        for r in reads:
            w = self.last_w.get(r)
            if w is not None:
                deps.add(w)
            if (isinstance(r, tuple) and r[0] == "ps") or (isinstance(r, str) and r.startswith("psb")):
                rd = self.readers.get(r)
                if rd:
                    deps.update(rd.values())
        for r in writes:
            w = self.last_w.get(r)
            if w is not None:
                deps.add(w)
            rd = self.readers.get(r)
            if rd:
                deps.update(rd.values())
        if self.pending[eng]:
            deps.update(self.pending[eng])
            self.pending[eng] = set()
        keep = []
        for d in deps:
            o = self.ops[d]
            if o["dma"] is None and o["eng"] == eng:
                if eng == PE or not self.same_engine_sync or o["n"] >= RELAX_N:
                    continue
            keep.append(d)
            o["target"] = True
        rec = dict(id=oid, eng=eng, emit=emit, deps=keep, dma=dma, target=False, ev=None, n=n)
        self.ops.append(rec)
        self.by_eng[eng].append(rec)
        lane = ("dma", dma) if dma is not None else ("eng", eng)
        for r in reads:
            self.readers.setdefault(r, {})[lane] = oid
        for r in writes:
            self.last_w[r] = oid
            self.readers[r] = {}
        return oid

    def barrier(self):
        last = {}
        for o in self.ops:
            lane = ("dma", o["dma"]) if o["dma"] is not None else ("eng", o["eng"])
            last[lane] = o["id"]
        pre = set(last.values())
        for e in ENGS:
            self.pending[e] = set(pre)

    def finalize(self, stack):
        nc = self.nc
        sems = {}

        def get_sem(key):
            if key not in sems:
                sems[key] = stack.enter_context(nc.semaphore("s%d" % len(sems)))
            return sems[key]

        cnt = {e: 0 for e in ENGS}
        gen = {e: 0 for e in ENGS}
        dcnt = {}
        final = {}
        for o in self.ops:
            if o["dma"] is not None:
                k = ("dma", o["dma"])
                dcnt[k] = dcnt.get(k, 0) + 16
                o["ev"] = (get_sem(k), dcnt[k], k)
                final[k] = (o["ev"][0], dcnt[k])
            elif o["target"]:
                e = o["eng"]
                if cnt[e] >= SEM_LIMIT:
                    gen[e] += 1
                    cnt[e] = 0
                cnt[e] += 1
                k = ("eng", e, gen[e])
                o["ev"] = (get_sem(k), cnt[e], k)
        self.n_sems = len(sems)
        block = stack.enter_context(nc.Block())
        ops = self.ops
        by_eng = self.by_eng

        def run_stream(engname, eobj):
            known = {}
            for o in by_eng[engname]:
                for d in o["deps"]:
                    s, v, k = ops[d]["ev"]
                    if known.get(k, 0) >= v:
                        continue
                    known[k] = v
                    eobj.wait_ge(s, v)
                ins = o["emit"](eobj)
                if o["ev"] is not None:
                    s, v, k = o["ev"]
                    ins.then_inc(s, 16 if o["dma"] is not None else 1)
            if engname == SP:
                for k, (s, v) in final.items():
                    if known.get(k, 0) < v:
                        eobj.wait_ge(s, v)

        @block.tensor
        def _(e):
            run_stream(PE, e)

        @block.scalar
        def _(e):
            run_stream(ACT, e)

        @block.vector
        def _(e):
            run_stream(DVE, e)

        @block.gpsimd
        def _(e):
            run_stream(POOL, e)

        @block.sync
        def _(e):
            run_stream(SP, e)


NVEC_PER_L = 8 * 6
VEC_LB = DEPTH * NVEC_PER_L
NVEC = VEC_LB + DEPTH * 16


def _fm(v):
    return np.ascontiguousarray(v.reshape(-1, 128).T)


def make_consts():
    c = {}
    c["c_idf"] = np.eye(128, dtype=np.float32)
    c["c_onesD"] = np.full((128, 128), 1.0 / D, np.float32)
    reset = np.ones((128, 512), np.float32)
    reset[:, ::CH] = 0.0
    c["c_reset"] = reset
    s = np.arange(128)[:, None]
    t = np.arange(128)[None, :]
    same = (s // CH) == (t // CH)
    mF = (same & (s <= t)).astype(np.float32)
    mB = (same & (s >= t)).astype(np.float32)
    lo0 = np.ones((128, 128), np.float32)
    lo0[:64] = 0
    hi0 = np.ones((128, 128), np.float32)
    hi0[64:] = 0
    bfc = np.concatenate([np.eye(128, dtype=np.float32), np.ones((128, 128), np.float32), lo0, hi0, mF, mB], axis=1)
    c["c_bf"] = bfc.astype(ml_dtypes.bfloat16)
    slopes = 2.0 ** (-8.0 * np.arange(1, 13, dtype=np.float64) / 12)
    kp = np.arange(128)[:, None]
    qf = np.arange(128)[None, :]
    offA = kp - 64 - qf
    offB = kp + 64 - qf
    bias = np.zeros((12, 128, 256), np.float32)
    sq = np.sqrt(128.0)
    for h in range(12):
        d = ATT_D[h // 4]
        for j, off in enumerate((offA, offB)):
            b = -slopes[h] * d * np.abs(off) * sq
            b = np.where(np.abs(off) <= 64, b, -30000.0)
            bias[h, :, j * 128:(j + 1) * 128] = b
    c["c_abias"] = bias.astype(ml_dtypes.bfloat16)
    rc = np.zeros((128, 4, 16), np.float32)
    for g in range(4):
        h = 1 << g
        tt = np.concatenate([np.arange(8), np.arange(S - 8, S)])
        cntv = np.minimum(tt + h, S) - np.maximum(tt - h, 0)
        rc[:, g, :] = (1.0 / cntv)[None, :]
    c["c_rc"] = rc.reshape(128, 64)
    return c


def pack_vecs(inp):
    v = np.zeros((128, NVEC), np.float32)
    for l in range(DEPTH):
        for j, nm in enumerate(("pool_scale", "hgrn_norm_w", "ln1_g", "ln1_b", "ln2_g", "ln2_b")):
            v[:, l * NVEC_PER_L + j * 8:l * NVEC_PER_L + (j + 1) * 8] = _fm(np.asarray(inp[nm][l], np.float32))
        v[:, VEC_LB + l * 16:VEC_LB + (l + 1) * 16] = np.asarray(inp["hgrn_lb_logits"][l], np.float32).reshape(16, 128).T
    return v


class Builder:
    def __init__(self, n_layers=DEPTH, dbg=None, sync=True):
        self.n_layers = n_layers
        self.dbg = dbg
        self.nc = bass.Bass("TRN2", target_bir_lowering=False)
        self.P = Prog(self.nc, same_engine_sync=sync)
        self.stack = contextlib.ExitStack()
        self.wi = 0
        self.ri = 0
        self.pbi = 0

    def dram_in(self, name, shape, dt=F32):
        return self.nc.dram_tensor(name, list(shape), dt, kind="ExternalInput").ap()

    def sb(self, name, shape, dt):
        return self.stack.enter_context(self.nc.sbuf_tensor("sb_" + name, list(shape), dt))

    def declare(self):
        nc = self.nc
        self.x = self.dram_in("x", [S, D])
        self.p = self.dram_in("p", [DEPTH, S, 256])
        self.w_in = self.dram_in("w_in", [DEPTH, D, INW])
        self.pool_w = self.dram_in("pool_w", [DEPTH, 4, 256, 256])
        self.w_br = [self.dram_in("w_branch_a", [DEPTH, 1024, D]), self.dram_in("w_branch_b", [DEPTH, 512, D]),
                     self.dram_in("w_branch_c", [DEPTH, 1024, D])]
        self.w_out = self.dram_in("w_out", [DEPTH, D, D])
        self.w_fg = self.dram_in("w_ffn_gate", [DEPTH, D, FF])
        self.w_fu = self.dram_in("w_ffn_up", [DEPTH, D, FF])
        self.w_fd = self.dram_in("w_ffn_down", [DEPTH, FF, D])
        self.w_pp = self.dram_in("w_ple_proj", [DEPTH, 256, D])
        self.w_pg = self.dram_in("w_ple_gate", [DEPTH, D, D])
        self.vecs_d = self.dram_in("vecs", [128, NVEC])
        self.c_idf = self.dram_in("c_idf", [128, 128])
        self.c_onesD = self.dram_in("c_onesD", [128, 128])
        self.c_reset = self.dram_in("c_reset", [128, 512])
        self.c_bf = self.dram_in("c_bf", [128, 768], BF16)
        self.c_abias = self.dram_in("c_abias", [12, 128, 256], BF16)
        self.c_rc = self.dram_in("c_rc", [128, 64])
        self.out = nc.dram_tensor("out", [S, D], F32, kind="ExternalOutput").ap()
        if self.dbg:
            self.dbg_out = nc.dram_tensor("dbg", [20, 128, S], F32, kind="ExternalOutput").ap()
        self.XRES = nc.dram_tensor("XRES", [8, 128, S], F32, kind="Internal").ap()
        self.OUTX = nc.dram_tensor("OUTX", [20, 128, S], BF16, kind="Internal").ap()
        self.GY = nc.dram_tensor("GY", [3, 8, 128, S], BF16, kind="Internal").ap()
        self.HH = nc.dram_tensor("HH", [NF, 128, S], BF16, kind="Internal").ap()
        self.PLE = nc.dram_tensor("PLE", [8, 128, S], BF16, kind="Internal").ap()
        self.A = self.sb("A", [128, 8, S], BF16)
        self.AF_ = self.sb("arenaF", [128, 16384], F32)
        self.AB_ = self.sb("arenaB", [128, 35840], BF16)
        self.idf = self.sb("idf", [128, 128], F32)
        self.onesD = self.sb("onesD", [128, 128], F32)
        self.reset = self.sb("reset", [128, 512], F32)
        self.cbf = self.sb("cbf", [128, 768], BF16)
        self.rc = self.sb("rc", [128, 64], F32)
        self.vecs = self.sb("vecs", [128, NVEC], F32)
        self.lb = self.sb("lb", [128, DEPTH * 16], F32)
        self.oml = self.sb("oml", [128, DEPTH * 16], F32)
        self.lnoml = self.sb("lnoml", [128, DEPTH * 16], F32)
        self.misc = self.sb("misc", [128, 128], F32)
        self.ps = [self.stack.enter_context(nc.psum_tensor("ps%d" % i, [128, 512], F32)) for i in range(7)]
        self.psb = self.stack.enter_context(nc.psum_tensor("psb", [128, 1024], BF16))
        self.identb = self.cbf[:, 0:128]
        self.onesb = self.cbf[:, 128:256]
        self.lo0 = self.cbf[:, 256:384]
        self.hi0 = self.cbf[:, 384:512]
        self.maskF = self.cbf[:, 512:640]
        self.maskB = self.cbf[:, 640:768]
        self.wst = [self.AF_[:, i * 1024:(i + 1) * 1024].rearrange("p (k c) -> p k c", c=128) for i in range(2)]
        self.rst = [self.AF_[:, 2048 + i * 1024:2048 + (i + 1) * 1024] for i in range(2)]
        self.wbf = [self.AB_[:, i * 1024:(i + 1) * 1024].rearrange("p (k c) -> p k c", c=128) for i in range(4)]
        self.F0 = 4096
        self.B0 = 4096

    def fa(self, off, n):
        assert self.F0 + off + n <= 16384, (off, n)
        return self.AF_[:, self.F0 + off:self.F0 + off + n]

    def ba(self, off, n):
        assert self.B0 + off + n <= 35840, (off, n)
        return self.AB_[:, self.B0 + off:self.B0 + off + n]

    def dma(self, out, in_, reads, writes, key, eng=SP):
        self.P.op(eng, lambda e: e.dma_start(out=out, in_=in_), reads=reads, writes=writes, dma=key)

    def load_w(self, W2d, K, c0, ncols=128):
        s = self.wi % 2
        b = self.wi % 4
        self.wi += 1
        st = self.wst[s][:, 0:K, 0:ncols]
        wb = self.wbf[b][:, 0:K, 0:ncols]
        src = W2d[:, c0:c0 + ncols].rearrange("(k p) c -> p k c", p=128)
        self.dma(st, src, [], [("wst", s)], ("wst", s))
        self.P.op(POOL, lambda e: e.tensor_copy(out=wb, in_=st), reads=[("wst", s)], writes=[("wbf", b)])
        return self.wbf[b], ("wbf", b)

    def load_rows(self, Wrows, dest, dreg, ncols=1024):
        s = self.ri % 2
        self.ri += 1
        st = self.rst[s][:, 0:ncols]
        self.dma(st, Wrows, [], [("rst", s)], ("rst", s))
        self.P.op(POOL, lambda e: e.tensor_copy(out=dest, in_=st), reads=[("rst", s)], writes=[dreg])

    def mm(self, out, outreg, pairs, reads):
        n = len(pairs)

        def emit(e):
            for i, (l, r) in enumerate(pairs):
                ins = e.matmul(out, l, r, start=(i == 0), stop=(i == n - 1))
            return ins
        self.P.op(PE, emit, reads=reads, writes=[outreg])

    def act(self, out, in_, func, reads, writes, scale=None, bias=None):
        kw = {}
        if scale is not None:
            kw["scale"] = scale
        if bias is not None:
            kw["bias"] = bias
        self.P.op(ACT, lambda e: e.activation(out=out, in_=in_, func=func, **kw), reads=reads, writes=writes, n=fsz(out))

    def tt(self, eng, out, in0, in1, op, reads, writes):
        self.P.op(eng, lambda e: e.tensor_tensor(out=out, in0=in0, in1=in1, op=op), reads=reads, writes=writes, n=fsz(out))

    def ts(self, eng, out, in0, s1, s2, op0, op1, reads, writes):
        if s2 is None:
            self.P.op(eng, lambda e: e.tensor_scalar(out=out, in0=in0, scalar1=s1, scalar2=None, op0=op0), reads=reads, writes=writes, n=fsz(out))
        else:
            self.P.op(eng, lambda e: e.tensor_scalar(out=out, in0=in0, scalar1=s1, scalar2=s2, op0=op0, op1=op1), reads=reads, writes=writes, n=fsz(out))

    def stt(self, eng, out, in0, scalar, in1, op0, op1, reads, writes):
        self.P.op(eng, lambda e: e.scalar_tensor_tensor(out=out, in0=in0, scalar=scalar, in1=in1, op0=op0, op1=op1), reads=reads, writes=writes, n=fsz(out))

    def cp(self, eng, out, in_, reads, writes):
        self.P.op(eng, lambda e: e.tensor_copy(out=out, in_=in_), reads=reads, writes=writes, n=fsz(out))

    def memset(self, eng, ap, val, writes):
        self.P.op(eng, lambda e: e.memset(ap, val), writes=writes, n=fsz(ap))

    def vec(self, l, j, c):
        o = l * NVEC_PER_L + j * 8 + c
        return self.vecs[:, o:o + 1]

    def proj(self, bank, wb, wreg, t, K=8, src=None, sreg=None, ntok=512):
        pairs = []
        reads = [wreg]
        for k in range(K):
            pairs.append((wb[:, k, :], self.A[:, k, t * ntok:(t + 1) * ntok]))
            reads.append(("A", k, (t * ntok) // 512))
        self.mm(self.ps[bank][:, 0:ntok], ("ps", bank), pairs, reads)

    def prologue(self):
        P = self.P
        for nm, dst, src in (("idf", self.idf, self.c_idf), ("onesD", self.onesD, self.c_onesD), ("reset", self.reset, self.c_reset),
                             ("cbf", self.cbf, self.c_bf), ("rc", self.rc, self.c_rc), ("vecs", self.vecs, self.vecs_d)):
            self.dma(dst[:], src, [], [nm], nm)
        self.memset(DVE, self.misc[:], 0.0, ["misc"])
        self.memset(DVE, self.misc[:, 0:1], LN_EPS, ["misc"])
        self.memset(DVE, self.misc[:, 1:2], RMS_EPS, ["misc"])
        self.memset(DVE, self.misc[:, 2:3], 1.0, ["misc"])
        E = self.misc[:, 8:8 + 16 * DEPTH]
        self.act(E, self.vecs[:, VEC_LB:VEC_LB + 16 * DEPTH], AF.Exp, ["vecs"], ["misc"])
        ssum = self.lb[:, 0:16]
        self.tt(DVE, ssum, E[:, 0:16], E[:, 16:32], ALU.add, ["misc"], ["lb"])
        for l in range(2, DEPTH):
            self.tt(DVE, ssum, ssum, E[:, l * 16:(l + 1) * 16], ALU.add, ["misc", "lb"], ["lb"])
        self.P.op(DVE, lambda e: e.reciprocal(out=ssum, in_=ssum), reads=["lb"], writes=["lb"])
        for l in range(1, DEPTH):
            self.tt(DVE, E[:, l * 16:(l + 1) * 16], E[:, l * 16:(l + 1) * 16], ssum, ALU.mult, ["misc", "lb"], ["misc"])
        self.memset(DVE, self.lb[:, 0:16], 0.0, ["lb"])
        for l in range(1, DEPTH):
            self.tt(DVE, self.lb[:, l * 16:(l + 1) * 16], self.lb[:, (l - 1) * 16:l * 16], E[:, l * 16:(l + 1) * 16], ALU.add, ["misc", "lb"], ["lb"])
        self.ts(DVE, self.oml[:], self.lb[:], -1.0, 1.0, ALU.mult, ALU.add, ["lb"], ["oml"])
        self.act(self.lnoml[:], self.oml[:], AF.Ln, ["oml"], ["lnoml"])
        xin = [self.fa(i * 1024, 1024) for i in range(2)]
        xr = [self.fa(2048 + i * 1024, 1024).rearrange("p (c t) -> p c t", t=128) for i in range(2)]
        import os
        for tt_ in range(int(os.environ.get('KDBG_NX', '32'))):
            s = tt_ % 2
            self.dma(xin[s], self.x[tt_ * 128:(tt_ + 1) * 128, :], [], [("xin", s)], ("xin", s))
            for hf in range(2):
                bank = (tt_ * 2 + hf) % 4

                def emit(e, s=s, hf=hf, bank=bank):
                    for j in range(4):
                        c = hf * 4 + j
                        ins = e.transpose(out=self.ps[bank][:, j * 128:(j + 1) * 128], in_=xin[s][:, c * 128:(c + 1) * 128], identity=self.idf[:])
                    return ins
                P.op(PE, emit, reads=[("xin", s), "idf"], writes=[("ps", bank)])
                psv = self.ps[bank][:, :].rearrange("p (c t) -> p c t", t=128)
                self.cp(DVE, self.A[:, hf * 4:(hf + 1) * 4, tt_ * 128:(tt_ + 1) * 128], psv, [("ps", bank)], [("A", k, tt_ // 4) for k in range(hf * 4, hf * 4 + 4)])
                self.act(xr[s][:, hf * 4:(hf + 1) * 4, :], psv, AF.Copy, [("ps", bank)], [("xr", s)])
            dst = self.XRES.rearrange("c p t -> p c t")[:, :, tt_ * 128:(tt_ + 1) * 128]
            self.dma(dst, xr[s], [("xr", s)], [("XRES", tt_ // 4)], ("xrst", s))
        P.barrier()

    def stage_pool(self, L):
        P = self.P
        W = self.w_in[L]
        apad = self.fa(0, 4128)
        sa = self.fa(4128, 2112)
        sb_ = self.fa(4128 + 2112, 2112)
        tmpe = self.fa(4128 + 4224, 16)
        mixed = [self.ba(i * 4096, 4096) for i in range(2)]
        pwb = self.ba(8192, 512).rearrange("p (k c) -> p k c", c=256)
        pws = self.fa(4128 + 4224 + 16, 512).rearrange("p (k c) -> p k c", c=256)
        otile = [self.ba(8704 + i * 512, 512) for i in range(2)]
        self.memset(POOL, apad, 0.0, ["apad"])
        oi = 0
        for g in range(4):
            h = 1 << g
            for cc in range(2):
                c = 2 * g + cc
                wb, wreg = self.load_w(W, 8, OFF_POOL + c * 128)
                for t in range(8):
                    bank = t % 4
                    self.proj(bank, wb, wreg, t)
                    self.act(apad[:, 16 + t * 512:16 + (t + 1) * 512], self.ps[bank][:, :], AF.Copy, [("ps", bank)], ["apad"])
                for hf in range(2):
                    j0 = 16 + hf * 2048 - 16
                    a = apad[:, j0:j0 + 2080]
                    self.tt(DVE, sa[:, 1:2080], a[:, 0:2079], a[:, 1:2080], ALU.add, ["apad"], ["sa"])
                    fin = sa
                    if g >= 1:
                        self.tt(POOL, sb_[:, 2:2079], sa[:, 1:2078], sa[:, 3:2080], ALU.add, ["sa"], ["sb"])
                        fin = sb_
                    if g >= 2:
                        self.tt(DVE, sa[:, 4:2077], sb_[:, 2:2075], sb_[:, 6:2079], ALU.add, ["sb"], ["sa"])
                        fin = sa
                    if g >= 3:
                        self.tt(POOL, sb_[:, 8:2073], sa[:, 4:2069], sa[:, 12:2077], ALU.add, ["sa"], ["sb"])
                        fin = sb_
                    freg = "sa" if fin is sa else "sb"
                    mo = mixed[cc][:, hf * 2048:(hf + 1) * 2048]
                    self.stt(DVE, mo, fin[:, 16:2064], 1.0 / (2 * h), a[:, 16:2064], ALU.mult, ALU.subtract, [freg, "apad"], [("mixed", cc)])
                    e0 = 0 if hf == 0 else 8
                    u0 = 16 if hf == 0 else 2064 - 8
                    self.tt(DVE, tmpe[:, e0:e0 + 8], fin[:, u0:u0 + 8], self.rc[:, g * 16 + e0:g * 16 + e0 + 8], ALU.mult, [freg, "rc"], ["tmpe"])
                    t0 = 0 if hf == 0 else S - 8
                    self.tt(DVE, mixed[cc][:, t0:t0 + 8], tmpe[:, e0:e0 + 8], a[:, u0:u0 + 8], ALU.subtract, ["tmpe", "apad"], [("mixed", cc)])
            src = self.pool_w[L, g].rearrange("(k p) c -> p k c", p=128)
            self.dma(pws, src, [], ["pws"], "pws")
            self.cp(POOL, pwb, pws, ["pws"], ["pwb"])
            for cc in range(2):
                c = 2 * g + cc
                for t in range(8):
                    bank = 4 + (t % 2)
                    pairs = [(pwb[:, k, cc * 128:(cc + 1) * 128], mixed[k][:, t * 512:(t + 1) * 512]) for k in range(2)]
                    self.mm(self.ps[bank][:, :], ("ps", bank), pairs, ["pwb", ("mixed", 0), ("mixed", 1)])
                    o = oi % 2
                    oi += 1
                    self.ts(DVE, otile[o], self.ps[bank][:, :], self.vec(L, 0, c), None, ALU.mult, None, [("ps", bank), "vecs"], [("otile", o)])
                    self.dma(self.OUTX[c, :, t * 512:(t + 1) * 512], otile[o], [("otile", o)], [("OUTX", c, t)], ("otile", o))
        P.barrier()

    def stage_attn(self, L):
        P = self.P
        W = self.w_in[L]
        num = self.fa(0, 4096)
        den = self.fa(4096, 4096)
        QT = self.ba(0, 4096)
        KTp = self.ba(4096, 6144)
        VTp = self.ba(10240, 6144)
        Vtok = self.ba(16384, 6144).rearrange("p (n c) -> p n c", c=128)
        PT = [self.ba(22528 + i * 256, 256) for i in range(2)]
        bia = [self.ba(23040 + i * 256, 256) for i in range(2)]
        otile = self.ba(23552, 4096)
        bi = 0
        pi = 0
        for j in range(4):
            for g in range(3):
                h = 4 * g + j
                d = ATT_D[g]
                Lr = S // d
                Lp = Lr + 128
                nblk = Lr // 128
                b = bi % 2
                bi += 1
                self.dma(bia[b], self.c_abias[h], [], [("bia", b)], ("bia", b))
                self.memset(POOL, KTp, 0.0, ["KTp"])
                self.memset(POOL, VTp, 0.0, ["VTp"])
                wq, rq = self.load_w(W, 8, OFF_AQ + h * 128)
                wk, rk = self.load_w(W, 8, OFF_AK + h * 128)
                wv, rv = self.load_w(W, 8, OFF_AV + h * 128)
                n = 512 // d
                for t in range(8):
                    i0 = t * n
                    psv = lambda bank: self.ps[bank][:, :].rearrange("p (i r) -> p r i", r=d)
                    self.proj(0, wq, rq, t)
                    dq = QT[:, 0:d * Lr].rearrange("p (r l) -> p r l", r=d)[:, :, i0:i0 + n]
                    self.act(dq, psv(0), AF.Copy, [("ps", 0)], ["QT"])
                    self.proj(1, wk, rk, t)
                    dk = KTp[:, 0:d * Lp].rearrange("p (r l) -> p r l", r=d)[:, :, 64 + i0:64 + i0 + n]
                    self.cp(DVE, dk, psv(1), [("ps", 1)], ["KTp"])
                    self.proj(2, wv, rv, t)
                    dv = VTp[:, 0:d * Lp].rearrange("p (r l) -> p r l", r=d)[:, :, 64 + i0:64 + i0 + n]
                    self.act(dv, psv(2), AF.Copy, [("ps", 2)], ["VTp"])
                ntile = d * (nblk + 1)
                tiles = [(r, i) for r in range(d) for i in range(nblk + 1)]
                for q0 in range(0, ntile, 8):
                    grp = tiles[q0:q0 + 8]

                    def emit(e, grp=grp, Lp=Lp):
                        for jj, (r, i) in enumerate(grp):
                            ins = e.transpose(out=self.psb[:, jj * 128:(jj + 1) * 128], in_=VTp[:, r * Lp + 128 * i:r * Lp + 128 * i + 128], identity=self.identb)
                        return ins
                    P.op(PE, emit, reads=["VTp", "cbf"], writes=["psb"])
                    ng = len(grp)
                    self.cp(DVE, Vtok[:, q0:q0 + ng, :], self.psb[:, 0:ng * 128].rearrange("p (n c) -> p n c", c=128), ["psb"], ["Vtok"])
                gq = min(4, nblk)
                for r in range(d):
                    for m0 in range(0, nblk, gq):
                        for jb in range(gq):
                            m = m0 + jb
                            sbk = 3 + (pi % 2)
                            pt = PT[pi % 2]
                            ptr = ("PT", pi % 2)
                            pi += 1
                            for side in range(2):
                                kt = KTp[:, r * Lp + 128 * (m + side):r * Lp + 128 * (m + side) + 128]
                                qb = QT[:, r * Lr + 128 * m:r * Lr + 128 * m + 128]
                                pairs = [(kt, qb), (self.identb, bia[b][:, side * 128:(side + 1) * 128])]
                                self.mm(self.ps[sbk][:, side * 128:(side + 1) * 128], ("ps", sbk), pairs, ["KTp", "QT", "cbf", ("bia", b)])
                            self.act(pt, self.ps[sbk][:, 0:256], AF.Exp, [("ps", sbk)], [ptr], scale=float(128 ** -0.5))
                            pv = []
                            dn = []
                            for side in range(2):
                                ti = m + side
                                vt = Vtok[:, r * (nblk + 1) + ti, :]
                                val = self.lo0 if ti == 0 else (self.hi0 if ti == nblk else self.onesb)
                                pv.append((vt, pt[:, side * 128:(side + 1) * 128]))
                                dn.append((val, pt[:, side * 128:(side + 1) * 128]))
                            self.mm(self.ps[5][:, jb * 128:(jb + 1) * 128], ("ps", 5), pv, ["Vtok", ptr])
                            self.mm(self.ps[6][:, jb * 128:(jb + 1) * 128], ("ps", 6), dn, ["cbf", ptr])
                        nn = gq * 128
                        nv = num.rearrange("p (i r) -> p r i", r=d)[:, r, m0 * 128:m0 * 128 + nn]
                        dvv = den.rearrange("p (i r) -> p r i", r=d)[:, r, m0 * 128:m0 * 128 + nn]
                        if g == 0:
                            self.cp(DVE, nv, self.ps[5][:, 0:nn], [("ps", 5)], ["num"])
                            self.act(dvv, self.ps[6][:, 0:nn], AF.Copy, [("ps", 6)], ["den"])
                        else:
                            self.tt(DVE, nv, self.ps[5][:, 0:nn], nv, ALU.add, [("ps", 5), "num"], ["num"])
                            self.tt(DVE, dvv, self.ps[6][:, 0:nn], dvv, ALU.add, [("ps", 6), "den"], ["den"])
            P.op(DVE, lambda e: e.reciprocal(out=den, in_=den), reads=["den"], writes=["den"])
            self.tt(DVE, otile, num, den, ALU.mult, ["num", "den"], ["aotile"])
            self.dma(self.OUTX[8 + j, :, :], otile, ["aotile"], [("OUTX", 8 + j, t) for t in range(8)], "aotile")
        P.barrier()

    def stage_hgrn(self, L):
        P = self.P
        W = self.w_in[L]
        fofs = [0]
        bofs = [0]

        def falloc(n):
            a = self.fa(fofs[0], n)
            fofs[0] += n
            return a

        def balloc(n):
            a = self.ba(bofs[0], n)
            bofs[0] += n
            return a
        CB = []
        for dr in range(2):
            b = dict(qraw=falloc(512), zraw=falloc(512), sf=falloc(512), lg=falloc(512), Bn=falloc(512), bc=falloc(512), tq=falloc(512),
                     S32=falloc(9 * 128).rearrange("p (n c) -> p n c", c=128), DEC=falloc(128)[:, 0:8],
                     QE=balloc(512), KK=balloc(512), KDT=balloc(512), VT=balloc(512),
                     Vtok=balloc(512).rearrange("p (n c) -> p n c", c=128), KDtok=balloc(512).rearrange("p (n c) -> p n c", c=128),
                     attm=balloc(512).rearrange("p (n c) -> p n c", c=128), Sbf=balloc(9 * 128).rearrange("p (n c) -> p n c", c=128),
                     O=balloc(4096))
            b["S32_flat"] = b["S32"].rearrange("p n c -> p (n c)")
            b["Sbf_flat"] = b["Sbf"].rearrange("p n c -> p (n c)")
            CB.append(b)
        og = falloc(512)
        rs = falloc(512)
        gsl = falloc(512)
        osq = balloc(512)
        otl = [balloc(512) for _ in range(2)]
        wsl = [balloc(1024).rearrange("p (k c) -> p k c", c=128) for _ in range(5)]
        one_c = self.misc[:, 2:3]

        def chain(h, dr):
            b = CB[dr]
            R = lambda nm: (nm, dr)
            col = L * 16 + dr * 8 + h
            lbc = self.lb[:, col:col + 1]
            lnoml = self.lnoml[:, col:col + 1]
            mask = self.maskF if dr == 0 else self.maskB
            qraw, zraw, sf, lg, Bn, bc, tq, S32, DEC = b["qraw"], b["zraw"], b["sf"], b["lg"], b["Bn"], b["bc"], b["tq"], b["S32"], b["DEC"]
            QE, KK, KDT, VT, Vtok, KDtok, attm, Sbf, O = b["QE"], b["KK"], b["KDT"], b["VT"], b["Vtok"], b["KDtok"], b["attm"], b["Sbf"], b["O"]
            self.memset(DVE, S32[:, 8, :], 0.0, [R("S32")])
            self.memset(POOL, Sbf[:, 8, :], 0.0, [R("Sbf")])
            for t in (range(8) if dr == 0 else range(7, -1, -1)):
                self.proj(0, wsl[0], ("hw", 0), t)
                self.proj(1, wsl[1], ("hw", 1), t)
                self.proj(2, wsl[2 + dr], ("hw", 2 + dr), t)
                self.act(tq, self.ps[0][:, :], AF.Exp, [("ps", 0)], [R("tq")], scale=-1.0)
                self.cp(DVE, qraw, self.ps[0][:, :], [("ps", 0)], [R("qraw")])
                self.act(VT, self.ps[1][:, :], AF.Copy, [("ps", 1)], [R("VT")])
                self.act(sf, self.ps[2][:, :], AF.Exp, [("ps", 2)], [R("sf")], scale=-1.0)
                self.cp(DVE, zraw, self.ps[2][:, :], [("ps", 2)], [R("zraw")])

                def emitv(e):
                    for jj in range(4):
                        ins = e.transpose(out=self.psb[:, jj * 128:(jj + 1) * 128], in_=VT[:, jj * 128:(jj + 1) * 128], identity=self.identb)
                    return ins
                P.op(PE, emitv, reads=[R("VT"), "cbf"], writes=["psb"])
                self.cp(DVE, Vtok, self.psb[:, 0:512].rearrange("p (n c) -> p n c", c=128), ["psb"], [R("Vtok")])
                yield
                self.act(lg, sf, AF.Ln, [R("sf"), "lb", "misc"], [R("lg")], scale=lbc, bias=one_c)
                self.act(Bn, sf, AF.Ln, [R("sf"), "misc"], [R("Bn")], bias=one_c)
                self.act(tq, tq, AF.Ln, [R("tq"), "misc"], [R("tq")], bias=one_c)
                self.tt(DVE, lg, lg, Bn, ALU.subtract, [R("lg"), R("Bn")], [R("lg")])
                P.op(DVE, lambda e: e.tensor_tensor_scan(out=bc, data0=self.reset[:], data1=lg, initial=0.0, op0=ALU.mult, op1=ALU.add),
                     reads=[R("lg"), "reset"], writes=[R("bc")], n=512)
                bcv = bc.rearrange("p (n c) -> p n c", c=CH)
                if dr == 1:
                    self.tt(POOL, lg, lg, bc, ALU.subtract, [R("lg"), R("bc")], [R("lg")])
                    self.tt(DVE, bcv, lg.rearrange("p (n c) -> p n c", c=CH), bcv[:, :, CH - 1:CH].broadcast_to([128, 8, CH]), ALU.add, [R("lg"), R("bc")], [R("bc")])
                yield
                dsel = CH - 1 if dr == 0 else 0
                bend = bcv[:, :, dsel:dsel + 1]
                self.act(DEC.rearrange("p (n c) -> p n c", c=1), bend, AF.Exp, [R("bc")], [R("DEC")])
                self.tt(POOL, tq, bc, tq, ALU.subtract, [R("bc"), R("tq")], [R("tq")])
                self.act(tq, tq, AF.Exp, [R("tq")], [R("tq")])
                self.tt(DVE, QE, qraw, tq, ALU.mult, [R("qraw"), R("tq")], [R("QE")])
                self.tt(DVE, zraw, zraw, Bn, ALU.add, [R("zraw"), R("Bn")], [R("zraw")])
                self.tt(DVE, zraw, zraw, bc, ALU.add, [R("zraw"), R("bc")], [R("zraw")])
                self.act(KK, zraw, AF.Exp, [R("zraw"), "lnoml"], [R("KK")], scale=-1.0, bias=lnoml)
                self.tt(POOL, zraw.rearrange("p (n c) -> p n c", c=CH), zraw.rearrange("p (n c) -> p n c", c=CH),
                        bend.broadcast_to([128, 8, CH]), ALU.subtract, [R("zraw"), R("bc")], [R("zraw")])
                self.act(KDT, zraw, AF.Exp, [R("zraw"), "lnoml"], [R("KDT")], scale=-1.0, bias=lnoml)
                yield
                for blk in range(4):
                    bsl = slice(blk * 128, (blk + 1) * 128)
                    self.mm(self.ps[4][:, bsl], ("ps", 4), [(KK[:, bsl], QE[:, bsl])], [R("KK"), R("QE")])
                for blk in range(4):
                    bsl = slice(blk * 128, (blk + 1) * 128)
                    self.tt(DVE, attm[:, blk, :], self.ps[4][:, bsl], mask, ALU.mult, [("ps", 4), "cbf"], [R("attm")])

                def emitk(e):
                    for jj in range(4):
                        ins = e.transpose(out=self.psb[:, 512 + jj * 128:512 + (jj + 1) * 128], in_=KDT[:, jj * 128:(jj + 1) * 128], identity=self.identb)
                    return ins
                P.op(PE, emitk, reads=[R("KDT"), "cbf"], writes=["psb"])
                self.act(KDtok, self.psb[:, 512:1024].rearrange("p (n c) -> p n c", c=128), AF.Copy, ["psb"], [R("KDtok")])
                yield
                order = list(range(8)) if dr == 0 else list(range(7, -1, -1))
                for q, cpos in enumerate(order):
                    blk, c = cpos // 2, cpos % 2
                    pr = slice(c * CH, (c + 1) * CH)
                    bank = 5 + q % 2
                    self.mm(self.ps[bank][:, (q // 2) * 128:(q // 2 + 1) * 128], ("ps", bank), [(KDtok[pr, blk, :], Vtok[pr, blk, :])], [R("KDtok"), R("Vtok")])
                self.cp(DVE, S32[:, 0, :], S32[:, 8, :], [R("S32")], [R("S32")])
                self.cp(POOL, Sbf[:, 0, :], Sbf[:, 8, :], [R("Sbf")], [R("Sbf")])
                for q, cpos in enumerate(order):
                    bank = 5 + q % 2
                    self.stt(DVE, S32[:, q + 1, :], S32[:, q, :], DEC[:, cpos:cpos + 1], self.ps[bank][:, (q // 2) * 128:(q // 2 + 1) * 128],
                             ALU.mult, ALU.add, [R("S32"), R("DEC"), ("ps", bank)], [R("S32")])
                self.act(b["Sbf_flat"][:, 128:1152], b["S32_flat"][:, 128:1152], AF.Copy, [R("S32")], [R("Sbf")])
                yield
                for q, cpos in enumerate(order):
                    blk, c = cpos // 2, cpos % 2
                    cs = slice(cpos * CH, (cpos + 1) * CH)
                    pairs = [(Vtok[:, blk, :], attm[:, blk, c * CH:(c + 1) * CH]), (Sbf[:, q, :], QE[:, cs])]
                    self.mm(self.ps[3][:, cs], ("ps", 3), pairs, [R("Vtok"), R("attm"), R("Sbf"), R("QE")])
                tsl = slice(t * 512, (t + 1) * 512)
                self.act(O[:, tsl], self.ps[3][:, :], AF.Copy, [("ps", 3)], [("O", dr, t)])
                yield ("done", t)

        oi = [0]

        def finalize(h, t):
            tsl = slice(t * 512, (t + 1) * 512)
            self.proj(0, wsl[4], ("hw", 4), t)
            self.act(gsl, self.ps[0][:, :], AF.Exp, [("ps", 0)], ["gsl"], scale=-1.0)
            self.tt(DVE, og, CB[0]["O"][:, tsl], CB[1]["O"][:, tsl], ALU.add, [("O", 0, t), ("O", 1, t)], ["og"])
            self.act(osq, og, AF.Square, ["og"], ["osq"])
            self.mm(self.ps[1][:, :], ("ps", 1), [(self.onesb, osq)], ["cbf", "osq"])
            self.act(rs, self.ps[1][:, :], AF.Ln, [("ps", 1), "misc"], ["rs"], scale=1.0 / 128, bias=self.misc[:, 1:2])
            self.act(rs, rs, AF.Exp, ["rs"], ["rs"], scale=-0.5)
            self.act(gsl, gsl, AF.Ln, ["gsl", "misc"], ["gsl"], bias=one_c)
            self.act(gsl, gsl, AF.Exp, ["gsl"], ["gsl"], scale=-1.0)
            self.tt(DVE, og, og, rs, ALU.mult, ["og", "rs"], ["og"])
            self.tt(POOL, og, og, gsl, ALU.mult, ["og", "gsl"], ["og"])
            o = otl[oi[0] % 2]
            oreg = ("hotl", oi[0] % 2)
            oi[0] += 1
            self.stt(DVE, o, og, self.vec(L, 1, h), self.ps[0][:, :], ALU.mult, ALU.mult, ["og", ("ps", 0), "vecs"], [oreg])
            self.dma(self.OUTX[12 + h, :, tsl], o, [oreg], [("OUTX", 12 + h, t)], oreg)

        for h in range(8):
            for wi_, off in enumerate((OFF_HQ, OFF_HI, OFF_HFF, OFF_HFB, OFF_HG)):
                s = self.wi % 2
                self.wi += 1
                st = self.wst[s][:, 0:8, :]
                self.dma(st, W[:, off + h * 128:off + (h + 1) * 128].rearrange("(k p) c -> p k c", p=128), [], [("wst", s)], ("wst", s))
                self.cp(POOL, wsl[wi_], st, [("wst", s)], [("hw", wi_)])
            gens = [chain(h, 0), chain(h, 1)] if not os.environ.get('KDBG_ONECHAIN') else [chain(h, 0)]
            done = [set(), set()]
            alive = [True] * len(gens)
            for _ in range(int(os.environ.get('KSKEW', '3'))):
                next(gens[0])
            while any(alive):
                for gi, g in enumerate(gens):
                    if not alive[gi]:
                        continue
                    try:
                        r = next(g)
                    except StopIteration:
                        alive[gi] = False
                        continue
                    if r is not None:
                        t = r[1]
                        done[gi].add(t)
                        if len(gens) == 2 and t in done[1 - gi]:
                            finalize(h, t)
        P.barrier()

    def stage_merge(self, L, X):
        KX = (8, 4, 8)[X]
        c0 = (0, 8, 12)[X]
        Wb = self.ba(0, 8192).rearrange("p (k c) -> p k c", c=1024)
        Wg = self.ba(8192, 8192).rearrange("p (k c) -> p k c", c=1024)
        otl = [self.ba(16384 + i * 4096, 4096).rearrange("p (k t) -> p k t", t=512) for i in range(2)]
        gyt = [self.ba(24576 + i * 512, 512) for i in range(2)]
        sg = [self.fa(i * 512, 512) for i in range(2)]
        for k in range(KX):
            self.load_rows(self.w_br[X][L, k * 128:(k + 1) * 128, :], Wb[:, k, :], ("Wb", k))
        goff = OFF_GATE + X * 1024
        for k in range(8):
            self.load_rows(self.w_in[L, k * 128:(k + 1) * 128, goff:goff + 1024], Wg[:, k, :], ("Wg", k))
        gi = 0
        for t in range(8):
            o = t % 2
            tsl = slice(t * 512, (t + 1) * 512)
            self.dma(otl[o][:, 0:KX, :], self.OUTX[c0:c0 + KX, :, tsl].rearrange("c p t -> p c t"),
                     [("OUTX", c0 + k, t) for k in range(KX)], [("mo", o)], ("mo", o))
            for c in range(8):
                by, bg = (0, 1) if c % 2 == 0 else (2, 3)
                csl = slice(c * 128, (c + 1) * 128)
                self.mm(self.ps[by][:, :], ("ps", by), [(Wb[:, k, csl], otl[o][:, k, :]) for k in range(KX)], [("Wb", k) for k in range(KX)] + [("mo", o)])
                self.mm(self.ps[bg][:, :], ("ps", bg), [(Wg[:, k, csl], self.A[:, k, tsl]) for k in range(8)], [("Wg", k) for k in range(8)] + [("A", k, t) for k in range(8)])
                s = gi % 2
                gi += 1
                self.act(sg[s], self.ps[bg][:, :], AF.Sigmoid, [("ps", bg)], [("sg", s)])
                self.tt(DVE, gyt[s], self.ps[by][:, :], sg[s], ALU.mult, [("ps", by), ("sg", s)], [("gyt", s)])
                self.dma(self.GY[X, c, :, tsl], gyt[s], [("gyt", s)], [("GY", X, c, t)], ("gyt", s))
        self.P.barrier()

    def ln_tile(self, L, which, xn, xnreg, xsq, stat, xr, xreg, tok0, ntok):
        gj, bj = (2, 3) if which == 1 else (4, 5)
        mean, msq, rstd = stat
        self.act(xsq, xn, AF.Square, [xnreg], ["xsq"])
        self.mm(self.ps[4][:, 0:ntok], ("ps", 4), [(self.onesD[:], xn[:, c, :]) for c in range(8)], ["onesD", xnreg])
        self.mm(self.ps[5][:, 0:ntok], ("ps", 5), [(self.onesD[:], xsq[:, c, :]) for c in range(8)], ["onesD", "xsq"])
        self.cp(DVE, mean, self.ps[4][:, 0:ntok], [("ps", 4)], ["mean"])
        self.tt(DVE, msq, mean, mean, ALU.mult, ["mean"], ["msq"])
        self.tt(DVE, msq, self.ps[5][:, 0:ntok], msq, ALU.subtract, [("ps", 5), "msq"], ["msq"])
        self.act(rstd, msq, AF.Ln, ["msq", "misc"], ["rstd"], bias=self.misc[:, 0:1])
        self.act(rstd, rstd, AF.Exp, ["rstd"], ["rstd"], scale=-0.5)
        mb = mean.rearrange("p (o t) -> p o t", o=1).broadcast_to([128, 8, ntok])
        rb = rstd.rearrange("p (o t) -> p o t", o=1).broadcast_to([128, 8, ntok])
        self.tt(DVE, xn, xn, mb, ALU.subtract, [xnreg, "mean"], [xnreg])
        self.tt(DVE, xn, xn, rb, ALU.mult, [xnreg, "rstd"], [xnreg])
        for c in range(8):
            self.act(xr[:, c, :], xn[:, c, :], AF.Identity, [xnreg, "vecs"], [xreg], scale=self.vec(L, gj, c), bias=self.vec(L, bj, c))
        t512 = tok0 // 512
        self.cp(DVE, self.A[:, :, tok0:tok0 + ntok], xr, [xreg], [("A", k, t512) for k in range(8)])

    def stage_mix(self, L):
        NTK = 256
        NT_ = S // NTK
        wo = self.ba(0, 8192).rearrange("p (k c) -> p k c", c=1024)
        gy3 = self.ba(8192, 3 * 2048).rearrange("p (x k t) -> p x k t", x=3, k=8)
        mg = self.ba(8192 + 6144, 2048).rearrange("p (k t) -> p k t", t=NTK)
        xr = [self.fa(i * 2048, 2048).rearrange("p (k t) -> p k t", t=NTK) for i in range(2)]
        xn = [self.fa(4096 + i * 2048, 2048).rearrange("p (k t) -> p k t", t=NTK) for i in range(2)]
        xsq = self.fa(8192, 2048).rearrange("p (k t) -> p k t", t=NTK)
        stat = [self.fa(10240 + i * NTK, NTK) for i in range(3)]
        tmpm = self.AF_[:, 0:2048].rearrange("p (k t) -> p k t", t=NTK)
        for k in range(8):
            self.load_rows(self.w_out[L, k * 128:(k + 1) * 128, :], wo[:, k, :], ("wo", k))

        def front(t):
            sl = t % 2
            tsl = slice(t * NTK, (t + 1) * NTK)
            t5 = (t * NTK) // 512
            for X in range(3):
                self.dma(gy3[:, X], self.GY[X, :, :, tsl].rearrange("c p t -> p c t"), [("GY", X, c, t5) for c in range(8)], [("gy3", X)], ("gy3", X))
            self.dma(xr[sl], self.XRES[:, :, tsl].rearrange("c p t -> p c t"), [("XRES", t5)], [("xr", sl)], ("xr_ld", sl))
            self.tt(POOL, tmpm, gy3[:, 0], gy3[:, 1], ALU.add, [("gy3", 0), ("gy3", 1)], ["tmpm"])
            self.tt(POOL, mg, tmpm, gy3[:, 2], ALU.add, ["tmpm", ("gy3", 2)], ["mg"])
            for c in range(8):
                bank = c % 4
                self.mm(self.ps[bank][:, 0:NTK], ("ps", bank), [(wo[:, k, c * 128:(c + 1) * 128], mg[:, k, :]) for k in range(8)], [("wo", k) for k in range(8)] + ["mg"])
                self.stt(DVE, xn[sl][:, c, :], xr[sl][:, c, :], ALPHA, self.ps[bank][:, 0:NTK], ALU.mult, ALU.add, [("xr", sl), ("ps", bank)], [("xn", sl)])

        def back(t):
            sl = t % 2
            tsl = slice(t * NTK, (t + 1) * NTK)
            t5 = (t * NTK) // 512
            self.ln_tile(L, 1, xn[sl], ("xn", sl), xsq, stat, xr[sl], ("xr", sl), t * NTK, NTK)
            self.dma(self.XRES[:, :, tsl].rearrange("c p t -> p c t"), xr[sl], [("xr", sl)], [("XRES", t5)], ("xr_st", sl))

        front(0)
        for t in range(NT_):
            if t + 1 < NT_:
                front(t + 1)
            back(t)
        self.P.barrier()

    def stage_ffn_up(self, L):
        P = self.P
        ht = [self.ba(i * 512, 512) for i in range(2)]
        sgl = [self.fa(i * 512, 512) for i in range(2)]
        hi_ = 0
        for f in range(NF):
            wg, rg = self.load_w(self.w_fg[L], 8, f * 128)
            wu, ru = self.load_w(self.w_fu[L], 8, f * 128)
            for t in range(8):
                bg, bu = (0, 1) if t % 2 == 0 else (2, 3)
                self.proj(bg, wg, rg, t)
                self.proj(bu, wu, ru, t)
                s = hi_ % 2
                hi_ += 1
                self.act(sgl[s], self.ps[bg][:, :], AF.Silu, [("ps", bg)], [("sgl", s)])
                self.tt(DVE, ht[s], self.ps[bu][:, :], sgl[s], ALU.mult, [("ps", bu), ("sgl", s)], [("ht", s)])
                self.dma(self.HH[f, :, t * 512:(t + 1) * 512], ht[s], [("ht", s)], [("HH", f, t)], ("ht", s))
        PT = self.ba(1024, 8192).rearrange("p (k t) -> p k t", k=2)
        pin = [self.fa(1024 + i * 256, 256) for i in range(2)]
        for tt_ in range(32):
            s = tt_ % 2
            self.dma(pin[s], self.p[L, tt_ * 128:(tt_ + 1) * 128, :], [], [("pin", s)], ("pin", s))
            bank = 4 + s

            def emit(e, s=s, bank=bank):
                for j in range(2):
                    ins = e.transpose(out=self.ps[bank][:, j * 128:(j + 1) * 128], in_=pin[s][:, j * 128:(j + 1) * 128], identity=self.idf[:])
                return ins
            P.op(PE, emit, reads=[("pin", s), "idf"], writes=[("ps", bank)])
            self.cp(DVE, PT[:, :, tt_ * 128:(tt_ + 1) * 128], self.ps[bank][:, 0:256].rearrange("p (c t) -> p c t", t=128), [("ps", bank)], ["PT"])
        wppb = self.ba(9216, 256).rearrange("p (k c) -> p k c", c=128)
        wpps = self.fa(1536, 256).rearrange("p (k c) -> p k c", c=128)
        for c in range(8):
            self.dma(wpps, self.w_pp[L][:, c * 128:(c + 1) * 128].rearrange("(k p) c -> p k c", p=128), [], ["wpps"], "wpps")
            self.cp(POOL, wppb, wpps, ["wpps"], ["wppb"])
            wpg, rpg = self.load_w(self.w_pg[L], 8, c * 128)
            for t in range(8):
                bp, bg = (0, 1) if t % 2 == 0 else (2, 3)
                tsl = slice(t * 512, (t + 1) * 512)
                self.mm(self.ps[bp][:, :], ("ps", bp), [(wppb[:, k, :], PT[:, k, tsl]) for k in range(2)], ["wppb", "PT"])
                self.proj(bg, wpg, rpg, t)
                s = hi_ % 2
                hi_ += 1
                self.act(sgl[s], self.ps[bg][:, :], AF.Sigmoid, [("ps", bg)], [("sgl", s)])
                self.tt(DVE, ht[s], self.ps[bp][:, :], sgl[s], ALU.mult, [("ps", bp), ("sgl", s)], [("ht", s)])
                self.dma(self.PLE[c, :, tsl], ht[s], [("ht", s)], [("PLE", c, t)], ("ht", s))
        P.barrier()

    def stage_ffn_down(self, L, last):
        P = self.P
        NTK = 256
        NT_ = S // NTK
        Wd = self.ba(0, NF * 1024).rearrange("p (k c) -> p k c", c=1024)
        ht = [self.ba(NF * 1024, NF * NTK).rearrange("p (k t) -> p k t", t=NTK),
              self.AB_[:, 0:4096 + 0][:, 0:0]]
        ht[1] = None
        plt = self.ba(NF * 1024 + NF * NTK, 8 * NTK).rearrange("p (k t) -> p k t", t=NTK)
        xr = [self.fa(i * 2048, 2048).rearrange("p (k t) -> p k t", t=NTK) for i in range(2)]
        xn = [self.fa(4096 + i * 2048, 2048).rearrange("p (k t) -> p k t", t=NTK) for i in range(2)]
        xsq = self.fa(8192, 2048).rearrange("p (k t) -> p k t", t=NTK)
        stat = [self.fa(10240 + i * NTK, NTK) for i in range(3)]
        otok = self.AF_[:, 0:2048]
        for k in range(NF):
            self.load_rows(self.w_fd[L, k * 128:(k + 1) * 128, :], Wd[:, k, :], ("Wd", k))
        if last:
            P.barrier()

        def front(t):
            sl = t % 2
            tsl = slice(t * NTK, (t + 1) * NTK)
            t5 = (t * NTK) // 512
            self.dma(ht[0], self.HH[:, :, tsl].rearrange("c p t -> p c t"), [("HH", f, t5) for f in range(NF)], ["htd"], "htd")
            self.dma(plt, self.PLE[:, :, tsl].rearrange("c p t -> p c t"), [("PLE", c, t5) for c in range(8)], ["plt"], "plt")
            self.dma(xr[sl], self.XRES[:, :, tsl].rearrange("c p t -> p c t"), [("XRES", t5)], [("xr", sl)], ("xr_ld", sl))
            for c in range(8):
                bank = c % 4
                self.mm(self.ps[bank][:, 0:NTK], ("ps", bank), [(Wd[:, k, c * 128:(c + 1) * 128], ht[0][:, k, :]) for k in range(NF)], [("Wd", k) for k in range(NF)] + ["htd"])
                self.stt(DVE, xn[sl][:, c, :], xr[sl][:, c, :], ALPHA, self.ps[bank][:, 0:NTK], ALU.mult, ALU.add, [("xr", sl), ("ps", bank)], [("xn", sl)])
            self.tt(POOL, xn[sl], xn[sl], plt, ALU.add, [("xn", sl), "plt"], [("xn", sl)])

        def back(t):
            sl = t % 2
            tsl = slice(t * NTK, (t + 1) * NTK)
            t5 = (t * NTK) // 512
            self.ln_tile(L, 2, xn[sl], ("xn", sl), xsq, stat, xr[sl], ("xr", sl), t * NTK, NTK)
            if not last:
                self.dma(self.XRES[:, :, tsl].rearrange("c p t -> p c t"), xr[sl], [("xr", sl)], [("XRES", t5)], ("xr_st", sl))
            else:
                for tb in range(NTK // 128):
                    for hf in range(2):
                        bank = 6

                        def emit(e, tb=tb, hf=hf, bank=bank, sl=sl):
                            for j in range(4):
                                ins = e.transpose(out=self.ps[bank][:, j * 128:(j + 1) * 128], in_=xr[sl][:, hf * 4 + j, tb * 128:(tb + 1) * 128], identity=self.idf[:])
                            return ins
                        P.op(PE, emit, reads=[("xr", sl), "idf"], writes=[("ps", bank)])
                        self.act(otok[:, tb * 1024 + hf * 512:tb * 1024 + (hf + 1) * 512], self.ps[bank][:, :], AF.Copy, [("ps", bank)], [("otok", tb)])
                    self.dma(self.out[t * NTK + tb * 128:t * NTK + (tb + 1) * 128, :], otok[:, tb * 1024:(tb + 1) * 1024], [("otok", tb)], [("out", t, tb)], ("ost", tb))

        front(0)
        for t in range(NT_):
            if t + 1 < NT_:
                front(t + 1)
            back(t)
        P.barrier()

    def dump(self, slot, src_dram_bf16_rows):
        pass

    def build(self, stages=None):
        self.declare()
        with self.stack:
            self.prologue()
            for L in range(self.n_layers):
                last = L == self.n_layers - 1
                on = lambda nm: stages is None or nm in stages
                if on("pool"):
                    self.stage_pool(L)
                if on("merge0"):
                    self.stage_merge(L, 0)
                if on("attn"):
                    self.stage_attn(L)
                if on("merge1"):
                    self.stage_merge(L, 1)
                if on("hgrn"):
                    self.stage_hgrn(L)
                if on("merge2"):
                    self.stage_merge(L, 2)
                if on("mix"):
                    self.stage_mix(L)
                if on("ffn_up"):
                    self.stage_ffn_up(L)
                if on("ffn_down"):
                    self.stage_ffn_down(L, last)
            if self.dbg and not os.environ.get('KDBG_NODUMP'):
                self.dbg_dump()
            self.P.finalize(self.stack)
        return self.nc

    def dbg_dump(self):
        tb = self.ba(0, 4096)
        tf = self.fa(0, 4096)
        for c in range(20):
            self.dma(tb, self.OUTX[c], [("OUTX", c, t) for t in range(8)], ["dtb"], "dtb")
            self.cp(DVE, tf, tb, ["dtb"], ["dtf"])
            self.dma(self.dbg_out[c], tf, ["dtf"], [("dbg", c)], "dtf")


_CACHE = {}


def kernel(**inputs):
    consts = make_consts()
    vecs = pack_vecs(inputs)
    if "nc" not in _CACHE:
        _CACHE["nc"] = Builder().build()
    nc = _CACHE["nc"]
    shared = {k: np.ascontiguousarray(np.asarray(inputs[k], np.float32)) for k in (
        "w_in", "pool_w", "w_branch_a", "w_branch_b", "w_branch_c", "w_out", "w_ffn_gate", "w_ffn_up",
        "w_ffn_down", "w_ple_proj", "w_ple_gate")}
    shared.update(consts)
    shared["vecs"] = vecs
    x = np.asarray(inputs["x"], np.float32)
    p = np.asarray(inputs["p"], np.float32)
    in_maps = []
    for c in range(NCORES):
        m = dict(shared)
        m["x"] = np.ascontiguousarray(x[c])
        m["p"] = np.ascontiguousarray(p[:, c])
        in_maps.append(m)
    res = run_bass_kernel_spmd(nc, in_maps, core_ids=list(range(NCORES)))
    return np.stack([np.asarray(r["out"], np.float32) for r in res.results], axis=0)
```

```python
import contextlib
import os
import numpy as np
import ml_dtypes
import concourse.bass as bass
import concourse.mybir as mybir
from concourse.bass_utils import run_bass_kernel_spmd

F32 = mybir.dt.float32
BF16 = mybir.dt.bfloat16
AF = mybir.ActivationFunctionType
ALU = mybir.AluOpType
PE, ACT, DVE, POOL, SP = "pe", "act", "dve", "pool", "sp"
ENGS = (PE, ACT, DVE, POOL, SP)
SEM_LIMIT = 24000
RELAX_N = int(os.environ.get('KRELAX', '256'))

S = 4096
D = 1024
DEPTH = 4
NCORES = 8
FF = 2816
NF = 22
INW = 13824
ALPHA = float((2 * DEPTH) ** 0.25)
LN_EPS = 1e-5
RMS_EPS = 1e-6
OFF_POOL, OFF_AQ, OFF_AK, OFF_AV = 0, 1024, 2560, 4096
OFF_HQ, OFF_HI, OFF_HFF, OFF_HFB, OFF_HG, OFF_GATE = 5632, 6656, 7680, 8704, 9728, 10752
ATT_D = (1, 4, 16)
CH = 64


def fsz(ap):
    n = 1
    for d in ap.shape[1:]:
        n *= int(d)
    return n


class Prog:
    def __init__(self, nc, same_engine_sync=True):
        self.nc = nc
        self.ops = []
        self.by_eng = {e: [] for e in ENGS}
        self.last_w = {}
        self.readers = {}
        self.same_engine_sync = same_engine_sync
        self.pending = {e: set() for e in ENGS}

    def op(self, eng, emit, reads=(), writes=(), dma=None, n=0):
        oid = len(self.ops)
        deps = set()
        for r in reads:
            w = self.last_w.get(r)
            if w is not None:
                deps.add(w)
            if (isinstance(r, tuple) and r[0] == "ps") or (isinstance(r, str) and r.startswith("psb")):
                rd = self.readers.get(r)
                if rd:
                    deps.update(rd.values())
        for r in writes:
            w = self.last_w.get(r)
            if w is not None:
                deps.add(w)
            rd = self.readers.get(r)
            if rd:
                deps.update(rd.values())
        if self.pending[eng]:
            deps.update(self.pending[eng])
            self.pending[eng] = set()
        keep = []
        for d in deps:
            o = self.ops[d]
            if o["dma"] is None and o["eng"] == eng:
                if eng == PE or not self.same_engine_sync or o["n"] >= RELAX_N:
                    continue
            keep.append(d)
            o["target"] = True
        rec = dict(id=oid, eng=eng, emit=emit, deps=keep, dma=dma, target=False, ev=None, n=n)
        self.ops.append(rec)
        self.by_eng[eng].append(rec)
        lane = ("dma", dma) if dma is not None else ("eng", eng)
        for r in reads:
            self.readers.setdefault(r, {})[lane] = oid
        for r in writes:
            self.last_w[r] = oid
            self.readers[r] = {}
        return oid

    def barrier(self):
        last = {}
        for o in self.ops:
            lane = ("dma", o["dma"]) if o["dma"] is not None else ("eng", o["eng"])
            last[lane] = o["id"]
        pre = set(last.values())
        for e in ENGS:
            self.pending[e] = set(pre)

    def finalize(self, stack):
        nc = self.nc
        sems = {}

        def get_sem(key):
            if key not in sems:
                sems[key] = stack.enter_context(nc.semaphore("s%d" % len(sems)))
            return sems[key]

        cnt = {e: 0 for e in ENGS}
        gen = {e: 0 for e in ENGS}
        dcnt = {}
        final = {}
        for o in self.ops:
            if o["dma"] is not None:
                k = ("dma", o["dma"])
                dcnt[k] = dcnt.get(k, 0) + 16
                o["ev"] = (get_sem(k), dcnt[k], k)
                final[k] = (o["ev"][0], dcnt[k])
            elif o["target"]:
                e = o["eng"]
                if cnt[e] >= SEM_LIMIT:
                    gen[e] += 1
                    cnt[e] = 0
                cnt[e] += 1
                k = ("eng", e, gen[e])
                o["ev"] = (get_sem(k), cnt[e], k)
        self.n_sems = len(sems)
        block = stack.enter_context(nc.Block())
        ops = self.ops
        by_eng = self.by_eng

        def run_stream(engname, eobj):
            known = {}
            for o in by_eng[engname]:
                for d in o["deps"]:
                    s, v, k = ops[d]["ev"]
                    if known.get(k, 0) >= v:
                        continue
                    known[k] = v
                    eobj.wait_ge(s, v)
                ins = o["emit"](eobj)
                if o["ev"] is not None:
                    s, v, k = o["ev"]
                    ins.then_inc(s, 16 if o["dma"] is not None else 1)
            if engname == SP:
                for k, (s, v) in final.items():
                    if known.get(k, 0) < v:
                        eobj.wait_ge(s, v)

        @block.tensor
        def _(e):
            run_stream(PE, e)

        @block.scalar
        def _(e):
            run_stream(ACT, e)

        @block.vector
        def _(e):
            run_stream(DVE, e)

        @block.gpsimd
        def _(e):
            run_stream(POOL, e)

        @block.sync
        def _(e):
            run_stream(SP, e)


NVEC_PER_L = 8 * 6
VEC_LB = DEPTH * NVEC_PER_L
NVEC = VEC_LB + DEPTH * 16


def _fm(v):
    return np.ascontiguousarray(v.reshape(-1, 128).T)


def make_consts():
    c = {}
    c["c_idf"] = np.eye(128, dtype=np.float32)
    c["c_onesD"] = np.full((128, 128), 1.0 / D, np.float32)
    reset = np.ones((128, 512), np.float32)
    reset[:, ::CH] = 0.0
    c["c_reset"] = reset
    s = np.arange(128)[:, None]
    t = np.arange(128)[None, :]
    same = (s // CH) == (t // CH)
    mF = (same & (s <= t)).astype(np.float32)
    mB = (same & (s >= t)).astype(np.float32)
    lo0 = np.ones((128, 128), np.float32)
    lo0[:64] = 0
    hi0 = np.ones((128, 128), np.float32)
    hi0[64:] = 0
    bfc = np.concatenate([np.eye(128, dtype=np.float32), np.ones((128, 128), np.float32), lo0, hi0, mF, mB], axis=1)
    c["c_bf"] = bfc.astype(ml_dtypes.bfloat16)
    slopes = 2.0 ** (-8.0 * np.arange(1, 13, dtype=np.float64) / 12)
    kp = np.arange(128)[:, None]
    qf = np.arange(128)[None, :]
    offA = kp - 64 - qf
    offB = kp + 64 - qf
    bias = np.zeros((12, 128, 256), np.float32)
    sq = np.sqrt(128.0)
    for h in range(12):
        d = ATT_D[h // 4]
        for j, off in enumerate((offA, offB)):
            b = -slopes[h] * d * np.abs(off) * sq
            b = np.where(np.abs(off) <= 64, b, -30000.0)
            bias[h, :, j * 128:(j + 1) * 128] = b
    c["c_abias"] = bias.astype(ml_dtypes.bfloat16)
    rc = np.zeros((128, 4, 16), np.float32)
    for g in range(4):
        h = 1 << g
        tt = np.concatenate([np.arange(8), np.arange(S - 8, S)])
        cntv = np.minimum(tt + h, S) - np.maximum(tt - h, 0)
        rc[:, g, :] = (1.0 / cntv)[None, :]
    c["c_rc"] = rc.reshape(128, 64)
    return c


def pack_vecs(inp):
    v = np.zeros((128, NVEC), np.float32)
    for l in range(DEPTH):
        for j, nm in enumerate(("pool_scale", "hgrn_norm_w", "ln1_g", "ln1_b", "ln2_g", "ln2_b")):
            v[:, l * NVEC_PER_L + j * 8:l * NVEC_PER_L + (j + 1) * 8] = _fm(np.asarray(inp[nm][l], np.float32))
        v[:, VEC_LB + l * 16:VEC_LB + (l + 1) * 16] = np.asarray(inp["hgrn_lb_logits"][l], np.float32).reshape(16, 128).T
    return v


class Builder:
    def __init__(self, n_layers=DEPTH, dbg=None, sync=True):
        self.n_layers = n_layers
        self.dbg = dbg
        self.nc = bass.Bass("TRN2", target_bir_lowering=False)
        self.P = Prog(self.nc, same_engine_sync=sync)
        self.stack = contextlib.ExitStack()
        self.wi = 0
        self.ri = 0
        self.pbi = 0

    def dram_in(self, name, shape, dt=F32):
        return self.nc.dram_tensor(name, list(shape), dt, kind="ExternalInput").ap()

    def sb(self, name, shape, dt):
        return self.stack.enter_context(self.nc.sbuf_tensor("sb_" + name, list(shape), dt))

    def declare(self):
        nc = self.nc
        self.x = self.dram_in("x", [S, D])
        self.p = self.dram_in("p", [DEPTH, S, 256])
        self.w_in = self.dram_in("w_in", [DEPTH, D, INW])
        self.pool_w = self.dram_in("pool_w", [DEPTH, 4, 256, 256])
        self.w_br = [self.dram_in("w_branch_a", [DEPTH, 1024, D]), self.dram_in("w_branch_b", [DEPTH, 512, D]),
                     self.dram_in("w_branch_c", [DEPTH, 1024, D])]
        self.w_out = self.dram_in("w_out", [DEPTH, D, D])
        self.w_fg = self.dram_in("w_ffn_gate", [DEPTH, D, FF])
        self.w_fu = self.dram_in("w_ffn_up", [DEPTH, D, FF])
        self.w_fd = self.dram_in("w_ffn_down", [DEPTH, FF, D])
        self.w_pp = self.dram_in("w_ple_proj", [DEPTH, 256, D])
        self.w_pg = self.dram_in("w_ple_gate", [DEPTH, D, D])
        self.vecs_d = self.dram_in("vecs", [128, NVEC])
        self.c_idf = self.dram_in("c_idf", [128, 128])
        self.c_onesD = self.dram_in("c_onesD", [128, 128])
        self.c_reset = self.dram_in("c_reset", [128, 512])
        self.c_bf = self.dram_in("c_bf", [128, 768], BF16)
        self.c_abias = self.dram_in("c_abias", [12, 128, 256], BF16)
        self.c_rc = self.dram_in("c_rc", [128, 64])
        self.out = nc.dram_tensor("out", [S, D], F32, kind="ExternalOutput").ap()
        if self.dbg:
            self.dbg_out = nc.dram_tensor("dbg", [20, 128, S], F32, kind="ExternalOutput").ap()
        self.XRES = nc.dram_tensor("XRES", [8, 128, S], F32, kind="Internal").ap()
        self.OUTX = nc.dram_tensor("OUTX", [20, 128, S], BF16, kind="Internal").ap()
        self.GY = nc.dram_tensor("GY", [3, 8, 128, S], BF16, kind="Internal").ap()
        self.HH = nc.dram_tensor("HH", [NF, 128, S], BF16, kind="Internal").ap()
        self.PLE = nc.dram_tensor("PLE", [8, 128, S], BF16, kind="Internal").ap()
        self.A = self.sb("A", [128, 8, S], BF16)
        self.AF_ = self.sb("arenaF", [128, 16384], F32)
        self.AB_ = self.sb("arenaB", [128, 35840], BF16)
        self.idf = self.sb("idf", [128, 128], F32)
        self.onesD = self.sb("onesD", [128, 128], F32)
        self.reset = self.sb("reset", [128, 512], F32)
        self.cbf = self.sb("cbf", [128, 768], BF16)
        self.rc = self.sb("rc", [128, 64], F32)
        self.vecs = self.sb("vecs", [128, NVEC], F32)
        self.lb = self.sb("lb", [128, DEPTH * 16], F32)
        self.oml = self.sb("oml", [128, DEPTH * 16], F32)
        self.lnoml = self.sb("lnoml", [128, DEPTH * 16], F32)
        self.misc = self.sb("misc", [128, 128], F32)
        self.ps = [self.stack.enter_context(nc.psum_tensor("ps%d" % i, [128, 512], F32)) for i in range(7)]
        self.psb = self.stack.enter_context(nc.psum_tensor("psb", [128, 1024], BF16))
        self.identb = self.cbf[:, 0:128]
        self.onesb = self.cbf[:, 128:256]
        self.lo0 = self.cbf[:, 256:384]
        self.hi0 = self.cbf[:, 384:512]
        self.maskF = self.cbf[:, 512:640]
        self.maskB = self.cbf[:, 640:768]
        self.wst = [self.AF_[:, i * 1024:(i + 1) * 1024].rearrange("p (k c) -> p k c", c=128) for i in range(2)]
        self.rst = [self.AF_[:, 2048 + i * 1024:2048 + (i + 1) * 1024] for i in range(2)]
        self.wbf = [self.AB_[:, i * 1024:(i + 1) * 1024].rearrange("p (k c) -> p k c", c=128) for i in range(6)]
        self.F0 = 4096
        self.B0 = 6144

    def fa(self, off, n):
        assert self.F0 + off + n <= 16384, (off, n)
        return self.AF_[:, self.F0 + off:self.F0 + off + n]

    def ba(self, off, n):
        assert self.B0 + off + n <= 35840, (off, n)
        return self.AB_[:, self.B0 + off:self.B0 + off + n]

    def dma(self, out, in_, reads, writes, key, eng=SP):
        self.P.op(eng, lambda e: e.dma_start(out=out, in_=in_), reads=reads, writes=writes, dma=key)

    def load_w(self, W2d, K, c0, ncols=128):
        s = self.wi % 2
        b = self.wi % 6
        self.wi += 1
        st = self.wst[s][:, 0:K, 0:ncols]
        wb = self.wbf[b][:, 0:K, 0:ncols]
        src = W2d[:, c0:c0 + ncols].rearrange("(k p) c -> p k c", p=128)
        self.dma(st, src, [], [("wst", s)], ("wst", s))
        self.P.op(POOL, lambda e: e.tensor_copy(out=wb, in_=st), reads=[("wst", s)], writes=[("wbf", b)])
        return self.wbf[b], ("wbf", b)

    def load_rows(self, Wrows, dest, dreg, ncols=1024):
        s = self.ri % 2
        self.ri += 1
        st = self.rst[s][:, 0:ncols]
        self.dma(st, Wrows, [], [("rst", s)], ("rst", s))
        self.P.op(POOL, lambda e: e.tensor_copy(out=dest, in_=st), reads=[("rst", s)], writes=[dreg])

    def mm(self, out, outreg, pairs, reads):
        n = len(pairs)

        def emit(e):
            for i, (l, r) in enumerate(pairs):
                ins = e.matmul(out, l, r, start=(i == 0), stop=(i == n - 1))
            return ins
        self.P.op(PE, emit, reads=reads, writes=[outreg])

    def act(self, out, in_, func, reads, writes, scale=None, bias=None):
        kw = {}
        if scale is not None:
            kw["scale"] = scale
        if bias is not None:
            kw["bias"] = bias
        self.P.op(ACT, lambda e: e.activation(out=out, in_=in_, func=func, **kw), reads=reads, writes=writes, n=fsz(out))

    def tt(self, eng, out, in0, in1, op, reads, writes):
        self.P.op(eng, lambda e: e.tensor_tensor(out=out, in0=in0, in1=in1, op=op), reads=reads, writes=writes, n=fsz(out))

    def ts(self, eng, out, in0, s1, s2, op0, op1, reads, writes):
        if s2 is None:
            self.P.op(eng, lambda e: e.tensor_scalar(out=out, in0=in0, scalar1=s1, scalar2=None, op0=op0), reads=reads, writes=writes, n=fsz(out))
        else:
            self.P.op(eng, lambda e: e.tensor_scalar(out=out, in0=in0, scalar1=s1, scalar2=s2, op0=op0, op1=op1), reads=reads, writes=writes, n=fsz(out))

    def stt(self, eng, out, in0, scalar, in1, op0, op1, reads, writes):
        self.P.op(eng, lambda e: e.scalar_tensor_tensor(out=out, in0=in0, scalar=scalar, in1=in1, op0=op0, op1=op1), reads=reads, writes=writes, n=fsz(out))

    def cp(self, eng, out, in_, reads, writes):
        self.P.op(eng, lambda e: e.tensor_copy(out=out, in_=in_), reads=reads, writes=writes, n=fsz(out))

    def memset(self, eng, ap, val, writes):
        self.P.op(eng, lambda e: e.memset(ap, val), writes=writes, n=fsz(ap))

    def vec(self, l, j, c):
        o = l * NVEC_PER_L + j * 8 + c
        return self.vecs[:, o:o + 1]

    def proj(self, bank, wb, wreg, t, K=8, src=None, sreg=None, ntok=512):
        pairs = []
        reads = [wreg]
        for k in range(K):
            pairs.append((wb[:, k, :], self.A[:, k, t * ntok:(t + 1) * ntok]))
            reads.append(("A", k, (t * ntok) // 512))
        self.mm(self.ps[bank][:, 0:ntok], ("ps", bank), pairs, reads)

    def prologue(self):
        P = self.P
        for nm, dst, src in (("idf", self.idf, self.c_idf), ("onesD", self.onesD, self.c_onesD), ("reset", self.reset, self.c_reset),
                             ("cbf", self.cbf, self.c_bf), ("rc", self.rc, self.c_rc), ("vecs", self.vecs, self.vecs_d)):
            self.dma(dst[:], src, [], [nm], nm)
        self.memset(DVE, self.misc[:], 0.0, ["misc"])
        self.memset(DVE, self.misc[:, 0:1], LN_EPS, ["misc"])
        self.memset(DVE, self.misc[:, 1:2], RMS_EPS, ["misc"])
        self.memset(DVE, self.misc[:, 2:3], 1.0, ["misc"])
        E = self.misc[:, 8:8 + 16 * DEPTH]
        self.act(E, self.vecs[:, VEC_LB:VEC_LB + 16 * DEPTH], AF.Exp, ["vecs"], ["misc"])
        ssum = self.lb[:, 0:16]
        self.tt(DVE, ssum, E[:, 0:16], E[:, 16:32], ALU.add, ["misc"], ["lb"])
        for l in range(2, DEPTH):
            self.tt(DVE, ssum, ssum, E[:, l * 16:(l + 1) * 16], ALU.add, ["misc", "lb"], ["lb"])
        self.P.op(DVE, lambda e: e.reciprocal(out=ssum, in_=ssum), reads=["lb"], writes=["lb"])
        for l in range(1, DEPTH):
            self.tt(DVE, E[:, l * 16:(l + 1) * 16], E[:, l * 16:(l + 1) * 16], ssum, ALU.mult, ["misc", "lb"], ["misc"])
        self.memset(DVE, self.lb[:, 0:16], 0.0, ["lb"])
        for l in range(1, DEPTH):
            self.tt(DVE, self.lb[:, l * 16:(l + 1) * 16], self.lb[:, (l - 1) * 16:l * 16], E[:, l * 16:(l + 1) * 16], ALU.add, ["misc", "lb"], ["lb"])
        self.ts(DVE, self.oml[:], self.lb[:], -1.0, 1.0, ALU.mult, ALU.add, ["lb"], ["oml"])
        self.act(self.lnoml[:], self.oml[:], AF.Ln, ["oml"], ["lnoml"])
        xin = [self.fa(i * 1024, 1024) for i in range(2)]
        xr = [self.fa(2048 + i * 1024, 1024).rearrange("p (c t) -> p c t", t=128) for i in range(2)]
        import os
        for tt_ in range(int(os.environ.get('KDBG_NX', '32'))):
            s = tt_ % 2
            self.dma(xin[s], self.x[tt_ * 128:(tt_ + 1) * 128, :], [], [("xin", s)], ("xin", s))
            for hf in range(2):
                bank = (tt_ * 2 + hf) % 4

                def emit(e, s=s, hf=hf, bank=bank):
                    for j in range(4):
                        c = hf * 4 + j
                        ins = e.transpose(out=self.ps[bank][:, j * 128:(j + 1) * 128], in_=xin[s][:, c * 128:(c + 1) * 128], identity=self.idf[:])
                    return ins
                P.op(PE, emit, reads=[("xin", s), "idf"], writes=[("ps", bank)])
                psv = self.ps[bank][:, :].rearrange("p (c t) -> p c t", t=128)
                self.cp(DVE, self.A[:, hf * 4:(hf + 1) * 4, tt_ * 128:(tt_ + 1) * 128], psv, [("ps", bank)], [("A", k, tt_ // 4) for k in range(hf * 4, hf * 4 + 4)])
                self.act(xr[s][:, hf * 4:(hf + 1) * 4, :], psv, AF.Copy, [("ps", bank)], [("xr", s)])
            dst = self.XRES.rearrange("c p t -> p c t")[:, :, tt_ * 128:(tt_ + 1) * 128]
            self.dma(dst, xr[s], [("xr", s)], [("XRES", tt_ // 4)], ("xrst", s))
        P.barrier()

    def stage_pool(self, L):
        P = self.P
        W = self.w_in[L]
        apad = self.fa(0, 4128)
        sa = self.fa(4128, 2112)
        sb_ = self.fa(4128 + 2112, 2112)
        tmpe = self.fa(4128 + 4224, 16)
        mixed = [self.ba(i * 4096, 4096) for i in range(2)]
        pwb = self.ba(8192, 512).rearrange("p (k c) -> p k c", c=256)
        pws = self.fa(4128 + 4224 + 16, 512).rearrange("p (k c) -> p k c", c=256)
        otile = [self.ba(8704 + i * 512, 512) for i in range(2)]
        self.memset(POOL, apad, 0.0, ["apad"])
        oi = 0
        nxtw = self.load_w(W, 8, OFF_POOL)
        for g in range(4):
            h = 1 << g
            for cc in range(2):
                c = 2 * g + cc
                wb, wreg = nxtw
                if c + 1 < 8:
                    nxtw = self.load_w(W, 8, OFF_POOL + (c + 1) * 128)
                for t in range(8):
                    bank = t % 4
                    self.proj(bank, wb, wreg, t)
                    self.act(apad[:, 16 + t * 512:16 + (t + 1) * 512], self.ps[bank][:, :], AF.Copy, [("ps", bank)], ["apad"])
                for hf in range(2):
                    j0 = 16 + hf * 2048 - 16
                    a = apad[:, j0:j0 + 2080]
                    self.tt(DVE, sa[:, 1:2080], a[:, 0:2079], a[:, 1:2080], ALU.add, ["apad"], ["sa"])
                    fin = sa
                    if g >= 1:
                        self.tt(POOL, sb_[:, 2:2079], sa[:, 1:2078], sa[:, 3:2080], ALU.add, ["sa"], ["sb"])
                        fin = sb_
                    if g >= 2:
                        self.tt(DVE, sa[:, 4:2077], sb_[:, 2:2075], sb_[:, 6:2079], ALU.add, ["sb"], ["sa"])
                        fin = sa
                    if g >= 3:
                        self.tt(POOL, sb_[:, 8:2073], sa[:, 4:2069], sa[:, 12:2077], ALU.add, ["sa"], ["sb"])
                        fin = sb_
                    freg = "sa" if fin is sa else "sb"
                    mo = mixed[cc][:, hf * 2048:(hf + 1) * 2048]
                    self.stt(DVE, mo, fin[:, 16:2064], 1.0 / (2 * h), a[:, 16:2064], ALU.mult, ALU.subtract, [freg, "apad"], [("mixed", cc)])
                    e0 = 0 if hf == 0 else 8
                    u0 = 16 if hf == 0 else 2064 - 8
                    self.tt(DVE, tmpe[:, e0:e0 + 8], fin[:, u0:u0 + 8], self.rc[:, g * 16 + e0:g * 16 + e0 + 8], ALU.mult, [freg, "rc"], ["tmpe"])
                    t0 = 0 if hf == 0 else S - 8
                    self.tt(DVE, mixed[cc][:, t0:t0 + 8], tmpe[:, e0:e0 + 8], a[:, u0:u0 + 8], ALU.subtract, ["tmpe", "apad"], [("mixed", cc)])
            src = self.pool_w[L, g].rearrange("(k p) c -> p k c", p=128)
            self.dma(pws, src, [], ["pws"], "pws")
            self.cp(POOL, pwb, pws, ["pws"], ["pwb"])
            for cc in range(2):
                c = 2 * g + cc
                for t in range(8):
                    bank = 4 + (t % 2)
                    pairs = [(pwb[:, k, cc * 128:(cc + 1) * 128], mixed[k][:, t * 512:(t + 1) * 512]) for k in range(2)]
                    self.mm(self.ps[bank][:, :], ("ps", bank), pairs, ["pwb", ("mixed", 0), ("mixed", 1)])
                    o = oi % 2
                    oi += 1
                    self.ts(DVE, otile[o], self.ps[bank][:, :], self.vec(L, 0, c), None, ALU.mult, None, [("ps", bank), "vecs"], [("otile", o)])
                    self.dma(self.OUTX[c, :, t * 512:(t + 1) * 512], otile[o], [("otile", o)], [("OUTX", c, t)], ("otile", o))
        P.barrier()

    def stage_attn(self, L):
        P = self.P
        W = self.w_in[L]
        num = self.fa(0, 4096)
        den = self.fa(4096, 4096)
        QT = self.ba(0, 4096)
        KTp = self.ba(4096, 6144)
        VTp = self.ba(10240, 6144)
        Vtok = self.ba(16384, 6144).rearrange("p (n c) -> p n c", c=128)
        PT = [self.ba(22528 + i * 256, 256) for i in range(2)]
        bia = [self.ba(23040 + i * 256, 256) for i in range(2)]
        otile = self.ba(23552, 4096)
        bi = 0
        pi = 0
        for j in range(4):
            for g in range(3):
                h = 4 * g + j
                d = ATT_D[g]
                Lr = S // d
                Lp = Lr + 128
                nblk = Lr // 128
                b = bi % 2
                bi += 1
                self.dma(bia[b], self.c_abias[h], [], [("bia", b)], ("bia", b))
                self.memset(POOL, KTp, 0.0, ["KTp"])
                self.memset(POOL, VTp, 0.0, ["VTp"])
                if j == 0 and g == 0:
                    anxt = [self.load_w(W, 8, o_ + h * 128) for o_ in (OFF_AQ, OFF_AK, OFF_AV)]
                (wq, rq), (wk, rk), (wv, rv) = anxt
                hn = (4 * (g + 1) + j) if g < 2 else (j + 1 if j < 3 else None)
                if hn is not None:
                    anxt = [self.load_w(W, 8, o_ + hn * 128) for o_ in (OFF_AQ, OFF_AK, OFF_AV)]
                n = 512 // d
                for t in range(8):
                    i0 = t * n
                    psv = lambda bank: self.ps[bank][:, :].rearrange("p (i r) -> p r i", r=d)
                    self.proj(0, wq, rq, t)
                    dq = QT[:, 0:d * Lr].rearrange("p (r l) -> p r l", r=d)[:, :, i0:i0 + n]
                    self.act(dq, psv(0), AF.Copy, [("ps", 0)], ["QT"])
                    self.proj(1, wk, rk, t)
                    dk = KTp[:, 0:d * Lp].rearrange("p (r l) -> p r l", r=d)[:, :, 64 + i0:64 + i0 + n]
                    self.cp(DVE, dk, psv(1), [("ps", 1)], ["KTp"])
                    self.proj(2, wv, rv, t)
                    dv = VTp[:, 0:d * Lp].rearrange("p (r l) -> p r l", r=d)[:, :, 64 + i0:64 + i0 + n]
                    self.act(dv, psv(2), AF.Copy, [("ps", 2)], ["VTp"])
                ntile = d * (nblk + 1)
                tiles = [(r, i) for r in range(d) for i in range(nblk + 1)]
                for q0 in range(0, ntile, 8):
                    grp = tiles[q0:q0 + 8]

                    def emit(e, grp=grp, Lp=Lp):
                        for jj, (r, i) in enumerate(grp):
                            ins = e.transpose(out=self.psb[:, jj * 128:(jj + 1) * 128], in_=VTp[:, r * Lp + 128 * i:r * Lp + 128 * i + 128], identity=self.identb)
                        return ins
                    P.op(PE, emit, reads=["VTp", "cbf"], writes=["psb"])
                    ng = len(grp)
                    self.cp(DVE, Vtok[:, q0:q0 + ng, :], self.psb[:, 0:ng * 128].rearrange("p (n c) -> p n c", c=128), ["psb"], ["Vtok"])
                gq = min(4, nblk)
                blocks = [(r, m0, jb) for r in range(d) for m0 in range(0, nblk, gq) for jb in range(gq)]

                def scores(i, b=b, Lp=Lp, Lr=Lr):
                    r, m0, jb = blocks[i]
                    m = m0 + jb
                    sbk = 3 + (i % 2)
                    for side in range(2):
                        kt = KTp[:, r * Lp + 128 * (m + side):r * Lp + 128 * (m + side) + 128]
                        qb = QT[:, r * Lr + 128 * m:r * Lr + 128 * m + 128]
                        pairs = [(kt, qb), (self.identb, bia[b][:, side * 128:(side + 1) * 128])]
                        self.mm(self.ps[sbk][:, side * 128:(side + 1) * 128], ("ps", sbk), pairs, ["KTp", "QT", "cbf", ("bia", b)])
                    self.act(PT[i % 2], self.ps[sbk][:, 0:256], AF.Exp, [("ps", sbk)], [("PT", i % 2)], scale=float(128 ** -0.5))

                def pvden(i, g=g, d=d, nblk=nblk, gq=gq):
                    r, m0, jb = blocks[i]
                    m = m0 + jb
                    pt = PT[i % 2]
                    ptr = ("PT", i % 2)
                    pv = []
                    dn = []
                    for side in range(2):
                        ti = m + side
                        vt = Vtok[:, r * (nblk + 1) + ti, :]
                        val = self.lo0 if ti == 0 else (self.hi0 if ti == nblk else self.onesb)
                        pv.append((vt, pt[:, side * 128:(side + 1) * 128]))
                        dn.append((val, pt[:, side * 128:(side + 1) * 128]))
                    self.mm(self.ps[5][:, jb * 128:(jb + 1) * 128], ("ps", 5), pv, ["Vtok", ptr])
                    self.mm(self.ps[6][:, jb * 128:(jb + 1) * 128], ("ps", 6), dn, ["cbf", ptr])
                    if jb == gq - 1:
                        nn = gq * 128
                        nv = num.rearrange("p (i r) -> p r i", r=d)[:, r, m0 * 128:m0 * 128 + nn]
                        dvv = den.rearrange("p (i r) -> p r i", r=d)[:, r, m0 * 128:m0 * 128 + nn]
                        if g == 0:
                            self.cp(DVE, nv, self.ps[5][:, 0:nn], [("ps", 5)], ["num"])
                            self.act(dvv, self.ps[6][:, 0:nn], AF.Copy, [("ps", 6)], ["den"])
                        else:
                            self.tt(DVE, nv, self.ps[5][:, 0:nn], nv, ALU.add, [("ps", 5), "num"], ["num"])
                            self.tt(DVE, dvv, self.ps[6][:, 0:nn], dvv, ALU.add, [("ps", 6), "den"], ["den"])

                scores(0)
                for i in range(len(blocks)):
                    if i + 1 < len(blocks):
                        scores(i + 1)
                    pvden(i)
            P.op(DVE, lambda e: e.reciprocal(out=den, in_=den), reads=["den"], writes=["den"])
            self.tt(DVE, otile, num, den, ALU.mult, ["num", "den"], ["aotile"])
            self.dma(self.OUTX[8 + j, :, :], otile, ["aotile"], [("OUTX", 8 + j, t) for t in range(8)], "aotile")
        P.barrier()

    def stage_hgrn(self, L):
        P = self.P
        W = self.w_in[L]
        fofs = [0]
        bofs = [0]

        def falloc(n):
            a = self.fa(fofs[0], n)
            fofs[0] += n
            return a

        def balloc(n):
            a = self.ba(bofs[0], n)
            bofs[0] += n
            return a
        CB = []
        for dr in range(2):
            b = dict(qraw=falloc(512), zraw=falloc(512), sf=falloc(512), lg=falloc(512), Bn=falloc(512), bc=falloc(512), tq=falloc(512),
                     S32=falloc(9 * 128).rearrange("p (n c) -> p n c", c=128), DEC=falloc(128)[:, 0:8],
                     QE=balloc(512), KK=balloc(512), KDT=balloc(512), VT=balloc(512),
                     Vtok=balloc(512).rearrange("p (n c) -> p n c", c=128), KDtok=balloc(512).rearrange("p (n c) -> p n c", c=128),
                     attm=balloc(512).rearrange("p (n c) -> p n c", c=128), Sbf=balloc(9 * 128).rearrange("p (n c) -> p n c", c=128),
                     O=balloc(4096))
            b["S32_flat"] = b["S32"].rearrange("p n c -> p (n c)")
            b["Sbf_flat"] = b["Sbf"].rearrange("p n c -> p (n c)")
            CB.append(b)
        og = falloc(512)
        rs = falloc(512)
        gsl = falloc(512)
        osq = balloc(512)
        otl = [balloc(512) for _ in range(2)]
        wsl2 = [[balloc(1024).rearrange("p (k c) -> p k c", c=128) for _ in range(5)] for _ in range(2)]
        wsl = list(wsl2[0])
        hwset = [0]
        one_c = self.misc[:, 2:3]

        def chain(h, dr):
            b = CB[dr]
            R = lambda nm: (nm, dr)
            col = L * 16 + dr * 8 + h
            lbc = self.lb[:, col:col + 1]
            lnoml = self.lnoml[:, col:col + 1]
            mask = self.maskF if dr == 0 else self.maskB
            qraw, zraw, sf, lg, Bn, bc, tq, S32, DEC = b["qraw"], b["zraw"], b["sf"], b["lg"], b["Bn"], b["bc"], b["tq"], b["S32"], b["DEC"]
            QE, KK, KDT, VT, Vtok, KDtok, attm, Sbf, O = b["QE"], b["KK"], b["KDT"], b["VT"], b["Vtok"], b["KDtok"], b["attm"], b["Sbf"], b["O"]
            self.memset(DVE, S32[:, 8, :], 0.0, [R("S32")])
            self.memset(POOL, Sbf[:, 8, :], 0.0, [R("Sbf")])
            for t in (range(8) if dr == 0 else range(7, -1, -1)):
                hs = hwset[0]
                self.proj(0, wsl[0], ("hw", hs, 0), t)
                self.proj(1, wsl[1], ("hw", hs, 1), t)
                self.proj(2, wsl[2 + dr], ("hw", hs, 2 + dr), t)
                self.act(tq, self.ps[0][:, :], AF.Exp, [("ps", 0)], [R("tq")], scale=-1.0)
                self.cp(DVE, qraw, self.ps[0][:, :], [("ps", 0)], [R("qraw")])
                self.act(VT, self.ps[1][:, :], AF.Copy, [("ps", 1)], [R("VT")])
                self.act(sf, self.ps[2][:, :], AF.Exp, [("ps", 2)], [R("sf")], scale=-1.0)
                self.cp(DVE, zraw, self.ps[2][:, :], [("ps", 2)], [R("zraw")])

                def emitv(e):
                    for jj in range(4):
                        ins = e.transpose(out=self.psb[:, jj * 128:(jj + 1) * 128], in_=VT[:, jj * 128:(jj + 1) * 128], identity=self.identb)
                    return ins
                P.op(PE, emitv, reads=[R("VT"), "cbf"], writes=["psb"])
                self.cp(DVE, Vtok, self.psb[:, 0:512].rearrange("p (n c) -> p n c", c=128), ["psb"], [R("Vtok")])
                yield
                self.act(lg, sf, AF.Ln, [R("sf"), "lb", "misc"], [R("lg")], scale=lbc, bias=one_c)
                self.act(Bn, sf, AF.Ln, [R("sf"), "misc"], [R("Bn")], bias=one_c)
                self.act(tq, tq, AF.Ln, [R("tq"), "misc"], [R("tq")], bias=one_c)
                self.tt(DVE, lg, lg, Bn, ALU.subtract, [R("lg"), R("Bn")], [R("lg")])
                P.op(DVE, lambda e: e.tensor_tensor_scan(out=bc, data0=self.reset[:], data1=lg, initial=0.0, op0=ALU.mult, op1=ALU.add),
                     reads=[R("lg"), "reset"], writes=[R("bc")], n=512)
                bcv = bc.rearrange("p (n c) -> p n c", c=CH)
                if dr == 1:
                    self.tt(POOL, lg, lg, bc, ALU.subtract, [R("lg"), R("bc")], [R("lg")])
                    self.tt(DVE, bcv, lg.rearrange("p (n c) -> p n c", c=CH), bcv[:, :, CH - 1:CH].broadcast_to([128, 8, CH]), ALU.add, [R("lg"), R("bc")], [R("bc")])
                yield
                dsel = CH - 1 if dr == 0 else 0
                bend = bcv[:, :, dsel:dsel + 1]
                self.act(DEC.rearrange("p (n c) -> p n c", c=1), bend, AF.Exp, [R("bc")], [R("DEC")])
                self.tt(POOL, tq, bc, tq, ALU.subtract, [R("bc"), R("tq")], [R("tq")])
                self.act(tq, tq, AF.Exp, [R("tq")], [R("tq")])
                self.tt(DVE, QE, qraw, tq, ALU.mult, [R("qraw"), R("tq")], [R("QE")])
                self.tt(DVE, zraw, zraw, Bn, ALU.add, [R("zraw"), R("Bn")], [R("zraw")])
                self.tt(DVE, zraw, zraw, bc, ALU.add, [R("zraw"), R("bc")], [R("zraw")])
                self.act(KK, zraw, AF.Exp, [R("zraw"), "lnoml"], [R("KK")], scale=-1.0, bias=lnoml)
                self.tt(POOL, zraw.rearrange("p (n c) -> p n c", c=CH), zraw.rearrange("p (n c) -> p n c", c=CH),
                        bend.broadcast_to([128, 8, CH]), ALU.subtract, [R("zraw"), R("bc")], [R("zraw")])
                self.act(KDT, zraw, AF.Exp, [R("zraw"), "lnoml"], [R("KDT")], scale=-1.0, bias=lnoml)
                yield
                for blk in range(4):
                    bsl = slice(blk * 128, (blk + 1) * 128)
                    self.mm(self.ps[4][:, bsl], ("ps", 4), [(KK[:, bsl], QE[:, bsl])], [R("KK"), R("QE")])
                for blk in range(4):
                    bsl = slice(blk * 128, (blk + 1) * 128)
                    self.tt(DVE, attm[:, blk, :], self.ps[4][:, bsl], mask, ALU.mult, [("ps", 4), "cbf"], [R("attm")])

                def emitk(e):
                    for jj in range(4):
                        ins = e.transpose(out=self.psb[:, 512 + jj * 128:512 + (jj + 1) * 128], in_=KDT[:, jj * 128:(jj + 1) * 128], identity=self.identb)
                    return ins
                P.op(PE, emitk, reads=[R("KDT"), "cbf"], writes=["psb"])
                self.act(KDtok, self.psb[:, 512:1024].rearrange("p (n c) -> p n c", c=128), AF.Copy, ["psb"], [R("KDtok")])
                yield
                order = list(range(8)) if dr == 0 else list(range(7, -1, -1))
                for q, cpos in enumerate(order):
                    blk, c = cpos // 2, cpos % 2
                    pr = slice(c * CH, (c + 1) * CH)
                    bank = 5 + q % 2
                    self.mm(self.ps[bank][:, (q // 2) * 128:(q // 2 + 1) * 128], ("ps", bank), [(KDtok[pr, blk, :], Vtok[pr, blk, :])], [R("KDtok"), R("Vtok")])
                self.cp(DVE, S32[:, 0, :], S32[:, 8, :], [R("S32")], [R("S32")])
                self.cp(POOL, Sbf[:, 0, :], Sbf[:, 8, :], [R("Sbf")], [R("Sbf")])
                for q, cpos in enumerate(order):
                    bank = 5 + q % 2
                    self.stt(DVE, S32[:, q + 1, :], S32[:, q, :], DEC[:, cpos:cpos + 1], self.ps[bank][:, (q // 2) * 128:(q // 2 + 1) * 128],
                             ALU.mult, ALU.add, [R("S32"), R("DEC"), ("ps", bank)], [R("S32")])
                self.act(b["Sbf_flat"][:, 128:1152], b["S32_flat"][:, 128:1152], AF.Copy, [R("S32")], [R("Sbf")])
                yield
                for q, cpos in enumerate(order):
                    blk, c = cpos // 2, cpos % 2
                    cs = slice(cpos * CH, (cpos + 1) * CH)
                    pairs = [(Vtok[:, blk, :], attm[:, blk, c * CH:(c + 1) * CH]), (Sbf[:, q, :], QE[:, cs])]
                    self.mm(self.ps[3][:, cs], ("ps", 3), pairs, [R("Vtok"), R("attm"), R("Sbf"), R("QE")])
                tsl = slice(t * 512, (t + 1) * 512)
                self.act(O[:, tsl], self.ps[3][:, :], AF.Copy, [("ps", 3)], [("O", dr, t)])
                yield ("done", t)

        oi = [0]

        def finalize(h, t):
            tsl = slice(t * 512, (t + 1) * 512)
            self.proj(0, wsl[4], ("hw", hwset[0], 4), t)
            self.act(gsl, self.ps[0][:, :], AF.Exp, [("ps", 0)], ["gsl"], scale=-1.0)
            self.tt(DVE, og, CB[0]["O"][:, tsl], CB[1]["O"][:, tsl], ALU.add, [("O", 0, t), ("O", 1, t)], ["og"])
            self.act(osq, og, AF.Square, ["og"], ["osq"])
            self.mm(self.ps[1][:, :], ("ps", 1), [(self.onesb, osq)], ["cbf", "osq"])
            self.act(rs, self.ps[1][:, :], AF.Ln, [("ps", 1), "misc"], ["rs"], scale=1.0 / 128, bias=self.misc[:, 1:2])
            self.act(rs, rs, AF.Exp, ["rs"], ["rs"], scale=-0.5)
            self.act(gsl, gsl, AF.Ln, ["gsl", "misc"], ["gsl"], bias=one_c)
            self.act(gsl, gsl, AF.Exp, ["gsl"], ["gsl"], scale=-1.0)
            self.tt(DVE, og, og, rs, ALU.mult, ["og", "rs"], ["og"])
            self.tt(POOL, og, og, gsl, ALU.mult, ["og", "gsl"], ["og"])
            o = otl[oi[0] % 2]
            oreg = ("hotl", oi[0] % 2)
            oi[0] += 1
            self.stt(DVE, o, og, self.vec(L, 1, h), self.ps[0][:, :], ALU.mult, ALU.mult, ["og", ("ps", 0), "vecs"], [oreg])
            self.dma(self.OUTX[12 + h, :, tsl], o, [oreg], [("OUTX", 12 + h, t)], oreg)

        def load_hw(hh):
            st_ = hh % 2
            for wi_, off in enumerate((OFF_HQ, OFF_HI, OFF_HFF, OFF_HFB, OFF_HG)):
                s = self.wi % 2
                self.wi += 1
                st = self.wst[s][:, 0:8, :]
                self.dma(st, W[:, off + hh * 128:off + (hh + 1) * 128].rearrange("(k p) c -> p k c", p=128), [], [("wst", s)], ("wst", s))
                self.cp(POOL, wsl2[st_][wi_], st, [("wst", s)], [("hw", st_, wi_)])
        load_hw(0)
        for h in range(8):
            hwset[0] = h % 2
            wsl[:] = wsl2[h % 2]
            if h + 1 < 8:
                load_hw(h + 1)
            gens = [chain(h, 0), chain(h, 1)] if not os.environ.get('KDBG_ONECHAIN') else [chain(h, 0)]
            done = [set(), set()]
            alive = [True] * len(gens)
            for _ in range(int(os.environ.get('KSKEW', '3'))):
                next(gens[0])
            while any(alive):
                for gi, g in enumerate(gens):
                    if not alive[gi]:
                        continue
                    try:
                        r = next(g)
                    except StopIteration:
                        alive[gi] = False
                        continue
                    if r is not None:
                        t = r[1]
                        done[gi].add(t)
                        if len(gens) == 2 and t in done[1 - gi]:
                            finalize(h, t)
        P.barrier()

    def stage_merge(self, L, X):
        KX = (8, 4, 8)[X]
        c0 = (0, 8, 12)[X]
        Wb = self.ba(0, 8192).rearrange("p (k c) -> p k c", c=1024)
        Wg = self.ba(8192, 8192).rearrange("p (k c) -> p k c", c=1024)
        otl = [self.ba(16384 + i * 4096, 4096).rearrange("p (k t) -> p k t", t=512) for i in range(2)]
        gyt = [self.ba(24576 + i * 512, 512) for i in range(2)]
        sg = [self.fa(i * 512, 512) for i in range(2)]
        for k in range(KX):
            self.load_rows(self.w_br[X][L, k * 128:(k + 1) * 128, :], Wb[:, k, :], ("Wb", k))
        goff = OFF_GATE + X * 1024
        for k in range(8):
            self.load_rows(self.w_in[L, k * 128:(k + 1) * 128, goff:goff + 1024], Wg[:, k, :], ("Wg", k))
        gi = 0
        for t in range(8):
            o = t % 2
            tsl = slice(t * 512, (t + 1) * 512)
            self.dma(otl[o][:, 0:KX, :], self.OUTX[c0:c0 + KX, :, tsl].rearrange("c p t -> p c t"),
                     [("OUTX", c0 + k, t) for k in range(KX)], [("mo", o)], ("mo", o))
            for c in range(8):
                by, bg = (0, 1) if c % 2 == 0 else (2, 3)
                csl = slice(c * 128, (c + 1) * 128)
                self.mm(self.ps[by][:, :], ("ps", by), [(Wb[:, k, csl], otl[o][:, k, :]) for k in range(KX)], [("Wb", k) for k in range(KX)] + [("mo", o)])
                self.mm(self.ps[bg][:, :], ("ps", bg), [(Wg[:, k, csl], self.A[:, k, tsl]) for k in range(8)], [("Wg", k) for k in range(8)] + [("A", k, t) for k in range(8)])
                s = gi % 2
                gi += 1
                self.act(sg[s], self.ps[bg][:, :], AF.Sigmoid, [("ps", bg)], [("sg", s)])
                self.tt(DVE, gyt[s], self.ps[by][:, :], sg[s], ALU.mult, [("ps", by), ("sg", s)], [("gyt", s)])
                self.dma(self.GY[X, c, :, tsl], gyt[s], [("gyt", s)], [("GY", X, c, t)], ("gyt", s))
        self.P.barrier()

    def ln_tile(self, L, which, xn, xnreg, xsq, stat, xr, xreg, tok0, ntok):
        gj, bj = (2, 3) if which == 1 else (4, 5)
        mean, msq, rstd = stat
        self.act(xsq, xn, AF.Square, [xnreg], ["xsq"])
        self.mm(self.ps[4][:, 0:ntok], ("ps", 4), [(self.onesD[:], xn[:, c, :]) for c in range(8)], ["onesD", xnreg])
        self.mm(self.ps[5][:, 0:ntok], ("ps", 5), [(self.onesD[:], xsq[:, c, :]) for c in range(8)], ["onesD", "xsq"])
        self.cp(DVE, mean, self.ps[4][:, 0:ntok], [("ps", 4)], ["mean"])
        self.tt(DVE, msq, mean, mean, ALU.mult, ["mean"], ["msq"])
        self.tt(DVE, msq, self.ps[5][:, 0:ntok], msq, ALU.subtract, [("ps", 5), "msq"], ["msq"])
        self.act(rstd, msq, AF.Ln, ["msq", "misc"], ["rstd"], bias=self.misc[:, 0:1])
        self.act(rstd, rstd, AF.Exp, ["rstd"], ["rstd"], scale=-0.5)
        mb = mean.rearrange("p (o t) -> p o t", o=1).broadcast_to([128, 8, ntok])
        rb = rstd.rearrange("p (o t) -> p o t", o=1).broadcast_to([128, 8, ntok])
        self.tt(DVE, xn, xn, mb, ALU.subtract, [xnreg, "mean"], [xnreg])
        self.tt(DVE, xn, xn, rb, ALU.mult, [xnreg, "rstd"], [xnreg])
        for c in range(8):
            self.act(xr[:, c, :], xn[:, c, :], AF.Identity, [xnreg, "vecs"], [xreg], scale=self.vec(L, gj, c), bias=self.vec(L, bj, c))
        t512 = tok0 // 512
        self.cp(DVE, self.A[:, :, tok0:tok0 + ntok], xr, [xreg], [("A", k, t512) for k in range(8)])

    def stage_mix(self, L):
        NTK = 256
        NT_ = S // NTK
        wo = self.ba(0, 8192).rearrange("p (k c) -> p k c", c=1024)
        gy3 = self.ba(8192, 3 * 2048).rearrange("p (x k t) -> p x k t", x=3, k=8)
        mg = self.ba(8192 + 6144, 2048).rearrange("p (k t) -> p k t", t=NTK)
        xr = [self.fa(i * 2048, 2048).rearrange("p (k t) -> p k t", t=NTK) for i in range(2)]
        xn = [self.fa(4096 + i * 2048, 2048).rearrange("p (k t) -> p k t", t=NTK) for i in range(2)]
        xsq = self.fa(8192, 2048).rearrange("p (k t) -> p k t", t=NTK)
        stat = [self.fa(10240 + i * NTK, NTK) for i in range(3)]
        tmpm = self.AF_[:, 0:2048].rearrange("p (k t) -> p k t", t=NTK)
        for k in range(8):
            self.load_rows(self.w_out[L, k * 128:(k + 1) * 128, :], wo[:, k, :], ("wo", k))

        def front(t):
            sl = t % 2
            tsl = slice(t * NTK, (t + 1) * NTK)
            t5 = (t * NTK) // 512
            for X in range(3):
                self.dma(gy3[:, X], self.GY[X, :, :, tsl].rearrange("c p t -> p c t"), [("GY", X, c, t5) for c in range(8)], [("gy3", X)], ("gy3", X))
            self.dma(xr[sl], self.XRES[:, :, tsl].rearrange("c p t -> p c t"), [("XRES", t5)], [("xr", sl)], ("xr_ld", sl))
            self.tt(POOL, tmpm, gy3[:, 0], gy3[:, 1], ALU.add, [("gy3", 0), ("gy3", 1)], ["tmpm"])
            self.tt(POOL, mg, tmpm, gy3[:, 2], ALU.add, ["tmpm", ("gy3", 2)], ["mg"])
            for c in range(8):
                bank = c % 4
                self.mm(self.ps[bank][:, 0:NTK], ("ps", bank), [(wo[:, k, c * 128:(c + 1) * 128], mg[:, k, :]) for k in range(8)], [("wo", k) for k in range(8)] + ["mg"])
                self.stt(DVE, xn[sl][:, c, :], xr[sl][:, c, :], ALPHA, self.ps[bank][:, 0:NTK], ALU.mult, ALU.add, [("xr", sl), ("ps", bank)], [("xn", sl)])

        def back(t):
            sl = t % 2
            tsl = slice(t * NTK, (t + 1) * NTK)
            t5 = (t * NTK) // 512
            self.ln_tile(L, 1, xn[sl], ("xn", sl), xsq, stat, xr[sl], ("xr", sl), t * NTK, NTK)
            self.dma(self.XRES[:, :, tsl].rearrange("c p t -> p c t"), xr[sl], [("xr", sl)], [("XRES", t5)], ("xr_st", sl))

        front(0)
        for t in range(NT_):
            if t + 1 < NT_:
                front(t + 1)
            back(t)
        self.P.barrier()

    def stage_ffn_up(self, L):
        P = self.P
        ht = [self.ba(i * 512, 512) for i in range(2)]
        sgl = [self.fa(i * 512, 512) for i in range(2)]
        hi_ = 0
        nxt = (self.load_w(self.w_fg[L], 8, 0), self.load_w(self.w_fu[L], 8, 0))
        for f in range(NF):
            (wg, rg), (wu, ru) = nxt
            if f + 1 < NF:
                nxt = (self.load_w(self.w_fg[L], 8, (f + 1) * 128), self.load_w(self.w_fu[L], 8, (f + 1) * 128))
            for t in range(8):
                bg, bu = (0, 1) if t % 2 == 0 else (2, 3)
                self.proj(bg, wg, rg, t)
                self.proj(bu, wu, ru, t)
                s = hi_ % 2
                hi_ += 1
                self.act(sgl[s], self.ps[bg][:, :], AF.Silu, [("ps", bg)], [("sgl", s)])
                self.tt(DVE, ht[s], self.ps[bu][:, :], sgl[s], ALU.mult, [("ps", bu), ("sgl", s)], [("ht", s)])
                self.dma(self.HH[f, :, t * 512:(t + 1) * 512], ht[s], [("ht", s)], [("HH", f, t)], ("ht", s))
        PT = self.ba(1024, 8192).rearrange("p (k t) -> p k t", k=2)
        pin = [self.fa(1024 + i * 256, 256) for i in range(2)]
        for tt_ in range(32):
            s = tt_ % 2
            self.dma(pin[s], self.p[L, tt_ * 128:(tt_ + 1) * 128, :], [], [("pin", s)], ("pin", s))
            bank = 4 + s

            def emit(e, s=s, bank=bank):
                for j in range(2):
                    ins = e.transpose(out=self.ps[bank][:, j * 128:(j + 1) * 128], in_=pin[s][:, j * 128:(j + 1) * 128], identity=self.idf[:])
                return ins
            P.op(PE, emit, reads=[("pin", s), "idf"], writes=[("ps", bank)])
            self.cp(DVE, PT[:, :, tt_ * 128:(tt_ + 1) * 128], self.ps[bank][:, 0:256].rearrange("p (c t) -> p c t", t=128), [("ps", bank)], ["PT"])
        wppb2 = [self.ba(9216 + i * 256, 256).rearrange("p (k c) -> p k c", c=128) for i in range(2)]
        wpps2 = [self.fa(1536 + i * 256, 256).rearrange("p (k c) -> p k c", c=128) for i in range(2)]

        def load_ple(c):
            i = c % 2
            self.dma(wpps2[i], self.w_pp[L][:, c * 128:(c + 1) * 128].rearrange("(k p) c -> p k c", p=128), [], [("wpps", i)], ("wpps", i))
            self.cp(POOL, wppb2[i], wpps2[i], [("wpps", i)], [("wppb", i)])
            return (wppb2[i], ("wppb", i)), self.load_w(self.w_pg[L], 8, c * 128)
        nxt = load_ple(0)
        for c in range(8):
            (wppb, rpp), (wpg, rpg) = nxt
            if c + 1 < 8:
                nxt = load_ple(c + 1)
            for t in range(8):
                bp, bg = (0, 1) if t % 2 == 0 else (2, 3)
                tsl = slice(t * 512, (t + 1) * 512)
                self.mm(self.ps[bp][:, :], ("ps", bp), [(wppb[:, k, :], PT[:, k, tsl]) for k in range(2)], [rpp, "PT"])
                self.proj(bg, wpg, rpg, t)
                s = hi_ % 2
                hi_ += 1
                self.act(sgl[s], self.ps[bg][:, :], AF.Sigmoid, [("ps", bg)], [("sgl", s)])
                self.tt(DVE, ht[s], self.ps[bp][:, :], sgl[s], ALU.mult, [("ps", bp), ("sgl", s)], [("ht", s)])
                self.dma(self.PLE[c, :, tsl], ht[s], [("ht", s)], [("PLE", c, t)], ("ht", s))
        P.barrier()

    def stage_ffn_down(self, L, last):
        P = self.P
        NTK = 256
        NT_ = S // NTK
        bb = lambda off, n: self.AB_[:, 4096 + off:4096 + off + n]
        Wd = bb(0, NF * 1024).rearrange("p (k c) -> p k c", c=1024)
        ht = [bb(NF * 1024, NF * NTK).rearrange("p (k t) -> p k t", t=NTK)]
        plt = bb(NF * 1024 + NF * NTK, 8 * NTK).rearrange("p (k t) -> p k t", t=NTK)
        xr = [self.fa(i * 2048, 2048).rearrange("p (k t) -> p k t", t=NTK) for i in range(2)]
        xn = [self.fa(4096 + i * 2048, 2048).rearrange("p (k t) -> p k t", t=NTK) for i in range(2)]
        xsq = self.fa(8192, 2048).rearrange("p (k t) -> p k t", t=NTK)
        stat = [self.fa(10240 + i * NTK, NTK) for i in range(3)]
        otok = self.AF_[:, 0:2048]
        for k in range(NF):
            self.load_rows(self.w_fd[L, k * 128:(k + 1) * 128, :], Wd[:, k, :], ("Wd", k))
        if last:
            P.barrier()

        def front(t):
            sl = t % 2
            tsl = slice(t * NTK, (t + 1) * NTK)
            t5 = (t * NTK) // 512
            self.dma(ht[0], self.HH[:, :, tsl].rearrange("c p t -> p c t"), [("HH", f, t5) for f in range(NF)], ["htd"], "htd")
            self.dma(plt, self.PLE[:, :, tsl].rearrange("c p t -> p c t"), [("PLE", c, t5) for c in range(8)], ["plt"], "plt")
            self.dma(xr[sl], self.XRES[:, :, tsl].rearrange("c p t -> p c t"), [("XRES", t5)], [("xr", sl)], ("xr_ld", sl))
            for c in range(8):
                bank = c % 4
                self.mm(self.ps[bank][:, 0:NTK], ("ps", bank), [(Wd[:, k, c * 128:(c + 1) * 128], ht[0][:, k, :]) for k in range(NF)], [("Wd", k) for k in range(NF)] + ["htd"])
                self.stt(DVE, xn[sl][:, c, :], xr[sl][:, c, :], ALPHA, self.ps[bank][:, 0:NTK], ALU.mult, ALU.add, [("xr", sl), ("ps", bank)], [("xn", sl)])
            self.tt(POOL, xn[sl], xn[sl], plt, ALU.add, [("xn", sl), "plt"], [("xn", sl)])

        def back(t):
            sl = t % 2
            tsl = slice(t * NTK, (t + 1) * NTK)
            t5 = (t * NTK) // 512
            self.ln_tile(L, 2, xn[sl], ("xn", sl), xsq, stat, xr[sl], ("xr", sl), t * NTK, NTK)
            if not last:
                self.dma(self.XRES[:, :, tsl].rearrange("c p t -> p c t"), xr[sl], [("xr", sl)], [("XRES", t5)], ("xr_st", sl))
            else:
                for tb in range(NTK // 128):
                    for hf in range(2):
                        bank = 6

                        def emit(e, tb=tb, hf=hf, bank=bank, sl=sl):
                            for j in range(4):
                                ins = e.transpose(out=self.ps[bank][:, j * 128:(j + 1) * 128], in_=xr[sl][:, hf * 4 + j, tb * 128:(tb + 1) * 128], identity=self.idf[:])
                            return ins
                        P.op(PE, emit, reads=[("xr", sl), "idf"], writes=[("ps", bank)])
                        self.act(otok[:, tb * 1024 + hf * 512:tb * 1024 + (hf + 1) * 512], self.ps[bank][:, :], AF.Copy, [("ps", bank)], [("otok", tb)])
                    self.dma(self.out[t * NTK + tb * 128:t * NTK + (tb + 1) * 128, :], otok[:, tb * 1024:(tb + 1) * 1024], [("otok", tb)], [("out", t, tb)], ("ost", tb))

        front(0)
        for t in range(NT_):
            if t + 1 < NT_:
                front(t + 1)
            back(t)
        P.barrier()

    def dump(self, slot, src_dram_bf16_rows):
        pass

    def build(self, stages=None):
        self.declare()
        with self.stack:
            self.prologue()
            for L in range(self.n_layers):
                last = L == self.n_layers - 1
                on = lambda nm: stages is None or nm in stages
                if on("pool"):
                    self.stage_pool(L)
                if on("merge0"):
                    self.stage_merge(L, 0)
                if on("attn"):
                    self.stage_attn(L)
                if on("merge1"):
                    self.stage_merge(L, 1)
                if on("hgrn"):
                    self.stage_hgrn(L)
                if on("merge2"):
                    self.stage_merge(L, 2)
                if on("mix"):
                    self.stage_mix(L)
                if on("ffn_up"):
                    self.stage_ffn_up(L)
                if on("ffn_down"):
                    self.stage_ffn_down(L, last)
            if self.dbg and not os.environ.get('KDBG_NODUMP'):
                self.dbg_dump()
            self.P.finalize(self.stack)
        return self.nc

    def dbg_dump(self):
        tb = self.ba(0, 4096)
        tf = self.fa(0, 4096)
        for c in range(20):
            self.dma(tb, self.OUTX[c], [("OUTX", c, t) for t in range(8)], ["dtb"], "dtb")
            self.cp(DVE, tf, tb, ["dtb"], ["dtf"])
            self.dma(self.dbg_out[c], tf, ["dtf"], [("dbg", c)], "dtf")


_CACHE = {}


def kernel(**inputs):
    consts = make_consts()
    vecs = pack_vecs(inputs)
    if "nc" not in _CACHE:
        _CACHE["nc"] = Builder().build()
    nc = _CACHE["nc"]
    shared = {k: np.ascontiguousarray(np.asarray(inputs[k], np.float32)) for k in (
        "w_in", "pool_w", "w_branch_a", "w_branch_b", "w_branch_c", "w_out", "w_ffn_gate", "w_ffn_up",
        "w_ffn_down", "w_ple_proj", "w_ple_gate")}
    shared.update(consts)
    shared["vecs"] = vecs
    x = np.asarray(inputs["x"], np.float32)
    p = np.asarray(inputs["p"], np.float32)
    in_maps = []
    for c in range(NCORES):
        m = dict(shared)
        m["x"] = np.ascontiguousarray(x[c])
        m["p"] = np.ascontiguousarray(p[:, c])
        in_maps.append(m)
    res = run_bass_kernel_spmd(nc, in_maps, core_ids=list(range(NCORES)))
    return np.stack([np.asarray(r["out"], np.float32) for r in res.results], axis=0)
```

```python
import contextlib
import os
import numpy as np
import ml_dtypes
import concourse.bass as bass
import concourse.mybir as mybir
from concourse.bass_utils import run_bass_kernel_spmd

F32 = mybir.dt.float32
BF16 = mybir.dt.bfloat16
AF = mybir.ActivationFunctionType
ALU = mybir.AluOpType
PE, ACT, DVE, POOL, SP = "pe", "act", "dve", "pool", "sp"
ENGS = (PE, ACT, DVE, POOL, SP)
SEM_LIMIT = 24000
RELAX_N = int(os.environ.get('KRELAX', '256'))
FILL1 = int(os.environ.get('KFILL1', '0'))
FILL2 = int(os.environ.get('KFILL2', '0'))

S = 4096
D = 1024
DEPTH = 4
NCORES = 8
FF = 2816
NF = 22
INW = 13824
ALPHA = float((2 * DEPTH) ** 0.25)
LN_EPS = 1e-5
RMS_EPS = 1e-6
OFF_POOL, OFF_AQ, OFF_AK, OFF_AV = 0, 1024, 2560, 4096
OFF_HQ, OFF_HI, OFF_HFF, OFF_HFB, OFF_HG, OFF_GATE = 5632, 6656, 7680, 8704, 9728, 10752
ATT_D = (1, 4, 16)
CH = 64


def fsz(ap):
    n = 1
    for d in ap.shape[1:]:
        n *= int(d)
    return n


class Prog:
    def __init__(self, nc, same_engine_sync=True):
        self.nc = nc
        self.ops = []
        self.by_eng = {e: [] for e in ENGS}
        self.last_w = {}
        self.readers = {}
        self.same_engine_sync = same_engine_sync
        self.pending = {e: set() for e in ENGS}

    def op(self, eng, emit, reads=(), writes=(), dma=None, n=0):
        oid = len(self.ops)
        deps = set()
        for r in reads:
            w = self.last_w.get(r)
            if w is not None:
                deps.add(w)
            if (isinstance(r, tuple) and r[0] == "ps") or (isinstance(r, str) and r.startswith("psb")):
                rd = self.readers.get(r)
                if rd:
                    deps.update(rd.values())
        for r in writes:
            w = self.last_w.get(r)
            if w is not None:
                deps.add(w)
            rd = self.readers.get(r)
            if rd:
                deps.update(rd.values())
        if self.pending[eng]:
            deps.update(self.pending[eng])
            self.pending[eng] = set()
        keep = []
        for d in deps:
            o = self.ops[d]
            if o["dma"] is None and o["eng"] == eng and dma is None:
                if eng == PE or not self.same_engine_sync or o["n"] >= RELAX_N:
                    continue
            keep.append(d)
            o["target"] = True
        rec = dict(id=oid, eng=eng, emit=emit, deps=keep, dma=dma, target=False, ev=None, n=n)
        self.ops.append(rec)
        self.by_eng[eng].append(rec)
        lane = ("dma", dma) if dma is not None else ("eng", eng)
        for r in reads:
            self.readers.setdefault(r, {})[lane] = oid
        for r in writes:
            self.last_w[r] = oid
            self.readers[r] = {}
        return oid

    def barrier(self):
        last = {}
        for o in self.ops:
            lane = ("dma", o["dma"]) if o["dma"] is not None else ("eng", o["eng"])
            last[lane] = o["id"]
        pre = set(last.values())
        for e in ENGS:
            self.pending[e] = set(pre)

    def finalize(self, stack):
        nc = self.nc
        sems = {}

        def get_sem(key):
            if key not in sems:
                sems[key] = stack.enter_context(nc.semaphore("s%d" % len(sems)))
            return sems[key]

        cnt = {e: 0 for e in ENGS}
        gen = {e: 0 for e in ENGS}
        dcnt = {}
        final = {}
        for o in self.ops:
            if o["dma"] is not None:
                k = ("dma", o["dma"])
                dcnt[k] = dcnt.get(k, 0) + 16
                o["ev"] = (get_sem(k), dcnt[k], k)
                final[k] = (o["ev"][0], dcnt[k])
            elif o["target"]:
                e = o["eng"]
                if cnt[e] >= SEM_LIMIT:
                    gen[e] += 1
                    cnt[e] = 0
                cnt[e] += 1
                k = ("eng", e, gen[e])
                o["ev"] = (get_sem(k), cnt[e], k)
        self.n_sems = len(sems)
        block = stack.enter_context(nc.Block())
        ops = self.ops
        by_eng = self.by_eng

        def run_stream(engname, eobj):
            known = {}
            for o in by_eng[engname]:
                for d in o["deps"]:
                    s, v, k = ops[d]["ev"]
                    if known.get(k, 0) >= v:
                        continue
                    known[k] = v
                    eobj.wait_ge(s, v)
                ins = o["emit"](eobj)
                if o["ev"] is not None:
                    s, v, k = o["ev"]
                    ins.then_inc(s, 16 if o["dma"] is not None else 1)
            if engname == SP:
                for k, (s, v) in final.items():
                    if known.get(k, 0) < v:
                        eobj.wait_ge(s, v)

        @block.tensor
        def _(e):
            run_stream(PE, e)

        @block.scalar
        def _(e):
            run_stream(ACT, e)

        @block.vector
        def _(e):
            run_stream(DVE, e)

        @block.gpsimd
        def _(e):
            run_stream(POOL, e)

        @block.sync
        def _(e):
            run_stream(SP, e)


NVEC_PER_L = 8 * 6
VEC_LB = DEPTH * NVEC_PER_L
NVEC = VEC_LB + DEPTH * 16


def _fm(v):
    return np.ascontiguousarray(v.reshape(-1, 128).T)


def make_consts():
    c = {}
    c["c_idf"] = np.eye(128, dtype=np.float32)
    c["c_onesD"] = np.full((128, 128), 1.0 / D, np.float32)
    reset = np.ones((128, 512), np.float32)
    reset[:, ::CH] = 0.0
    c["c_reset"] = reset
    s = np.arange(128)[:, None]
    t = np.arange(128)[None, :]
    same = (s // CH) == (t // CH)
    mF = (same & (s <= t)).astype(np.float32)
    mB = (same & (s >= t)).astype(np.float32)
    lo0 = np.ones((128, 128), np.float32)
    lo0[:64] = 0
    hi0 = np.ones((128, 128), np.float32)
    hi0[64:] = 0
    bfc = np.concatenate([np.eye(128, dtype=np.float32), np.ones((128, 128), np.float32), lo0, hi0, mF, mB], axis=1)
    c["c_bf"] = bfc.astype(ml_dtypes.bfloat16)
    slopes = 2.0 ** (-8.0 * np.arange(1, 13, dtype=np.float64) / 12)
    kp = np.arange(128)[:, None]
    qf = np.arange(128)[None, :]
    offA = kp - 64 - qf
    offB = kp + 64 - qf
    bias = np.zeros((12, 128, 256), np.float32)
    sq = np.sqrt(128.0)
    for h in range(12):
        d = ATT_D[h // 4]
        for j, off in enumerate((offA, offB)):
            b = -slopes[h] * d * np.abs(off) * sq
            b = np.where(np.abs(off) <= 64, b, -30000.0)
            bias[h, :, j * 128:(j + 1) * 128] = b
    c["c_abias"] = bias.astype(ml_dtypes.bfloat16)
    rc = np.zeros((128, 4, 16), np.float32)
    for g in range(4):
        h = 1 << g
        tt = np.concatenate([np.arange(8), np.arange(S - 8, S)])
        cntv = np.minimum(tt + h, S) - np.maximum(tt - h, 0)
        rc[:, g, :] = (1.0 / cntv)[None, :]
    c["c_rc"] = rc.reshape(128, 64)
    return c


def pack_vecs(inp):
    v = np.zeros((128, NVEC), np.float32)
    for l in range(DEPTH):
        for j, nm in enumerate(("pool_scale", "hgrn_norm_w", "ln1_g", "ln1_b", "ln2_g", "ln2_b")):
            v[:, l * NVEC_PER_L + j * 8:l * NVEC_PER_L + (j + 1) * 8] = _fm(np.asarray(inp[nm][l], np.float32))
        v[:, VEC_LB + l * 16:VEC_LB + (l + 1) * 16] = np.asarray(inp["hgrn_lb_logits"][l], np.float32).reshape(16, 128).T
    return v


class Builder:
    def __init__(self, n_layers=DEPTH, dbg=None, sync=True):
        self.n_layers = n_layers
        self.dbg = dbg
        self.nc = bass.Bass("TRN2", target_bir_lowering=False)
        self.P = Prog(self.nc, same_engine_sync=sync)
        self.stack = contextlib.ExitStack()
        self.wi = 0
        self.ri = 0
        self.pbi = 0

    def dram_in(self, name, shape, dt=F32):
        return self.nc.dram_tensor(name, list(shape), dt, kind="ExternalInput").ap()

    def sb(self, name, shape, dt):
        return self.stack.enter_context(self.nc.sbuf_tensor("sb_" + name, list(shape), dt))

    def declare(self):
        nc = self.nc
        self.x = self.dram_in("x", [S, D])
        self.p = self.dram_in("p", [DEPTH, S, 256])
        self.w_in = self.dram_in("w_in", [DEPTH, D, INW])
        self.pool_w = self.dram_in("pool_w", [DEPTH, 4, 256, 256])
        self.w_br = [self.dram_in("w_branch_a", [DEPTH, 1024, D]), self.dram_in("w_branch_b", [DEPTH, 512, D]),
                     self.dram_in("w_branch_c", [DEPTH, 1024, D])]
        self.w_out = self.dram_in("w_out", [DEPTH, D, D])
        self.w_fg = self.dram_in("w_ffn_gate", [DEPTH, D, FF])
        self.w_fu = self.dram_in("w_ffn_up", [DEPTH, D, FF])
        self.w_fd = self.dram_in("w_ffn_down", [DEPTH, FF, D])
        self.w_pp = self.dram_in("w_ple_proj", [DEPTH, 256, D])
        self.w_pg = self.dram_in("w_ple_gate", [DEPTH, D, D])
        self.vecs_d = self.dram_in("vecs", [128, NVEC])
        self.c_idf = self.dram_in("c_idf", [128, 128])
        self.c_onesD = self.dram_in("c_onesD", [128, 128])
        self.c_reset = self.dram_in("c_reset", [128, 512])
        self.c_bf = self.dram_in("c_bf", [128, 768], BF16)
        self.c_abias = self.dram_in("c_abias", [12, 128, 256], BF16)
        self.c_rc = self.dram_in("c_rc", [128, 64])
        self.out = nc.dram_tensor("out", [S, D], F32, kind="ExternalOutput").ap()
        if self.dbg:
            self.dbg_out = nc.dram_tensor("dbg", [20, 128, S], F32, kind="ExternalOutput").ap()
        self.XRES = nc.dram_tensor("XRES", [8, 128, S], F32, kind="Internal").ap()
        self.OUTX = nc.dram_tensor("OUTX", [20, 128, S], BF16, kind="Internal").ap()
        self.GY = nc.dram_tensor("GY", [3, 8, 128, S], BF16, kind="Internal").ap()
        self.HH = nc.dram_tensor("HH", [NF, 128, S], BF16, kind="Internal").ap()
        self.PLE = nc.dram_tensor("PLE", [8, 128, S], BF16, kind="Internal").ap()
        self.A = self.sb("A", [128, 8, S], BF16)
        self.AF_ = self.sb("arenaF", [128, 16384], F32)
        self.AB_ = self.sb("arenaB", [128, 35840], BF16)
        self.idf = self.sb("idf", [128, 128], F32)
        self.onesD = self.sb("onesD", [128, 128], F32)
        self.reset = self.sb("reset", [128, 512], F32)
        self.cbf = self.sb("cbf", [128, 768], BF16)
        self.rc = self.sb("rc", [128, 64], F32)
        self.vecs = self.sb("vecs", [128, NVEC], F32)
        self.lb = self.sb("lb", [128, DEPTH * 16], F32)
        self.oml = self.sb("oml", [128, DEPTH * 16], F32)
        self.lnoml = self.sb("lnoml", [128, DEPTH * 16], F32)
        self.misc = self.sb("misc", [128, 128], F32)
        self.ps = [self.stack.enter_context(nc.psum_tensor("ps%d" % i, [128, 512], F32)) for i in range(7)]
        self.psb = self.stack.enter_context(nc.psum_tensor("psb", [128, 1024], BF16))
        self.identb = self.cbf[:, 0:128]
        self.onesb = self.cbf[:, 128:256]
        self.lo0 = self.cbf[:, 256:384]
        self.hi0 = self.cbf[:, 384:512]
        self.maskF = self.cbf[:, 512:640]
        self.maskB = self.cbf[:, 640:768]
        self.wst = [self.AF_[:, i * 1024:(i + 1) * 1024].rearrange("p (k c) -> p k c", c=128) for i in range(2)]
        self.rst = [self.AF_[:, 2048 + i * 1024:2048 + (i + 1) * 1024] for i in range(2)]
        self.wbf = [self.AB_[:, i * 1024:(i + 1) * 1024].rearrange("p (k c) -> p k c", c=128) for i in range(6)]
        self.F0 = 4096
        self.B0 = 6144

    def fa(self, off, n):
        assert self.F0 + off + n <= 16384, (off, n)
        return self.AF_[:, self.F0 + off:self.F0 + off + n]

    def ba(self, off, n):
        assert self.B0 + off + n <= 35840, (off, n)
        return self.AB_[:, self.B0 + off:self.B0 + off + n]

    def dma(self, out, in_, reads, writes, key, eng=SP):
        self.P.op(eng, lambda e: e.dma_start(out=out, in_=in_), reads=reads, writes=writes, dma=key)

    def load_w(self, W2d, K, c0, ncols=128):
        s = self.wi % 2
        b = self.wi % 6
        self.wi += 1
        st = self.wst[s][:, 0:K, 0:ncols]
        wb = self.wbf[b][:, 0:K, 0:ncols]
        src = W2d[:, c0:c0 + ncols].rearrange("(k p) c -> p k c", p=128)
        self.dma(st, src, [], [("wst", s)], ("wst", s))
        self.P.op(POOL, lambda e: e.tensor_copy(out=wb, in_=st), reads=[("wst", s)], writes=[("wbf", b)])
        return self.wbf[b], ("wbf", b)

    def load_rows(self, Wrows, dest, dreg, ncols=1024):
        s = self.ri % 2
        self.ri += 1
        st = self.rst[s][:, 0:ncols]
        self.dma(st, Wrows, [], [("rst", s)], ("rst", s))
        self.P.op(POOL, lambda e: e.tensor_copy(out=dest, in_=st), reads=[("rst", s)], writes=[dreg])

    def mm(self, out, outreg, pairs, reads):
        n = len(pairs)

        def emit(e):
            for i, (l, r) in enumerate(pairs):
                ins = e.matmul(out, l, r, start=(i == 0), stop=(i == n - 1))
            return ins
        self.P.op(PE, emit, reads=reads, writes=[outreg])

    def pe_fill(self, n, bank):
        if n <= 0:
            return

        def emit(e):
            for _ in range(n):
                ins = e.matmul(self.ps[bank][:, :], self.identb, self.A[:, 0, 0:512], start=True, stop=True)
            return ins
        self.P.op(PE, emit, reads=["cbf"], writes=[("ps", bank)])

    def act(self, out, in_, func, reads, writes, scale=None, bias=None):
        kw = {}
        if scale is not None:
            kw["scale"] = scale
        if bias is not None:
            kw["bias"] = bias
        self.P.op(ACT, lambda e: e.activation(out=out, in_=in_, func=func, **kw), reads=reads, writes=writes, n=fsz(out))

    def tt(self, eng, out, in0, in1, op, reads, writes):
        self.P.op(eng, lambda e: e.tensor_tensor(out=out, in0=in0, in1=in1, op=op), reads=reads, writes=writes, n=fsz(out))

    def ts(self, eng, out, in0, s1, s2, op0, op1, reads, writes):
        if s2 is None:
            self.P.op(eng, lambda e: e.tensor_scalar(out=out, in0=in0, scalar1=s1, scalar2=None, op0=op0), reads=reads, writes=writes, n=fsz(out))
        else:
            self.P.op(eng, lambda e: e.tensor_scalar(out=out, in0=in0, scalar1=s1, scalar2=s2, op0=op0, op1=op1), reads=reads, writes=writes, n=fsz(out))

    def stt(self, eng, out, in0, scalar, in1, op0, op1, reads, writes):
        self.P.op(eng, lambda e: e.scalar_tensor_tensor(out=out, in0=in0, scalar=scalar, in1=in1, op0=op0, op1=op1), reads=reads, writes=writes, n=fsz(out))

    def cp(self, eng, out, in_, reads, writes):
        self.P.op(eng, lambda e: e.tensor_copy(out=out, in_=in_), reads=reads, writes=writes, n=fsz(out))

    def memset(self, eng, ap, val, writes):
        self.P.op(eng, lambda e: e.memset(ap, val), writes=writes, n=fsz(ap))

    def vec(self, l, j, c):
        o = l * NVEC_PER_L + j * 8 + c
        return self.vecs[:, o:o + 1]

    def proj(self, bank, wb, wreg, t, K=8, src=None, sreg=None, ntok=512):
        pairs = []
        reads = [wreg]
        for k in range(K):
            pairs.append((wb[:, k, :], self.A[:, k, t * ntok:(t + 1) * ntok]))
            reads.append(("A", k, (t * ntok) // 512))
        self.mm(self.ps[bank][:, 0:ntok], ("ps", bank), pairs, reads)

    def prologue(self):
        P = self.P
        for nm, dst, src in (("idf", self.idf, self.c_idf), ("onesD", self.onesD, self.c_onesD), ("reset", self.reset, self.c_reset),
                             ("cbf", self.cbf, self.c_bf), ("rc", self.rc, self.c_rc), ("vecs", self.vecs, self.vecs_d)):
            self.dma(dst[:], src, [], [nm], nm)
        self.memset(DVE, self.misc[:], 0.0, ["misc"])
        self.memset(DVE, self.misc[:, 0:1], LN_EPS, ["misc"])
        self.memset(DVE, self.misc[:, 1:2], RMS_EPS, ["misc"])
        self.memset(DVE, self.misc[:, 2:3], 1.0, ["misc"])
        E = self.misc[:, 8:8 + 16 * DEPTH]
        self.act(E, self.vecs[:, VEC_LB:VEC_LB + 16 * DEPTH], AF.Exp, ["vecs"], ["misc"])
        ssum = self.lb[:, 0:16]
        self.tt(DVE, ssum, E[:, 0:16], E[:, 16:32], ALU.add, ["misc"], ["lb"])
        for l in range(2, DEPTH):
            self.tt(DVE, ssum, ssum, E[:, l * 16:(l + 1) * 16], ALU.add, ["misc", "lb"], ["lb"])
        self.P.op(DVE, lambda e: e.reciprocal(out=ssum, in_=ssum), reads=["lb"], writes=["lb"])
        for l in range(1, DEPTH):
            self.tt(DVE, E[:, l * 16:(l + 1) * 16], E[:, l * 16:(l + 1) * 16], ssum, ALU.mult, ["misc", "lb"], ["misc"])
        self.memset(DVE, self.lb[:, 0:16], 0.0, ["lb"])
        for l in range(1, DEPTH):
            self.tt(DVE, self.lb[:, l * 16:(l + 1) * 16], self.lb[:, (l - 1) * 16:l * 16], E[:, l * 16:(l + 1) * 16], ALU.add, ["misc", "lb"], ["lb"])
        self.ts(DVE, self.oml[:], self.lb[:], -1.0, 1.0, ALU.mult, ALU.add, ["lb"], ["oml"])
        self.act(self.lnoml[:], self.oml[:], AF.Ln, ["oml"], ["lnoml"])
        xin = [self.fa(i * 1024, 1024) for i in range(2)]
        xr = [self.fa(2048 + i * 1024, 1024).rearrange("p (c t) -> p c t", t=128) for i in range(2)]
        import os
        for tt_ in range(int(os.environ.get('KDBG_NX', '32'))):
            s = tt_ % 2
            self.dma(xin[s], self.x[tt_ * 128:(tt_ + 1) * 128, :], [], [("xin", s)], ("xin", s))
            for hf in range(2):
                bank = (tt_ * 2 + hf) % 4

                def emit(e, s=s, hf=hf, bank=bank):
                    for j in range(4):
                        c = hf * 4 + j
                        ins = e.transpose(out=self.ps[bank][:, j * 128:(j + 1) * 128], in_=xin[s][:, c * 128:(c + 1) * 128], identity=self.idf[:])
                    return ins
                P.op(PE, emit, reads=[("xin", s), "idf"], writes=[("ps", bank)])
                psv = self.ps[bank][:, :].rearrange("p (c t) -> p c t", t=128)
                self.cp(DVE, self.A[:, hf * 4:(hf + 1) * 4, tt_ * 128:(tt_ + 1) * 128], psv, [("ps", bank)], [("A", k, tt_ // 4) for k in range(hf * 4, hf * 4 + 4)])
                self.act(xr[s][:, hf * 4:(hf + 1) * 4, :], psv, AF.Copy, [("ps", bank)], [("xr", s)])
            dst = self.XRES.rearrange("c p t -> p c t")[:, :, tt_ * 128:(tt_ + 1) * 128]
            self.dma(dst, xr[s], [("xr", s)], [("XRES", tt_ // 4)], ("xrst", s))
        P.barrier()

    def stage_pool(self, L):
        P = self.P
        W = self.w_in[L]
        apad = self.fa(0, 4128)
        sa = self.fa(4128, 2112)
        sb_ = self.fa(4128 + 2112, 2112)
        tmpe = self.fa(4128 + 4224, 16)
        mixed = [self.ba(i * 4096, 4096) for i in range(2)]
        pwb = self.ba(8192, 512).rearrange("p (k c) -> p k c", c=256)
        pws = self.fa(4128 + 4224 + 16, 512).rearrange("p (k c) -> p k c", c=256)
        otile = [self.ba(8704 + i * 512, 512) for i in range(2)]
        self.memset(POOL, apad, 0.0, ["apad"])
        oi = 0
        nxtw = self.load_w(W, 8, OFF_POOL)
        for g in range(4):
            h = 1 << g
            for cc in range(2):
                c = 2 * g + cc
                wb, wreg = nxtw
                if c + 1 < 8:
                    nxtw = self.load_w(W, 8, OFF_POOL + (c + 1) * 128)
                for t in range(8):
                    bank = t % 4
                    self.proj(bank, wb, wreg, t)
                    self.act(apad[:, 16 + t * 512:16 + (t + 1) * 512], self.ps[bank][:, :], AF.Copy, [("ps", bank)], ["apad"])
                for hf in range(2):
                    j0 = 16 + hf * 2048 - 16
                    a = apad[:, j0:j0 + 2080]
                    self.tt(DVE, sa[:, 1:2080], a[:, 0:2079], a[:, 1:2080], ALU.add, ["apad"], ["sa"])
                    fin = sa
                    if g >= 1:
                        self.tt(POOL, sb_[:, 2:2079], sa[:, 1:2078], sa[:, 3:2080], ALU.add, ["sa"], ["sb"])
                        fin = sb_
                    if g >= 2:
                        self.tt(DVE, sa[:, 4:2077], sb_[:, 2:2075], sb_[:, 6:2079], ALU.add, ["sb"], ["sa"])
                        fin = sa
                    if g >= 3:
                        self.tt(POOL, sb_[:, 8:2073], sa[:, 4:2069], sa[:, 12:2077], ALU.add, ["sa"], ["sb"])
                        fin = sb_
                    freg = "sa" if fin is sa else "sb"
                    mo = mixed[cc][:, hf * 2048:(hf + 1) * 2048]
                    self.stt(DVE, mo, fin[:, 16:2064], 1.0 / (2 * h), a[:, 16:2064], ALU.mult, ALU.subtract, [freg, "apad"], [("mixed", cc)])
                    e0 = 0 if hf == 0 else 8
                    u0 = 16 if hf == 0 else 2064 - 8
                    self.tt(DVE, tmpe[:, e0:e0 + 8], fin[:, u0:u0 + 8], self.rc[:, g * 16 + e0:g * 16 + e0 + 8], ALU.mult, [freg, "rc"], ["tmpe"])
                    t0 = 0 if hf == 0 else S - 8
                    self.tt(DVE, mixed[cc][:, t0:t0 + 8], tmpe[:, e0:e0 + 8], a[:, u0:u0 + 8], ALU.subtract, ["tmpe", "apad"], [("mixed", cc)])
            src = self.pool_w[L, g].rearrange("(k p) c -> p k c", p=128)
            self.dma(pws, src, [], ["pws"], "pws")
            self.cp(POOL, pwb, pws, ["pws"], ["pwb"])
            for cc in range(2):
                c = 2 * g + cc
                for t in range(8):
                    bank = 4 + (t % 2)
                    pairs = [(pwb[:, k, cc * 128:(cc + 1) * 128], mixed[k][:, t * 512:(t + 1) * 512]) for k in range(2)]
                    self.mm(self.ps[bank][:, :], ("ps", bank), pairs, ["pwb", ("mixed", 0), ("mixed", 1)])
                    o = oi % 2
                    oi += 1
                    self.ts(DVE, otile[o], self.ps[bank][:, :], self.vec(L, 0, c), None, ALU.mult, None, [("ps", bank), "vecs"], [("otile", o)])
                    self.dma(self.OUTX[c, :, t * 512:(t + 1) * 512], otile[o], [("otile", o)], [("OUTX", c, t)], ("otile", o))
        P.barrier()

    def stage_attn(self, L):
        P = self.P
        W = self.w_in[L]
        num = self.fa(0, 4096)
        den = self.fa(4096, 4096)
        QT = self.ba(0, 4096)
        KTp = self.ba(4096, 6144)
        VTp = self.ba(10240, 6144)
        Vtok = self.ba(16384, 6144).rearrange("p (n c) -> p n c", c=128)
        PT = [self.ba(22528 + i * 256, 256) for i in range(2)]
        bia = [self.ba(23040 + i * 256, 256) for i in range(2)]
        otile = self.ba(23552, 4096)
        bi = 0
        pi = 0
        for j in range(4):
            for g in range(3):
                h = 4 * g + j
                d = ATT_D[g]
                Lr = S // d
                Lp = Lr + 128
                nblk = Lr // 128
                b = bi % 2
                bi += 1
                self.dma(bia[b], self.c_abias[h], [], [("bia", b)], ("bia", b))
                self.memset(POOL, KTp, 0.0, ["KTp"])
                self.memset(POOL, VTp, 0.0, ["VTp"])
                if j == 0 and g == 0:
                    anxt = [self.load_w(W, 8, o_ + h * 128) for o_ in (OFF_AQ, OFF_AK, OFF_AV)]
                (wq, rq), (wk, rk), (wv, rv) = anxt
                hn = (4 * (g + 1) + j) if g < 2 else (j + 1 if j < 3 else None)
                if hn is not None:
                    anxt = [self.load_w(W, 8, o_ + hn * 128) for o_ in (OFF_AQ, OFF_AK, OFF_AV)]
                n = 512 // d
                for t in range(8):
                    i0 = t * n
                    psv = lambda bank: self.ps[bank][:, :].rearrange("p (i r) -> p r i", r=d)
                    self.proj(0, wq, rq, t)
                    dq = QT[:, 0:d * Lr].rearrange("p (r l) -> p r l", r=d)[:, :, i0:i0 + n]
                    self.act(dq, psv(0), AF.Copy, [("ps", 0)], ["QT"])
                    self.proj(1, wk, rk, t)
                    dk = KTp[:, 0:d * Lp].rearrange("p (r l) -> p r l", r=d)[:, :, 64 + i0:64 + i0 + n]
                    self.cp(DVE, dk, psv(1), [("ps", 1)], ["KTp"])
                    self.proj(2, wv, rv, t)
                    dv = VTp[:, 0:d * Lp].rearrange("p (r l) -> p r l", r=d)[:, :, 64 + i0:64 + i0 + n]
                    self.act(dv, psv(2), AF.Copy, [("ps", 2)], ["VTp"])
                ntile = d * (nblk + 1)
                tiles = [(r, i) for r in range(d) for i in range(nblk + 1)]
                for q0 in range(0, ntile, 8):
                    grp = tiles[q0:q0 + 8]

                    def emit(e, grp=grp, Lp=Lp):
                        for jj, (r, i) in enumerate(grp):
                            ins = e.transpose(out=self.psb[:, jj * 128:(jj + 1) * 128], in_=VTp[:, r * Lp + 128 * i:r * Lp + 128 * i + 128], identity=self.identb)
                        return ins
                    P.op(PE, emit, reads=["VTp", "cbf"], writes=["psb"])
                    ng = len(grp)
                    self.cp(DVE, Vtok[:, q0:q0 + ng, :], self.psb[:, 0:ng * 128].rearrange("p (n c) -> p n c", c=128), ["psb"], ["Vtok"])
                gq = min(4, nblk)
                blocks = [(r, m0, jb) for r in range(d) for m0 in range(0, nblk, gq) for jb in range(gq)]

                def scores(i, b=b, Lp=Lp, Lr=Lr):
                    r, m0, jb = blocks[i]
                    m = m0 + jb
                    sbk = 3 + (i % 2)
                    for side in range(2):
                        kt = KTp[:, r * Lp + 128 * (m + side):r * Lp + 128 * (m + side) + 128]
                        qb = QT[:, r * Lr + 128 * m:r * Lr + 128 * m + 128]
                        pairs = [(kt, qb), (self.identb, bia[b][:, side * 128:(side + 1) * 128])]
                        self.mm(self.ps[sbk][:, side * 128:(side + 1) * 128], ("ps", sbk), pairs, ["KTp", "QT", "cbf", ("bia", b)])
                    self.act(PT[i % 2], self.ps[sbk][:, 0:256], AF.Exp, [("ps", sbk)], [("PT", i % 2)], scale=float(128 ** -0.5))

                def pvden(i, g=g, d=d, nblk=nblk, gq=gq):
                    r, m0, jb = blocks[i]
                    m = m0 + jb
                    pt = PT[i % 2]
                    ptr = ("PT", i % 2)
                    pv = []
                    dn = []
                    for side in range(2):
                        ti = m + side
                        vt = Vtok[:, r * (nblk + 1) + ti, :]
                        val = self.lo0 if ti == 0 else (self.hi0 if ti == nblk else self.onesb)
                        pv.append((vt, pt[:, side * 128:(side + 1) * 128]))
                        dn.append((val, pt[:, side * 128:(side + 1) * 128]))
                    self.mm(self.ps[5][:, jb * 128:(jb + 1) * 128], ("ps", 5), pv, ["Vtok", ptr])
                    self.mm(self.ps[6][:, jb * 128:(jb + 1) * 128], ("ps", 6), dn, ["cbf", ptr])
                    if jb == gq - 1:
                        nn = gq * 128
                        nv = num.rearrange("p (i r) -> p r i", r=d)[:, r, m0 * 128:m0 * 128 + nn]
                        dvv = den.rearrange("p (i r) -> p r i", r=d)[:, r, m0 * 128:m0 * 128 + nn]
                        if g == 0:
                            self.cp(DVE, nv, self.ps[5][:, 0:nn], [("ps", 5)], ["num"])
                            self.act(dvv, self.ps[6][:, 0:nn], AF.Copy, [("ps", 6)], ["den"])
                        else:
                            self.tt(DVE, nv, self.ps[5][:, 0:nn], nv, ALU.add, [("ps", 5), "num"], ["num"])
                            self.tt(DVE, dvv, self.ps[6][:, 0:nn], dvv, ALU.add, [("ps", 6), "den"], ["den"])

                scores(0)
                for i in range(len(blocks)):
                    if i + 1 < len(blocks):
                        scores(i + 1)
                    pvden(i)
            P.op(DVE, lambda e: e.reciprocal(out=den, in_=den), reads=["den"], writes=["den"])
            self.tt(DVE, otile, num, den, ALU.mult, ["num", "den"], ["aotile"])
            self.dma(self.OUTX[8 + j, :, :], otile, ["aotile"], [("OUTX", 8 + j, t) for t in range(8)], "aotile")
        P.barrier()

    def stage_hgrn(self, L):
        P = self.P
        W = self.w_in[L]
        fofs = [0]
        bofs = [0]

        def falloc(n):
            a = self.fa(fofs[0], n)
            fofs[0] += n
            return a

        def balloc(n):
            a = self.ba(bofs[0], n)
            bofs[0] += n
            return a
        CB = []
        for dr in range(2):
            b = dict(qraw=falloc(512), zraw=falloc(512), sf=falloc(512), lg=falloc(512), Bn=falloc(512), bc=falloc(512), tq=falloc(512),
                     S32=falloc(9 * 128).rearrange("p (n c) -> p n c", c=128), DEC=falloc(128)[:, 0:8],
                     QE=balloc(512), KK=balloc(512), KDT=balloc(512), VT=balloc(512),
                     Vtok=balloc(512).rearrange("p (n c) -> p n c", c=128), KDtok=balloc(512).rearrange("p (n c) -> p n c", c=128),
                     attm=balloc(512).rearrange("p (n c) -> p n c", c=128), Sbf=balloc(9 * 128).rearrange("p (n c) -> p n c", c=128),
                     O=balloc(4096))
            b["S32_flat"] = b["S32"].rearrange("p n c -> p (n c)")
            b["Sbf_flat"] = b["Sbf"].rearrange("p n c -> p (n c)")
            CB.append(b)
        og = falloc(512)
        rs = falloc(512)
        gsl = falloc(512)
        osq = balloc(512)
        otl = [balloc(512) for _ in range(2)]
        wsl2 = [[balloc(1024).rearrange("p (k c) -> p k c", c=128) for _ in range(5)] for _ in range(2)]
        wsl = list(wsl2[0])
        hwset = [0]
        one_c = self.misc[:, 2:3]

        def chain(h, dr):
            b = CB[dr]
            R = lambda nm: (nm, dr)
            col = L * 16 + dr * 8 + h
            lbc = self.lb[:, col:col + 1]
            lnoml = self.lnoml[:, col:col + 1]
            mask = self.maskF if dr == 0 else self.maskB
            qraw, zraw, sf, lg, Bn, bc, tq, S32, DEC = b["qraw"], b["zraw"], b["sf"], b["lg"], b["Bn"], b["bc"], b["tq"], b["S32"], b["DEC"]
            QE, KK, KDT, VT, Vtok, KDtok, attm, Sbf, O = b["QE"], b["KK"], b["KDT"], b["VT"], b["Vtok"], b["KDtok"], b["attm"], b["Sbf"], b["O"]
            self.memset(DVE, S32[:, 8, :], 0.0, [R("S32")])
            self.memset(POOL, Sbf[:, 8, :], 0.0, [R("Sbf")])
            for t in (range(8) if dr == 0 else range(7, -1, -1)):
                hs = hwset[0]
                self.proj(0, wsl[0], ("hw", hs, 0), t)
                self.proj(1, wsl[1], ("hw", hs, 1), t)
                self.proj(2, wsl[2 + dr], ("hw", hs, 2 + dr), t)
                self.act(tq, self.ps[0][:, :], AF.Exp, [("ps", 0)], [R("tq")], scale=-1.0)
                self.cp(DVE, qraw, self.ps[0][:, :], [("ps", 0)], [R("qraw")])
                self.act(VT, self.ps[1][:, :], AF.Copy, [("ps", 1)], [R("VT")])
                self.act(sf, self.ps[2][:, :], AF.Exp, [("ps", 2)], [R("sf")], scale=-1.0)
                self.cp(DVE, zraw, self.ps[2][:, :], [("ps", 2)], [R("zraw")])

                def emitv(e):
                    for jj in range(4):
                        ins = e.transpose(out=self.psb[:, jj * 128:(jj + 1) * 128], in_=VT[:, jj * 128:(jj + 1) * 128], identity=self.identb)
                    return ins
                P.op(PE, emitv, reads=[R("VT"), "cbf"], writes=["psb"])
                self.cp(DVE, Vtok, self.psb[:, 0:512].rearrange("p (n c) -> p n c", c=128), ["psb"], [R("Vtok")])
                self.pe_fill(FILL1, 3)
                yield
                self.act(lg, sf, AF.Ln, [R("sf"), "lb", "misc"], [R("lg")], scale=lbc, bias=one_c)
                self.act(Bn, sf, AF.Ln, [R("sf"), "misc"], [R("Bn")], bias=one_c)
                self.act(tq, tq, AF.Ln, [R("tq"), "misc"], [R("tq")], bias=one_c)
                self.tt(DVE, lg, lg, Bn, ALU.subtract, [R("lg"), R("Bn")], [R("lg")])
                P.op(DVE, lambda e: e.tensor_tensor_scan(out=bc, data0=self.reset[:], data1=lg, initial=0.0, op0=ALU.mult, op1=ALU.add),
                     reads=[R("lg"), "reset"], writes=[R("bc")], n=512)
                bcv = bc.rearrange("p (n c) -> p n c", c=CH)
                if dr == 1:
                    self.tt(POOL, lg, lg, bc, ALU.subtract, [R("lg"), R("bc")], [R("lg")])
                    self.tt(DVE, bcv, lg.rearrange("p (n c) -> p n c", c=CH), bcv[:, :, CH - 1:CH].broadcast_to([128, 8, CH]), ALU.add, [R("lg"), R("bc")], [R("bc")])
                yield
                dsel = CH - 1 if dr == 0 else 0
                bend = bcv[:, :, dsel:dsel + 1]
                self.act(DEC.rearrange("p (n c) -> p n c", c=1), bend, AF.Exp, [R("bc")], [R("DEC")])
                self.tt(POOL, tq, bc, tq, ALU.subtract, [R("bc"), R("tq")], [R("tq")])
                self.act(tq, tq, AF.Exp, [R("tq")], [R("tq")])
                self.tt(DVE, QE, qraw, tq, ALU.mult, [R("qraw"), R("tq")], [R("QE")])
                self.tt(DVE, zraw, zraw, Bn, ALU.add, [R("zraw"), R("Bn")], [R("zraw")])
                self.tt(DVE, zraw, zraw, bc, ALU.add, [R("zraw"), R("bc")], [R("zraw")])
                self.act(KK, zraw, AF.Exp, [R("zraw"), "lnoml"], [R("KK")], scale=-1.0, bias=lnoml)
                self.tt(POOL, zraw.rearrange("p (n c) -> p n c", c=CH), zraw.rearrange("p (n c) -> p n c", c=CH),
                        bend.broadcast_to([128, 8, CH]), ALU.subtract, [R("zraw"), R("bc")], [R("zraw")])
                self.act(KDT, zraw, AF.Exp, [R("zraw"), "lnoml"], [R("KDT")], scale=-1.0, bias=lnoml)
                yield
                for blk in range(4):
                    bsl = slice(blk * 128, (blk + 1) * 128)
                    self.mm(self.ps[4][:, bsl], ("ps", 4), [(KK[:, bsl], QE[:, bsl])], [R("KK"), R("QE")])
                for blk in range(4):
                    bsl = slice(blk * 128, (blk + 1) * 128)
                    self.tt(DVE, attm[:, blk, :], self.ps[4][:, bsl], mask, ALU.mult, [("ps", 4), "cbf"], [R("attm")])

                def emitk(e):
                    for jj in range(4):
                        ins = e.transpose(out=self.psb[:, 512 + jj * 128:512 + (jj + 1) * 128], in_=KDT[:, jj * 128:(jj + 1) * 128], identity=self.identb)
                    return ins
                P.op(PE, emitk, reads=[R("KDT"), "cbf"], writes=["psb"])
                self.act(KDtok, self.psb[:, 512:1024].rearrange("p (n c) -> p n c", c=128), AF.Copy, ["psb"], [R("KDtok")])
                yield
                order = list(range(8)) if dr == 0 else list(range(7, -1, -1))
                for q, cpos in enumerate(order):
                    blk, c = cpos // 2, cpos % 2
                    pr = slice(c * CH, (c + 1) * CH)
                    bank = 5 + q % 2
                    self.mm(self.ps[bank][:, (q // 2) * 128:(q // 2 + 1) * 128], ("ps", bank), [(KDtok[pr, blk, :], Vtok[pr, blk, :])], [R("KDtok"), R("Vtok")])
                self.cp(DVE, S32[:, 0, :], S32[:, 8, :], [R("S32")], [R("S32")])
                self.cp(POOL, Sbf[:, 0, :], Sbf[:, 8, :], [R("Sbf")], [R("Sbf")])
                for q, cpos in enumerate(order):
                    bank = 5 + q % 2
                    self.stt(DVE, S32[:, q + 1, :], S32[:, q, :], DEC[:, cpos:cpos + 1], self.ps[bank][:, (q // 2) * 128:(q // 2 + 1) * 128],
                             ALU.mult, ALU.add, [R("S32"), R("DEC"), ("ps", bank)], [R("S32")])
                self.act(b["Sbf_flat"][:, 128:1152], b["S32_flat"][:, 128:1152], AF.Copy, [R("S32")], [R("Sbf")])
                self.pe_fill(FILL2, 3)
                yield
                for q, cpos in enumerate(order):
                    blk, c = cpos // 2, cpos % 2
                    cs = slice(cpos * CH, (cpos + 1) * CH)
                    pairs = [(Vtok[:, blk, :], attm[:, blk, c * CH:(c + 1) * CH]), (Sbf[:, q, :], QE[:, cs])]
                    self.mm(self.ps[3][:, cs], ("ps", 3), pairs, [R("Vtok"), R("attm"), R("Sbf"), R("QE")])
                tsl = slice(t * 512, (t + 1) * 512)
                self.act(O[:, tsl], self.ps[3][:, :], AF.Copy, [("ps", 3)], [("O", dr, t)])
                yield ("done", t)

        oi = [0]

        def finalize(h, t):
            tsl = slice(t * 512, (t + 1) * 512)
            self.proj(0, wsl[4], ("hw", hwset[0], 4), t)
            self.act(gsl, self.ps[0][:, :], AF.Exp, [("ps", 0)], ["gsl"], scale=-1.0)
            self.tt(DVE, og, CB[0]["O"][:, tsl], CB[1]["O"][:, tsl], ALU.add, [("O", 0, t), ("O", 1, t)], ["og"])
            self.act(osq, og, AF.Square, ["og"], ["osq"])
            self.mm(self.ps[1][:, :], ("ps", 1), [(self.onesb, osq)], ["cbf", "osq"])
            self.act(rs, self.ps[1][:, :], AF.Ln, [("ps", 1), "misc"], ["rs"], scale=1.0 / 128, bias=self.misc[:, 1:2])
            self.act(rs, rs, AF.Exp, ["rs"], ["rs"], scale=-0.5)
            self.act(gsl, gsl, AF.Ln, ["gsl", "misc"], ["gsl"], bias=one_c)
            self.act(gsl, gsl, AF.Exp, ["gsl"], ["gsl"], scale=-1.0)
            self.tt(DVE, og, og, rs, ALU.mult, ["og", "rs"], ["og"])
            self.tt(POOL, og, og, gsl, ALU.mult, ["og", "gsl"], ["og"])
            o = otl[oi[0] % 2]
            oreg = ("hotl", oi[0] % 2)
            oi[0] += 1
            self.stt(DVE, o, og, self.vec(L, 1, h), self.ps[0][:, :], ALU.mult, ALU.mult, ["og", ("ps", 0), "vecs"], [oreg])
            self.dma(self.OUTX[12 + h, :, tsl], o, [oreg], [("OUTX", 12 + h, t)], oreg)

        def load_hw(hh):
            st_ = hh % 2
            for wi_, off in enumerate((OFF_HQ, OFF_HI, OFF_HFF, OFF_HFB, OFF_HG)):
                s = self.wi % 2
                self.wi += 1
                st = self.wst[s][:, 0:8, :]
                self.dma(st, W[:, off + hh * 128:off + (hh + 1) * 128].rearrange("(k p) c -> p k c", p=128), [], [("wst", s)], ("wst", s))
                self.cp(POOL, wsl2[st_][wi_], st, [("wst", s)], [("hw", st_, wi_)])
        load_hw(0)
        for h in range(8):
            hwset[0] = h % 2
            wsl[:] = wsl2[h % 2]
            if h + 1 < 8:
                load_hw(h + 1)
            gens = [chain(h, 0), chain(h, 1)] if not os.environ.get('KDBG_ONECHAIN') else [chain(h, 0)]
            done = [set(), set()]
            alive = [True] * len(gens)
            for _ in range(int(os.environ.get('KSKEW', '3'))):
                next(gens[0])
            while any(alive):
                for gi, g in enumerate(gens):
                    if not alive[gi]:
                        continue
                    try:
                        r = next(g)
                    except StopIteration:
                        alive[gi] = False
                        continue
                    if r is not None:
                        t = r[1]
                        done[gi].add(t)
                        if len(gens) == 2 and t in done[1 - gi]:
                            finalize(h, t)
        P.barrier()

    def stage_merge(self, L, X):
        KX = (8, 4, 8)[X]
        c0 = (0, 8, 12)[X]
        Wb = self.ba(0, 8192).rearrange("p (k c) -> p k c", c=1024)
        Wg = self.ba(8192, 8192).rearrange("p (k c) -> p k c", c=1024)
        otl = [self.ba(16384 + i * 4096, 4096).rearrange("p (k t) -> p k t", t=512) for i in range(2)]
        gyt = [self.ba(24576 + i * 512, 512) for i in range(2)]
        sg = [self.fa(i * 512, 512) for i in range(2)]
        for k in range(KX):
            self.load_rows(self.w_br[X][L, k * 128:(k + 1) * 128, :], Wb[:, k, :], ("Wb", k))
        goff = OFF_GATE + X * 1024
        for k in range(8):
            self.load_rows(self.w_in[L, k * 128:(k + 1) * 128, goff:goff + 1024], Wg[:, k, :], ("Wg", k))
        gi = 0
        for t in range(8):
            o = t % 2
            tsl = slice(t * 512, (t + 1) * 512)
            self.dma(otl[o][:, 0:KX, :], self.OUTX[c0:c0 + KX, :, tsl].rearrange("c p t -> p c t"),
                     [("OUTX", c0 + k, t) for k in range(KX)], [("mo", o)], ("mo", o))
            for c in range(8):
                by, bg = (0, 1) if c % 2 == 0 else (2, 3)
                csl = slice(c * 128, (c + 1) * 128)
                self.mm(self.ps[by][:, :], ("ps", by), [(Wb[:, k, csl], otl[o][:, k, :]) for k in range(KX)], [("Wb", k) for k in range(KX)] + [("mo", o)])
                self.mm(self.ps[bg][:, :], ("ps", bg), [(Wg[:, k, csl], self.A[:, k, tsl]) for k in range(8)], [("Wg", k) for k in range(8)] + [("A", k, t) for k in range(8)])
                s = gi % 2
                gi += 1
                self.act(sg[s], self.ps[bg][:, :], AF.Sigmoid, [("ps", bg)], [("sg", s)])
                self.tt(DVE, gyt[s], self.ps[by][:, :], sg[s], ALU.mult, [("ps", by), ("sg", s)], [("gyt", s)])
                self.dma(self.GY[X, c, :, tsl], gyt[s], [("gyt", s)], [("GY", X, c, t)], ("gyt", s))
        self.P.barrier()

    def ln_tile(self, L, which, xn, xnreg, xsq, stat, xr, xreg, tok0, ntok):
        gj, bj = (2, 3) if which == 1 else (4, 5)
        mean, msq, rstd = stat
        self.act(xsq, xn, AF.Square, [xnreg], ["xsq"])
        self.mm(self.ps[4][:, 0:ntok], ("ps", 4), [(self.onesD[:], xn[:, c, :]) for c in range(8)], ["onesD", xnreg])
        self.mm(self.ps[5][:, 0:ntok], ("ps", 5), [(self.onesD[:], xsq[:, c, :]) for c in range(8)], ["onesD", "xsq"])
        self.cp(DVE, mean, self.ps[4][:, 0:ntok], [("ps", 4)], ["mean"])
        self.tt(DVE, msq, mean, mean, ALU.mult, ["mean"], ["msq"])
        self.tt(DVE, msq, self.ps[5][:, 0:ntok], msq, ALU.subtract, [("ps", 5), "msq"], ["msq"])
        self.act(rstd, msq, AF.Ln, ["msq", "misc"], ["rstd"], bias=self.misc[:, 0:1])
        self.act(rstd, rstd, AF.Exp, ["rstd"], ["rstd"], scale=-0.5)
        mb = mean.rearrange("p (o t) -> p o t", o=1).broadcast_to([128, 8, ntok])
        rb = rstd.rearrange("p (o t) -> p o t", o=1).broadcast_to([128, 8, ntok])
        self.tt(DVE, xn, xn, mb, ALU.subtract, [xnreg, "mean"], [xnreg])
        self.tt(DVE, xn, xn, rb, ALU.mult, [xnreg, "rstd"], [xnreg])
        for c in range(8):
            self.act(xr[:, c, :], xn[:, c, :], AF.Identity, [xnreg, "vecs"], [xreg], scale=self.vec(L, gj, c), bias=self.vec(L, bj, c))
        t512 = tok0 // 512
        self.cp(DVE, self.A[:, :, tok0:tok0 + ntok], xr, [xreg], [("A", k, t512) for k in range(8)])

    def stage_mix(self, L):
        NTK = 256
        NT_ = S // NTK
        wo = self.ba(0, 8192).rearrange("p (k c) -> p k c", c=1024)
        gy3 = self.ba(8192, 3 * 2048).rearrange("p (x k t) -> p x k t", x=3, k=8)
        mg = self.ba(8192 + 6144, 2048).rearrange("p (k t) -> p k t", t=NTK)
        xr = [self.fa(i * 2048, 2048).rearrange("p (k t) -> p k t", t=NTK) for i in range(2)]
        xn = [self.fa(4096 + i * 2048, 2048).rearrange("p (k t) -> p k t", t=NTK) for i in range(2)]
        xsq = self.fa(8192, 2048).rearrange("p (k t) -> p k t", t=NTK)
        stat = [self.fa(10240 + i * NTK, NTK) for i in range(3)]
        tmpm = self.AF_[:, 0:2048].rearrange("p (k t) -> p k t", t=NTK)
        for k in range(8):
            self.load_rows(self.w_out[L, k * 128:(k + 1) * 128, :], wo[:, k, :], ("wo", k))

        def front(t):
            sl = t % 2
            tsl = slice(t * NTK, (t + 1) * NTK)
            t5 = (t * NTK) // 512
            for X in range(3):
                self.dma(gy3[:, X], self.GY[X, :, :, tsl].rearrange("c p t -> p c t"), [("GY", X, c, t5) for c in range(8)], [("gy3", X)], ("gy3", X))
            self.dma(xr[sl], self.XRES[:, :, tsl].rearrange("c p t -> p c t"), [("XRES", t5)], [("xr", sl)], ("xr_ld", sl))
            self.tt(POOL, tmpm, gy3[:, 0], gy3[:, 1], ALU.add, [("gy3", 0), ("gy3", 1)], ["tmpm"])
            self.tt(POOL, mg, tmpm, gy3[:, 2], ALU.add, ["tmpm", ("gy3", 2)], ["mg"])
            for c in range(8):
                bank = c % 4
                self.mm(self.ps[bank][:, 0:NTK], ("ps", bank), [(wo[:, k, c * 128:(c + 1) * 128], mg[:, k, :]) for k in range(8)], [("wo", k) for k in range(8)] + ["mg"])
                self.stt(DVE, xn[sl][:, c, :], xr[sl][:, c, :], ALPHA, self.ps[bank][:, 0:NTK], ALU.mult, ALU.add, [("xr", sl), ("ps", bank)], [("xn", sl)])

        def back(t):
            sl = t % 2
            tsl = slice(t * NTK, (t + 1) * NTK)
            t5 = (t * NTK) // 512
            self.ln_tile(L, 1, xn[sl], ("xn", sl), xsq, stat, xr[sl], ("xr", sl), t * NTK, NTK)
            self.dma(self.XRES[:, :, tsl].rearrange("c p t -> p c t"), xr[sl], [("xr", sl)], [("XRES", t5)], ("xr_st", sl), eng=ACT)

        front(0)
        for t in range(NT_):
            if t + 1 < NT_:
                front(t + 1)
            back(t)
        self.P.barrier()

    def stage_ffn_up(self, L):
        P = self.P
        ht = [self.ba(i * 512, 512) for i in range(2)]
        sgl = [self.fa(i * 512, 512) for i in range(2)]
        hi_ = 0
        nxt = (self.load_w(self.w_fg[L], 8, 0), self.load_w(self.w_fu[L], 8, 0))
        for f in range(NF):
            (wg, rg), (wu, ru) = nxt
            if f + 1 < NF:
                nxt = (self.load_w(self.w_fg[L], 8, (f + 1) * 128), self.load_w(self.w_fu[L], 8, (f + 1) * 128))
            for t in range(8):
                bg, bu = (0, 1) if t % 2 == 0 else (2, 3)
                self.proj(bg, wg, rg, t)
                self.proj(bu, wu, ru, t)
                s = hi_ % 2
                hi_ += 1
                self.act(sgl[s], self.ps[bg][:, :], AF.Silu, [("ps", bg)], [("sgl", s)])
                self.tt(DVE, ht[s], self.ps[bu][:, :], sgl[s], ALU.mult, [("ps", bu), ("sgl", s)], [("ht", s)])
                self.dma(self.HH[f, :, t * 512:(t + 1) * 512], ht[s], [("ht", s)], [("HH", f, t)], ("ht", s))
        PT = self.ba(1024, 8192).rearrange("p (k t) -> p k t", k=2)
        pin = [self.fa(1024 + i * 256, 256) for i in range(2)]
        for tt_ in range(32):
            s = tt_ % 2
            self.dma(pin[s], self.p[L, tt_ * 128:(tt_ + 1) * 128, :], [], [("pin", s)], ("pin", s))
            bank = 4 + s

            def emit(e, s=s, bank=bank):
                for j in range(2):
                    ins = e.transpose(out=self.ps[bank][:, j * 128:(j + 1) * 128], in_=pin[s][:, j * 128:(j + 1) * 128], identity=self.idf[:])
                return ins
            P.op(PE, emit, reads=[("pin", s), "idf"], writes=[("ps", bank)])
            self.cp(DVE, PT[:, :, tt_ * 128:(tt_ + 1) * 128], self.ps[bank][:, 0:256].rearrange("p (c t) -> p c t", t=128), [("ps", bank)], ["PT"])
        wppb2 = [self.ba(9216 + i * 256, 256).rearrange("p (k c) -> p k c", c=128) for i in range(2)]
        wpps2 = [self.fa(1536 + i * 256, 256).rearrange("p (k c) -> p k c", c=128) for i in range(2)]

        def load_ple(c):
            i = c % 2
            self.dma(wpps2[i], self.w_pp[L][:, c * 128:(c + 1) * 128].rearrange("(k p) c -> p k c", p=128), [], [("wpps", i)], ("wpps", i))
            self.cp(POOL, wppb2[i], wpps2[i], [("wpps", i)], [("wppb", i)])
            return (wppb2[i], ("wppb", i)), self.load_w(self.w_pg[L], 8, c * 128)
        nxt = load_ple(0)
        for c in range(8):
            (wppb, rpp), (wpg, rpg) = nxt
            if c + 1 < 8:
                nxt = load_ple(c + 1)
            for t in range(8):
                bp, bg = (0, 1) if t % 2 == 0 else (2, 3)
                tsl = slice(t * 512, (t + 1) * 512)
                self.mm(self.ps[bp][:, :], ("ps", bp), [(wppb[:, k, :], PT[:, k, tsl]) for k in range(2)], [rpp, "PT"])
                self.proj(bg, wpg, rpg, t)
                s = hi_ % 2
                hi_ += 1
                self.act(sgl[s], self.ps[bg][:, :], AF.Sigmoid, [("ps", bg)], [("sgl", s)])
                self.tt(DVE, ht[s], self.ps[bp][:, :], sgl[s], ALU.mult, [("ps", bp), ("sgl", s)], [("ht", s)])
                self.dma(self.PLE[c, :, tsl], ht[s], [("ht", s)], [("PLE", c, t)], ("ht", s))
        P.barrier()

    def stage_ffn_down(self, L, last):
        P = self.P
        NTK = 256
        NT_ = S // NTK
        bb = lambda off, n: self.AB_[:, 4096 + off:4096 + off + n]
        Wd = bb(0, NF * 1024).rearrange("p (k c) -> p k c", c=1024)
        ht = [bb(NF * 1024, NF * NTK).rearrange("p (k t) -> p k t", t=NTK)]
        plt = bb(NF * 1024 + NF * NTK, 8 * NTK).rearrange("p (k t) -> p k t", t=NTK)
        xr = [self.fa(i * 2048, 2048).rearrange("p (k t) -> p k t", t=NTK) for i in range(2)]
        xn = [self.fa(4096 + i * 2048, 2048).rearrange("p (k t) -> p k t", t=NTK) for i in range(2)]
        xsq = self.fa(8192, 2048).rearrange("p (k t) -> p k t", t=NTK)
        stat = [self.fa(10240 + i * NTK, NTK) for i in range(3)]
        otok = self.AF_[:, 0:2048]
        for k in range(NF):
            self.load_rows(self.w_fd[L, k * 128:(k + 1) * 128, :], Wd[:, k, :], ("Wd", k))
        if last:
            P.barrier()

        def front(t):
            sl = t % 2
            tsl = slice(t * NTK, (t + 1) * NTK)
            t5 = (t * NTK) // 512
            self.dma(ht[0], self.HH[:, :, tsl].rearrange("c p t -> p c t"), [("HH", f, t5) for f in range(NF)], ["htd"], "htd")
            self.dma(plt, self.PLE[:, :, tsl].rearrange("c p t -> p c t"), [("PLE", c, t5) for c in range(8)], ["plt"], "plt")
            self.dma(xr[sl], self.XRES[:, :, tsl].rearrange("c p t -> p c t"), [("XRES", t5)], [("xr", sl)], ("xr_ld", sl))
            for c in range(8):
                bank = c % 4
                self.mm(self.ps[bank][:, 0:NTK], ("ps", bank), [(Wd[:, k, c * 128:(c + 1) * 128], ht[0][:, k, :]) for k in range(NF)], [("Wd", k) for k in range(NF)] + ["htd"])
                self.stt(DVE, xn[sl][:, c, :], xr[sl][:, c, :], ALPHA, self.ps[bank][:, 0:NTK], ALU.mult, ALU.add, [("xr", sl), ("ps", bank)], [("xn", sl)])
            self.tt(POOL, xn[sl], xn[sl], plt, ALU.add, [("xn", sl), "plt"], [("xn", sl)])

        def back(t):
            sl = t % 2
            tsl = slice(t * NTK, (t + 1) * NTK)
            t5 = (t * NTK) // 512
            self.ln_tile(L, 2, xn[sl], ("xn", sl), xsq, stat, xr[sl], ("xr", sl), t * NTK, NTK)
            if not last:
                self.dma(self.XRES[:, :, tsl].rearrange("c p t -> p c t"), xr[sl], [("xr", sl)], [("XRES", t5)], ("xr_st", sl), eng=ACT)
            else:
                for tb in range(NTK // 128):
                    for hf in range(2):
                        bank = 6

                        def emit(e, tb=tb, hf=hf, bank=bank, sl=sl):
                            for j in range(4):
                                ins = e.transpose(out=self.ps[bank][:, j * 128:(j + 1) * 128], in_=xr[sl][:, hf * 4 + j, tb * 128:(tb + 1) * 128], identity=self.idf[:])
                            return ins
                        P.op(PE, emit, reads=[("xr", sl), "idf"], writes=[("ps", bank)])
                        self.act(otok[:, tb * 1024 + hf * 512:tb * 1024 + (hf + 1) * 512], self.ps[bank][:, :], AF.Copy, [("ps", bank)], [("otok", tb)])
                    self.dma(self.out[t * NTK + tb * 128:t * NTK + (tb + 1) * 128, :], otok[:, tb * 1024:(tb + 1) * 1024], [("otok", tb)], [("out", t, tb)], ("ost", tb), eng=ACT)

        front(0)
        for t in range(NT_):
            if t + 1 < NT_:
                front(t + 1)
            back(t)
        P.barrier()

    def dump(self, slot, src_dram_bf16_rows):
        pass

    def build(self, stages=None):
        self.declare()
        with self.stack:
            self.prologue()
            for L in range(self.n_layers):
                last = L == self.n_layers - 1
                on = lambda nm: stages is None or nm in stages
                if on("pool"):
                    self.stage_pool(L)
                if on("merge0"):
                    self.stage_merge(L, 0)
                if on("attn"):
                    self.stage_attn(L)
                if on("merge1"):
                    self.stage_merge(L, 1)
                if on("hgrn"):
                    self.stage_hgrn(L)
                if on("merge2"):
                    self.stage_merge(L, 2)
                if on("mix"):
                    self.stage_mix(L)
                if on("ffn_up"):
                    self.stage_ffn_up(L)
                if on("ffn_down"):
                    self.stage_ffn_down(L, last)
            if self.dbg and not os.environ.get('KDBG_NODUMP'):
                self.dbg_dump()
            self.P.finalize(self.stack)
        return self.nc

    def dbg_dump(self):
        tb = self.ba(0, 4096)
        tf = self.fa(0, 4096)
        for c in range(20):
            self.dma(tb, self.OUTX[c], [("OUTX", c, t) for t in range(8)], ["dtb"], "dtb")
            self.cp(DVE, tf, tb, ["dtb"], ["dtf"])
            self.dma(self.dbg_out[c], tf, ["dtf"], [("dbg", c)], "dtf")


_CACHE = {}


def kernel(**inputs):
    consts = make_consts()
    vecs = pack_vecs(inputs)
    if "nc" not in _CACHE:
        _CACHE["nc"] = Builder().build()
    nc = _CACHE["nc"]
    shared = {k: np.ascontiguousarray(np.asarray(inputs[k], np.float32)) for k in (
        "w_in", "pool_w", "w_branch_a", "w_branch_b", "w_branch_c", "w_out", "w_ffn_gate", "w_ffn_up",
        "w_ffn_down", "w_ple_proj", "w_ple_gate")}
    shared.update(consts)
    shared["vecs"] = vecs
    x = np.asarray(inputs["x"], np.float32)
    p = np.asarray(inputs["p"], np.float32)
    in_maps = []
    for c in range(NCORES):
        m = dict(shared)
        m["x"] = np.ascontiguousarray(x[c])
        m["p"] = np.ascontiguousarray(p[:, c])
        in_maps.append(m)
    res = run_bass_kernel_spmd(nc, in_maps, core_ids=list(range(NCORES)))
    return np.stack([np.asarray(r["out"], np.float32) for r in res.results], axis=0)
```

```python
import contextlib
import os
import numpy as np
import ml_dtypes
import concourse.bass as bass
import concourse.mybir as mybir
from concourse.bass_utils import run_bass_kernel_spmd

F32 = mybir.dt.float32
BF16 = mybir.dt.bfloat16
AF = mybir.ActivationFunctionType
ALU = mybir.AluOpType
PE, ACT, DVE, POOL, SP = "pe", "act", "dve", "pool", "sp"
ENGS = (PE, ACT, DVE, POOL, SP)
SEM_LIMIT = 24000
RELAX_N = int(os.environ.get('KRELAX', '128'))
FILL1 = int(os.environ.get('KFILL1', '0'))
FILL2 = int(os.environ.get('KFILL2', '0'))

S = 4096
D = 1024
DEPTH = 4
NCORES = 8
FF = 2816
NF = 22
INW = 13824
ALPHA = float((2 * DEPTH) ** 0.25)
LN_EPS = 1e-5
RMS_EPS = 1e-6
OFF_POOL, OFF_AQ, OFF_AK, OFF_AV = 0, 1024, 2560, 4096
OFF_HQ, OFF_HI, OFF_HFF, OFF_HFB, OFF_HG, OFF_GATE = 5632, 6656, 7680, 8704, 9728, 10752
ATT_D = (1, 4, 16)
CH = 64


def fsz(ap):
    n = 1
    for d in ap.shape[1:]:
        n *= int(d)
    return n


class Prog:
    def __init__(self, nc, same_engine_sync=True):
        self.nc = nc
        self.ops = []
        self.by_eng = {e: [] for e in ENGS}
        self.last_w = {}
        self.readers = {}
        self.same_engine_sync = same_engine_sync
        self.pending = {e: set() for e in ENGS}

    def op(self, eng, emit, reads=(), writes=(), dma=None, n=0):
        oid = len(self.ops)
        deps = set()
        for r in reads:
            w = self.last_w.get(r)
            if w is not None:
                deps.add(w)
            if (isinstance(r, tuple) and r[0] == "ps") or (isinstance(r, str) and r.startswith("psb")):
                rd = self.readers.get(r)
                if rd:
                    deps.update(rd.values())
        for r in writes:
            w = self.last_w.get(r)
            if w is not None:
                deps.add(w)
            rd = self.readers.get(r)
            if rd:
                deps.update(rd.values())
        if self.pending[eng]:
            deps.update(self.pending[eng])
            self.pending[eng] = set()
        keep = []
        for d in deps:
            o = self.ops[d]
            if o["dma"] is None and o["eng"] == eng and dma is None:
                if eng == PE or not self.same_engine_sync or o["n"] >= RELAX_N:
                    continue
            keep.append(d)
            o["target"] = True
        rec = dict(id=oid, eng=eng, emit=emit, deps=keep, dma=dma, target=False, ev=None, n=n)
        self.ops.append(rec)
        self.by_eng[eng].append(rec)
        lane = ("dma", dma) if dma is not None else ("eng", eng)
        for r in reads:
            self.readers.setdefault(r, {})[lane] = oid
        for r in writes:
            self.last_w[r] = oid
            self.readers[r] = {}
        return oid

    def barrier(self):
        last = {}
        for o in self.ops:
            lane = ("dma", o["dma"]) if o["dma"] is not None else ("eng", o["eng"])
            last[lane] = o["id"]
        pre = set(last.values())
        for e in ENGS:
            self.pending[e] = set(pre)

    def finalize(self, stack):
        nc = self.nc
        sems = {}

        def get_sem(key):
            if key not in sems:
                sems[key] = stack.enter_context(nc.semaphore("s%d" % len(sems)))
            return sems[key]

        cnt = {e: 0 for e in ENGS}
        gen = {e: 0 for e in ENGS}
        dcnt = {}
        final = {}
        for o in self.ops:
            if o["dma"] is not None:
                k = ("dma", o["dma"])
                dcnt[k] = dcnt.get(k, 0) + 16
                o["ev"] = (get_sem(k), dcnt[k], k)
                final[k] = (o["ev"][0], dcnt[k])
            elif o["target"]:
                e = o["eng"]
                if cnt[e] >= SEM_LIMIT:
                    gen[e] += 1
                    cnt[e] = 0
                cnt[e] += 1
                k = ("eng", e, gen[e])
                o["ev"] = (get_sem(k), cnt[e], k)
        self.n_sems = len(sems)
        block = stack.enter_context(nc.Block())
        ops = self.ops
        by_eng = self.by_eng

        def run_stream(engname, eobj):
            known = {}
            for o in by_eng[engname]:
                for d in o["deps"]:
                    s, v, k = ops[d]["ev"]
                    if known.get(k, 0) >= v:
                        continue
                    known[k] = v
                    eobj.wait_ge(s, v)
                ins = o["emit"](eobj)
                if o["ev"] is not None:
                    s, v, k = o["ev"]
                    ins.then_inc(s, 16 if o["dma"] is not None else 1)
            if engname == SP:
                for k, (s, v) in final.items():
                    if known.get(k, 0) < v:
                        eobj.wait_ge(s, v)

        @block.tensor
        def _(e):
            run_stream(PE, e)

        @block.scalar
        def _(e):
            run_stream(ACT, e)

        @block.vector
        def _(e):
            run_stream(DVE, e)

        @block.gpsimd
        def _(e):
            run_stream(POOL, e)

        @block.sync
        def _(e):
            run_stream(SP, e)


NVEC_PER_L = 8 * 6
VEC_LB = DEPTH * NVEC_PER_L
NVEC = VEC_LB + DEPTH * 16


def _fm(v):
    return np.ascontiguousarray(v.reshape(-1, 128).T)


def make_consts():
    c = {}
    c["c_idf"] = np.eye(128, dtype=np.float32)
    c["c_onesD"] = np.full((128, 128), 1.0 / D, np.float32)
    reset = np.ones((128, 512), np.float32)
    reset[:, ::CH] = 0.0
    c["c_reset"] = reset
    s = np.arange(128)[:, None]
    t = np.arange(128)[None, :]
    same = (s // CH) == (t // CH)
    mF = (same & (s <= t)).astype(np.float32)
    mB = (same & (s >= t)).astype(np.float32)
    lo0 = np.ones((128, 128), np.float32)
    lo0[:64] = 0
    hi0 = np.ones((128, 128), np.float32)
    hi0[64:] = 0
    bfc = np.concatenate([np.eye(128, dtype=np.float32), np.ones((128, 128), np.float32), lo0, hi0, mF, mB], axis=1)
    c["c_bf"] = bfc.astype(ml_dtypes.bfloat16)
    slopes = 2.0 ** (-8.0 * np.arange(1, 13, dtype=np.float64) / 12)
    kp = np.arange(128)[:, None]
    qf = np.arange(128)[None, :]
    offA = kp - 64 - qf
    offB = kp + 64 - qf
    bias = np.zeros((12, 128, 256), np.float32)
    sq = np.sqrt(128.0)
    for h in range(12):
        d = ATT_D[h // 4]
        for j, off in enumerate((offA, offB)):
            b = -slopes[h] * d * np.abs(off) * sq
            b = np.where(np.abs(off) <= 64, b, -30000.0)
            bias[h, :, j * 128:(j + 1) * 128] = b
    c["c_abias"] = bias.astype(ml_dtypes.bfloat16)
    rc = np.zeros((128, 4, 16), np.float32)
    for g in range(4):
        h = 1 << g
        tt = np.concatenate([np.arange(8), np.arange(S - 8, S)])
        cntv = np.minimum(tt + h, S) - np.maximum(tt - h, 0)
        rc[:, g, :] = (1.0 / cntv)[None, :]
    c["c_rc"] = rc.reshape(128, 64)
    return c


def pack_vecs(inp):
    v = np.zeros((128, NVEC), np.float32)
    for l in range(DEPTH):
        for j, nm in enumerate(("pool_scale", "hgrn_norm_w", "ln1_g", "ln1_b", "ln2_g", "ln2_b")):
            v[:, l * NVEC_PER_L + j * 8:l * NVEC_PER_L + (j + 1) * 8] = _fm(np.asarray(inp[nm][l], np.float32))
        v[:, VEC_LB + l * 16:VEC_LB + (l + 1) * 16] = np.asarray(inp["hgrn_lb_logits"][l], np.float32).reshape(16, 128).T
    return v


class Builder:
    def __init__(self, n_layers=DEPTH, dbg=None, sync=True):
        self.n_layers = n_layers
        self.dbg = dbg
        self.nc = bass.Bass("TRN2", target_bir_lowering=False)
        self.P = Prog(self.nc, same_engine_sync=sync)
        self.stack = contextlib.ExitStack()
        self.wi = 0
        self.ri = 0
        self.pbi = 0

    def dram_in(self, name, shape, dt=F32):
        return self.nc.dram_tensor(name, list(shape), dt, kind="ExternalInput").ap()

    def sb(self, name, shape, dt):
        return self.stack.enter_context(self.nc.sbuf_tensor("sb_" + name, list(shape), dt))

    def declare(self):
        nc = self.nc
        self.x = self.dram_in("x", [S, D])
        self.p = self.dram_in("p", [DEPTH, S, 256])
        self.w_in = self.dram_in("w_in", [DEPTH, D, INW])
        self.pool_w = self.dram_in("pool_w", [DEPTH, 4, 256, 256])
        self.w_br = [self.dram_in("w_branch_a", [DEPTH, 1024, D]), self.dram_in("w_branch_b", [DEPTH, 512, D]),
                     self.dram_in("w_branch_c", [DEPTH, 1024, D])]
        self.w_out = self.dram_in("w_out", [DEPTH, D, D])
        self.w_fg = self.dram_in("w_ffn_gate", [DEPTH, D, FF])
        self.w_fu = self.dram_in("w_ffn_up", [DEPTH, D, FF])
        self.w_fd = self.dram_in("w_ffn_down", [DEPTH, FF, D])
        self.w_pp = self.dram_in("w_ple_proj", [DEPTH, 256, D])
        self.w_pg = self.dram_in("w_ple_gate", [DEPTH, D, D])
        self.vecs_d = self.dram_in("vecs", [128, NVEC])
        self.c_idf = self.dram_in("c_idf", [128, 128])
        self.c_onesD = self.dram_in("c_onesD", [128, 128])
        self.c_reset = self.dram_in("c_reset", [128, 512])
        self.c_bf = self.dram_in("c_bf", [128, 768], BF16)
        self.c_abias = self.dram_in("c_abias", [12, 128, 256], BF16)
        self.c_rc = self.dram_in("c_rc", [128, 64])
        self.out = nc.dram_tensor("out", [S, D], F32, kind="ExternalOutput").ap()
        if self.dbg:
            self.dbg_out = nc.dram_tensor("dbg", [20, 128, S], F32, kind="ExternalOutput").ap()
        self.XRES = nc.dram_tensor("XRES", [8, 128, S], F32, kind="Internal").ap()
        self.OUTX = nc.dram_tensor("OUTX", [20, 128, S], BF16, kind="Internal").ap()
        self.GY = nc.dram_tensor("GY", [3, 8, 128, S], BF16, kind="Internal").ap()
        self.HH = nc.dram_tensor("HH", [NF, 128, S], BF16, kind="Internal").ap()
        self.PLE = nc.dram_tensor("PLE", [8, 128, S], BF16, kind="Internal").ap()
        self.A = self.sb("A", [128, 8, S], BF16)
        self.AF_ = self.sb("arenaF", [128, 16384], F32)
        self.AB_ = self.sb("arenaB", [128, 35840], BF16)
        self.idf = self.sb("idf", [128, 128], F32)
        self.onesD = self.sb("onesD", [128, 128], F32)
        self.reset = self.sb("reset", [128, 512], F32)
        self.cbf = self.sb("cbf", [128, 768], BF16)
        self.rc = self.sb("rc", [128, 64], F32)
        self.vecs = self.sb("vecs", [128, NVEC], F32)
        self.lb = self.sb("lb", [128, DEPTH * 16], F32)
        self.oml = self.sb("oml", [128, DEPTH * 16], F32)
        self.lnoml = self.sb("lnoml", [128, DEPTH * 16], F32)
        self.misc = self.sb("misc", [128, 128], F32)
        self.ps = [self.stack.enter_context(nc.psum_tensor("ps%d" % i, [128, 512], F32)) for i in range(7)]
        self.psb = self.stack.enter_context(nc.psum_tensor("psb", [128, 1024], BF16))
        self.identb = self.cbf[:, 0:128]
        self.onesb = self.cbf[:, 128:256]
        self.lo0 = self.cbf[:, 256:384]
        self.hi0 = self.cbf[:, 384:512]
        self.maskF = self.cbf[:, 512:640]
        self.maskB = self.cbf[:, 640:768]
        self.wst = [self.AF_[:, i * 1024:(i + 1) * 1024].rearrange("p (k c) -> p k c", c=128) for i in range(2)]
        self.rst = [self.AF_[:, 2048 + i * 1024:2048 + (i + 1) * 1024] for i in range(2)]
        self.wbf = [self.AB_[:, i * 1024:(i + 1) * 1024].rearrange("p (k c) -> p k c", c=128) for i in range(6)]
        self.F0 = 4096
        self.B0 = 6144

    def fa(self, off, n):
        assert self.F0 + off + n <= 16384, (off, n)
        return self.AF_[:, self.F0 + off:self.F0 + off + n]

    def ba(self, off, n):
        assert self.B0 + off + n <= 35840, (off, n)
        return self.AB_[:, self.B0 + off:self.B0 + off + n]

    def dma(self, out, in_, reads, writes, key, eng=SP):
        self.P.op(eng, lambda e: e.dma_start(out=out, in_=in_), reads=reads, writes=writes, dma=key)

    def load_w(self, W2d, K, c0, ncols=128):
        s = self.wi % 2
        b = self.wi % 6
        self.wi += 1
        st = self.wst[s][:, 0:K, 0:ncols]
        wb = self.wbf[b][:, 0:K, 0:ncols]
        src = W2d[:, c0:c0 + ncols].rearrange("(k p) c -> p k c", p=128)
        self.dma(st, src, [], [("wst", s)], ("wst", s))
        self.P.op(POOL, lambda e: e.tensor_copy(out=wb, in_=st), reads=[("wst", s)], writes=[("wbf", b)])
        return self.wbf[b], ("wbf", b)

    def load_rows(self, Wrows, dest, dreg, ncols=1024):
        s = self.ri % 2
        self.ri += 1
        st = self.rst[s][:, 0:ncols]
        self.dma(st, Wrows, [], [("rst", s)], ("rst", s))
        self.P.op(POOL, lambda e: e.tensor_copy(out=dest, in_=st), reads=[("rst", s)], writes=[dreg])

    def mm(self, out, outreg, pairs, reads):
        n = len(pairs)

        def emit(e):
            for i, (l, r) in enumerate(pairs):
                ins = e.matmul(out, l, r, start=(i == 0), stop=(i == n - 1))
            return ins
        self.P.op(PE, emit, reads=reads, writes=[outreg])

    def pe_fill(self, n, bank):
        if n <= 0:
            return

        def emit(e):
            for _ in range(n):
                ins = e.matmul(self.ps[bank][:, :], self.identb, self.A[:, 0, 0:512], start=True, stop=True)
            return ins
        self.P.op(PE, emit, reads=["cbf"], writes=[("ps", bank)])

    def act(self, out, in_, func, reads, writes, scale=None, bias=None):
        kw = {}
        if scale is not None:
            kw["scale"] = scale
        if bias is not None:
            kw["bias"] = bias
        self.P.op(ACT, lambda e: e.activation(out=out, in_=in_, func=func, **kw), reads=reads, writes=writes, n=fsz(out))

    def tt(self, eng, out, in0, in1, op, reads, writes):
        self.P.op(eng, lambda e: e.tensor_tensor(out=out, in0=in0, in1=in1, op=op), reads=reads, writes=writes, n=fsz(out))

    def ts(self, eng, out, in0, s1, s2, op0, op1, reads, writes):
        if s2 is None:
            self.P.op(eng, lambda e: e.tensor_scalar(out=out, in0=in0, scalar1=s1, scalar2=None, op0=op0), reads=reads, writes=writes, n=fsz(out))
        else:
            self.P.op(eng, lambda e: e.tensor_scalar(out=out, in0=in0, scalar1=s1, scalar2=s2, op0=op0, op1=op1), reads=reads, writes=writes, n=fsz(out))

    def stt(self, eng, out, in0, scalar, in1, op0, op1, reads, writes):
        self.P.op(eng, lambda e: e.scalar_tensor_tensor(out=out, in0=in0, scalar=scalar, in1=in1, op0=op0, op1=op1), reads=reads, writes=writes, n=fsz(out))

    def cp(self, eng, out, in_, reads, writes):
        self.P.op(eng, lambda e: e.tensor_copy(out=out, in_=in_), reads=reads, writes=writes, n=fsz(out))

    def memset(self, eng, ap, val, writes):
        self.P.op(eng, lambda e: e.memset(ap, val), writes=writes, n=fsz(ap))

    def vec(self, l, j, c):
        o = l * NVEC_PER_L + j * 8 + c
        return self.vecs[:, o:o + 1]

    def proj(self, bank, wb, wreg, t, K=8, src=None, sreg=None, ntok=512):
        pairs = []
        reads = [wreg]
        for k in range(K):
            pairs.append((wb[:, k, :], self.A[:, k, t * ntok:(t + 1) * ntok]))
            reads.append(("A", k, (t * ntok) // 512))
        self.mm(self.ps[bank][:, 0:ntok], ("ps", bank), pairs, reads)

    def prologue(self):
        P = self.P
        for nm, dst, src in (("idf", self.idf, self.c_idf), ("onesD", self.onesD, self.c_onesD), ("reset", self.reset, self.c_reset),
                             ("cbf", self.cbf, self.c_bf), ("rc", self.rc, self.c_rc), ("vecs", self.vecs, self.vecs_d)):
            self.dma(dst[:], src, [], [nm], nm)
        self.memset(DVE, self.misc[:], 0.0, ["misc"])
        self.memset(DVE, self.misc[:, 0:1], LN_EPS, ["misc"])
        self.memset(DVE, self.misc[:, 1:2], RMS_EPS, ["misc"])
        self.memset(DVE, self.misc[:, 2:3], 1.0, ["misc"])
        E = self.misc[:, 8:8 + 16 * DEPTH]
        self.act(E, self.vecs[:, VEC_LB:VEC_LB + 16 * DEPTH], AF.Exp, ["vecs"], ["misc"])
        ssum = self.lb[:, 0:16]
        self.tt(DVE, ssum, E[:, 0:16], E[:, 16:32], ALU.add, ["misc"], ["lb"])
        for l in range(2, DEPTH):
            self.tt(DVE, ssum, ssum, E[:, l * 16:(l + 1) * 16], ALU.add, ["misc", "lb"], ["lb"])
        self.P.op(DVE, lambda e: e.reciprocal(out=ssum, in_=ssum), reads=["lb"], writes=["lb"])
        for l in range(1, DEPTH):
            self.tt(DVE, E[:, l * 16:(l + 1) * 16], E[:, l * 16:(l + 1) * 16], ssum, ALU.mult, ["misc", "lb"], ["misc"])
        self.memset(DVE, self.lb[:, 0:16], 0.0, ["lb"])
        for l in range(1, DEPTH):
            self.tt(DVE, self.lb[:, l * 16:(l + 1) * 16], self.lb[:, (l - 1) * 16:l * 16], E[:, l * 16:(l + 1) * 16], ALU.add, ["misc", "lb"], ["lb"])
        self.ts(DVE, self.oml[:], self.lb[:], -1.0, 1.0, ALU.mult, ALU.add, ["lb"], ["oml"])
        self.act(self.lnoml[:], self.oml[:], AF.Ln, ["oml"], ["lnoml"])
        xin = [self.fa(i * 1024, 1024) for i in range(2)]
        xr = [self.fa(2048 + i * 1024, 1024).rearrange("p (c t) -> p c t", t=128) for i in range(2)]
        import os
        for tt_ in range(int(os.environ.get('KDBG_NX', '32'))):
            s = tt_ % 2
            self.dma(xin[s], self.x[tt_ * 128:(tt_ + 1) * 128, :], [], [("xin", s)], ("xin", s))
            for hf in range(2):
                bank = (tt_ * 2 + hf) % 4

                def emit(e, s=s, hf=hf, bank=bank):
                    for j in range(4):
                        c = hf * 4 + j
                        ins = e.transpose(out=self.ps[bank][:, j * 128:(j + 1) * 128], in_=xin[s][:, c * 128:(c + 1) * 128], identity=self.idf[:])
                    return ins
                P.op(PE, emit, reads=[("xin", s), "idf"], writes=[("ps", bank)])
                psv = self.ps[bank][:, :].rearrange("p (c t) -> p c t", t=128)
                self.cp(DVE, self.A[:, hf * 4:(hf + 1) * 4, tt_ * 128:(tt_ + 1) * 128], psv, [("ps", bank)], [("A", k, tt_ // 4) for k in range(hf * 4, hf * 4 + 4)])
                self.act(xr[s][:, hf * 4:(hf + 1) * 4, :], psv, AF.Copy, [("ps", bank)], [("xr", s)])
            dst = self.XRES.rearrange("c p t -> p c t")[:, :, tt_ * 128:(tt_ + 1) * 128]
            self.dma(dst, xr[s], [("xr", s)], [("XRES", tt_ // 4)], ("xrst", s))
        P.barrier()

    def stage_pool(self, L):
        P = self.P
        W = self.w_in[L]
        apad = self.fa(0, 4128)
        sa = self.fa(4128, 2112)
        sb_ = self.fa(4128 + 2112, 2112)
        tmpe = self.fa(4128 + 4224, 16)
        mixed = [self.ba(i * 4096, 4096) for i in range(2)]
        pwb = self.ba(8192, 512).rearrange("p (k c) -> p k c", c=256)
        pws = self.fa(4128 + 4224 + 16, 512).rearrange("p (k c) -> p k c", c=256)
        otile = [self.ba(8704 + i * 512, 512) for i in range(2)]
        self.memset(POOL, apad, 0.0, ["apad"])
        oi = 0
        nxtw = self.load_w(W, 8, OFF_POOL)
        for g in range(4):
            h = 1 << g
            for cc in range(2):
                c = 2 * g + cc
                wb, wreg = nxtw
                if c + 1 < 8:
                    nxtw = self.load_w(W, 8, OFF_POOL + (c + 1) * 128)
                for t in range(8):
                    bank = t % 4
                    self.proj(bank, wb, wreg, t)
                    self.act(apad[:, 16 + t * 512:16 + (t + 1) * 512], self.ps[bank][:, :], AF.Copy, [("ps", bank)], ["apad"])
                for hf in range(2):
                    j0 = 16 + hf * 2048 - 16
                    a = apad[:, j0:j0 + 2080]
                    self.tt(DVE, sa[:, 1:2080], a[:, 0:2079], a[:, 1:2080], ALU.add, ["apad"], ["sa"])
                    fin = sa
                    if g >= 1:
                        self.tt(POOL, sb_[:, 2:2079], sa[:, 1:2078], sa[:, 3:2080], ALU.add, ["sa"], ["sb"])
                        fin = sb_
                    if g >= 2:
                        self.tt(DVE, sa[:, 4:2077], sb_[:, 2:2075], sb_[:, 6:2079], ALU.add, ["sb"], ["sa"])
                        fin = sa
                    if g >= 3:
                        self.tt(POOL, sb_[:, 8:2073], sa[:, 4:2069], sa[:, 12:2077], ALU.add, ["sa"], ["sb"])
                        fin = sb_
                    freg = "sa" if fin is sa else "sb"
                    mo = mixed[cc][:, hf * 2048:(hf + 1) * 2048]
                    self.stt(DVE, mo, fin[:, 16:2064], 1.0 / (2 * h), a[:, 16:2064], ALU.mult, ALU.subtract, [freg, "apad"], [("mixed", cc)])
                    e0 = 0 if hf == 0 else 8
                    u0 = 16 if hf == 0 else 2064 - 8
                    self.tt(DVE, tmpe[:, e0:e0 + 8], fin[:, u0:u0 + 8], self.rc[:, g * 16 + e0:g * 16 + e0 + 8], ALU.mult, [freg, "rc"], ["tmpe"])
                    t0 = 0 if hf == 0 else S - 8
                    self.tt(DVE, mixed[cc][:, t0:t0 + 8], tmpe[:, e0:e0 + 8], a[:, u0:u0 + 8], ALU.subtract, ["tmpe", "apad"], [("mixed", cc)])
            src = self.pool_w[L, g].rearrange("(k p) c -> p k c", p=128)
            self.dma(pws, src, [], ["pws"], "pws")
            self.cp(POOL, pwb, pws, ["pws"], ["pwb"])
            for cc in range(2):
                c = 2 * g + cc
                for t in range(8):
                    bank = 4 + (t % 2)
                    pairs = [(pwb[:, k, cc * 128:(cc + 1) * 128], mixed[k][:, t * 512:(t + 1) * 512]) for k in range(2)]
                    self.mm(self.ps[bank][:, :], ("ps", bank), pairs, ["pwb", ("mixed", 0), ("mixed", 1)])
                    o = oi % 2
                    oi += 1
                    self.ts(DVE, otile[o], self.ps[bank][:, :], self.vec(L, 0, c), None, ALU.mult, None, [("ps", bank), "vecs"], [("otile", o)])
                    self.dma(self.OUTX[c, :, t * 512:(t + 1) * 512], otile[o], [("otile", o)], [("OUTX", c, t)], ("otile", o))
        P.barrier()

    def stage_attn(self, L):
        P = self.P
        W = self.w_in[L]
        num = self.fa(0, 4096)
        den = self.fa(4096, 4096)
        QT = self.ba(0, 4096)
        KTp = self.ba(4096, 6144)
        VTp = self.ba(10240, 6144)
        Vtok = self.ba(16384, 6144).rearrange("p (n c) -> p n c", c=128)
        PT = [self.ba(22528 + i * 256, 256) for i in range(2)]
        bia = [self.ba(23040 + i * 256, 256) for i in range(2)]
        otile = self.ba(23552, 4096)
        bi = 0
        pi = 0
        for j in range(4):
            for g in range(3):
                h = 4 * g + j
                d = ATT_D[g]
                Lr = S // d
                Lp = Lr + 128
                nblk = Lr // 128
                b = bi % 2
                bi += 1
                self.dma(bia[b], self.c_abias[h], [], [("bia", b)], ("bia", b))
                self.memset(POOL, KTp, 0.0, ["KTp"])
                self.memset(POOL, VTp, 0.0, ["VTp"])
                if j == 0 and g == 0:
                    anxt = [self.load_w(W, 8, o_ + h * 128) for o_ in (OFF_AQ, OFF_AK, OFF_AV)]
                (wq, rq), (wk, rk), (wv, rv) = anxt
                hn = (4 * (g + 1) + j) if g < 2 else (j + 1 if j < 3 else None)
                if hn is not None:
                    anxt = [self.load_w(W, 8, o_ + hn * 128) for o_ in (OFF_AQ, OFF_AK, OFF_AV)]
                n = 512 // d
                for t in range(8):
                    i0 = t * n
                    psv = lambda bank: self.ps[bank][:, :].rearrange("p (i r) -> p r i", r=d)
                    self.proj(0, wq, rq, t)
                    dq = QT[:, 0:d * Lr].rearrange("p (r l) -> p r l", r=d)[:, :, i0:i0 + n]
                    self.act(dq, psv(0), AF.Copy, [("ps", 0)], ["QT"])
                    self.proj(1, wk, rk, t)
                    dk = KTp[:, 0:d * Lp].rearrange("p (r l) -> p r l", r=d)[:, :, 64 + i0:64 + i0 + n]
                    self.cp(DVE, dk, psv(1), [("ps", 1)], ["KTp"])
                    self.proj(2, wv, rv, t)
                    dv = VTp[:, 0:d * Lp].rearrange("p (r l) -> p r l", r=d)[:, :, 64 + i0:64 + i0 + n]
                    self.act(dv, psv(2), AF.Copy, [("ps", 2)], ["VTp"])
                ntile = d * (nblk + 1)
                tiles = [(r, i) for r in range(d) for i in range(nblk + 1)]
                for q0 in range(0, ntile, 8):
                    grp = tiles[q0:q0 + 8]

                    def emit(e, grp=grp, Lp=Lp):
                        for jj, (r, i) in enumerate(grp):
                            ins = e.transpose(out=self.psb[:, jj * 128:(jj + 1) * 128], in_=VTp[:, r * Lp + 128 * i:r * Lp + 128 * i + 128], identity=self.identb)
                        return ins
                    P.op(PE, emit, reads=["VTp", "cbf"], writes=["psb"])
                    ng = len(grp)
                    self.cp(DVE, Vtok[:, q0:q0 + ng, :], self.psb[:, 0:ng * 128].rearrange("p (n c) -> p n c", c=128), ["psb"], ["Vtok"])
                gq = min(4, nblk)
                blocks = [(r, m0, jb) for r in range(d) for m0 in range(0, nblk, gq) for jb in range(gq)]

                def scores(i, b=b, Lp=Lp, Lr=Lr):
                    r, m0, jb = blocks[i]
                    m = m0 + jb
                    sbk = 3 + (i % 2)
                    for side in range(2):
                        kt = KTp[:, r * Lp + 128 * (m + side):r * Lp + 128 * (m + side) + 128]
                        qb = QT[:, r * Lr + 128 * m:r * Lr + 128 * m + 128]
                        pairs = [(kt, qb), (self.identb, bia[b][:, side * 128:(side + 1) * 128])]
                        self.mm(self.ps[sbk][:, side * 128:(side + 1) * 128], ("ps", sbk), pairs, ["KTp", "QT", "cbf", ("bia", b)])
                    self.act(PT[i % 2], self.ps[sbk][:, 0:256], AF.Exp, [("ps", sbk)], [("PT", i % 2)], scale=float(128 ** -0.5))

                def pvden(i, g=g, d=d, nblk=nblk, gq=gq):
                    r, m0, jb = blocks[i]
                    m = m0 + jb
                    pt = PT[i % 2]
                    ptr = ("PT", i % 2)
                    pv = []
                    dn = []
                    for side in range(2):
                        ti = m + side
                        vt = Vtok[:, r * (nblk + 1) + ti, :]
                        val = self.lo0 if ti == 0 else (self.hi0 if ti == nblk else self.onesb)
                        pv.append((vt, pt[:, side * 128:(side + 1) * 128]))
                        dn.append((val, pt[:, side * 128:(side + 1) * 128]))
                    self.mm(self.ps[5][:, jb * 128:(jb + 1) * 128], ("ps", 5), pv, ["Vtok", ptr])
                    self.mm(self.ps[6][:, jb * 128:(jb + 1) * 128], ("ps", 6), dn, ["cbf", ptr])
                    if jb == gq - 1:
                        nn = gq * 128
                        nv = num.rearrange("p (i r) -> p r i", r=d)[:, r, m0 * 128:m0 * 128 + nn]
                        dvv = den.rearrange("p (i r) -> p r i", r=d)[:, r, m0 * 128:m0 * 128 + nn]
                        if g == 0:
                            self.cp(DVE, nv, self.ps[5][:, 0:nn], [("ps", 5)], ["num"])
                            self.act(dvv, self.ps[6][:, 0:nn], AF.Copy, [("ps", 6)], ["den"])
                        else:
                            self.tt(DVE, nv, self.ps[5][:, 0:nn], nv, ALU.add, [("ps", 5), "num"], ["num"])
                            self.tt(DVE, dvv, self.ps[6][:, 0:nn], dvv, ALU.add, [("ps", 6), "den"], ["den"])

                scores(0)
                for i in range(len(blocks)):
                    if i + 1 < len(blocks):
                        scores(i + 1)
                    pvden(i)
            P.op(DVE, lambda e: e.reciprocal(out=den, in_=den), reads=["den"], writes=["den"])
            self.tt(DVE, otile, num, den, ALU.mult, ["num", "den"], ["aotile"])
            self.dma(self.OUTX[8 + j, :, :], otile, ["aotile"], [("OUTX", 8 + j, t) for t in range(8)], "aotile")
        P.barrier()

    def stage_hgrn(self, L):
        P = self.P
        W = self.w_in[L]
        fofs = [0]
        bofs = [0]

        def falloc(n):
            a = self.fa(fofs[0], n)
            fofs[0] += n
            return a

        def balloc(n):
            a = self.ba(bofs[0], n)
            bofs[0] += n
            return a
        CB = []
        for dr in range(2):
            b = dict(qraw=falloc(512), zraw=falloc(512), sf=falloc(512), lg=falloc(512), Bn=falloc(512), bc=falloc(512), tq=falloc(512),
                     S32=falloc(9 * 128).rearrange("p (n c) -> p n c", c=128), DEC=falloc(128)[:, 0:8],
                     QE=balloc(512), KK=balloc(512), KDT=balloc(512), VT=balloc(512),
                     Vtok=balloc(512).rearrange("p (n c) -> p n c", c=128), KDtok=balloc(512).rearrange("p (n c) -> p n c", c=128),
                     attm=balloc(512).rearrange("p (n c) -> p n c", c=128), Sbf=balloc(9 * 128).rearrange("p (n c) -> p n c", c=128),
                     O=balloc(4096))
            b["S32_flat"] = b["S32"].rearrange("p n c -> p (n c)")
            b["Sbf_flat"] = b["Sbf"].rearrange("p n c -> p (n c)")
            CB.append(b)
        og = falloc(512)
        rs = falloc(512)
        gsl = falloc(512)
        osq = balloc(512)
        otl = [balloc(512) for _ in range(2)]
        wsl2 = [[balloc(1024).rearrange("p (k c) -> p k c", c=128) for _ in range(5)] for _ in range(2)]
        wsl = list(wsl2[0])
        hwset = [0]
        one_c = self.misc[:, 2:3]

        def chain(h, dr):
            b = CB[dr]
            R = lambda nm: (nm, dr)
            col = L * 16 + dr * 8 + h
            lbc = self.lb[:, col:col + 1]
            lnoml = self.lnoml[:, col:col + 1]
            mask = self.maskF if dr == 0 else self.maskB
            qraw, zraw, sf, lg, Bn, bc, tq, S32, DEC = b["qraw"], b["zraw"], b["sf"], b["lg"], b["Bn"], b["bc"], b["tq"], b["S32"], b["DEC"]
            QE, KK, KDT, VT, Vtok, KDtok, attm, Sbf, O = b["QE"], b["KK"], b["KDT"], b["VT"], b["Vtok"], b["KDtok"], b["attm"], b["Sbf"], b["O"]
            self.memset(DVE, S32[:, 8, :], 0.0, [R("S32")])
            self.memset(POOL, Sbf[:, 8, :], 0.0, [R("Sbf")])
            for t in (range(8) if dr == 0 else range(7, -1, -1)):
                hs = hwset[0]
                self.proj(0, wsl[0], ("hw", hs, 0), t)
                self.proj(1, wsl[1], ("hw", hs, 1), t)
                self.proj(2, wsl[2 + dr], ("hw", hs, 2 + dr), t)
                self.act(tq, self.ps[0][:, :], AF.Exp, [("ps", 0)], [R("tq")], scale=-1.0)
                self.cp(DVE, qraw, self.ps[0][:, :], [("ps", 0)], [R("qraw")])
                self.act(VT, self.ps[1][:, :], AF.Copy, [("ps", 1)], [R("VT")])
                self.act(sf, self.ps[2][:, :], AF.Exp, [("ps", 2)], [R("sf")], scale=-1.0)
                self.cp(DVE, zraw, self.ps[2][:, :], [("ps", 2)], [R("zraw")])

                def emitv(e):
                    for jj in range(4):
                        ins = e.transpose(out=self.psb[:, jj * 128:(jj + 1) * 128], in_=VT[:, jj * 128:(jj + 1) * 128], identity=self.identb)
                    return ins
                P.op(PE, emitv, reads=[R("VT"), "cbf"], writes=["psb"])
                self.cp(DVE, Vtok, self.psb[:, 0:512].rearrange("p (n c) -> p n c", c=128), ["psb"], [R("Vtok")])
                self.pe_fill(FILL1, 3)
                yield
                self.act(lg, sf, AF.Ln, [R("sf"), "lb", "misc"], [R("lg")], scale=lbc, bias=one_c)
                self.act(Bn, sf, AF.Ln, [R("sf"), "misc"], [R("Bn")], bias=one_c)
                self.act(tq, tq, AF.Ln, [R("tq"), "misc"], [R("tq")], bias=one_c)
                self.tt(DVE, lg, lg, Bn, ALU.subtract, [R("lg"), R("Bn")], [R("lg")])
                P.op(DVE, lambda e: e.tensor_tensor_scan(out=bc, data0=self.reset[:], data1=lg, initial=0.0, op0=ALU.mult, op1=ALU.add),
                     reads=[R("lg"), "reset"], writes=[R("bc")], n=512)
                bcv = bc.rearrange("p (n c) -> p n c", c=CH)
                if dr == 1:
                    self.tt(POOL, lg, lg, bc, ALU.subtract, [R("lg"), R("bc")], [R("lg")])
                    self.tt(DVE, bcv, lg.rearrange("p (n c) -> p n c", c=CH), bcv[:, :, CH - 1:CH].broadcast_to([128, 8, CH]), ALU.add, [R("lg"), R("bc")], [R("bc")])
                yield
                dsel = CH - 1 if dr == 0 else 0
                bend = bcv[:, :, dsel:dsel + 1]
                self.act(DEC.rearrange("p (n c) -> p n c", c=1), bend, AF.Exp, [R("bc")], [R("DEC")])
                self.tt(POOL, tq, bc, tq, ALU.subtract, [R("bc"), R("tq")], [R("tq")])
                self.act(tq, tq, AF.Exp, [R("tq")], [R("tq")])
                self.tt(DVE, QE, qraw, tq, ALU.mult, [R("qraw"), R("tq")], [R("QE")])
                self.tt(DVE, zraw, zraw, Bn, ALU.add, [R("zraw"), R("Bn")], [R("zraw")])
                self.tt(DVE, zraw, zraw, bc, ALU.add, [R("zraw"), R("bc")], [R("zraw")])
                self.act(KK, zraw, AF.Exp, [R("zraw"), "lnoml"], [R("KK")], scale=-1.0, bias=lnoml)
                self.tt(POOL, zraw.rearrange("p (n c) -> p n c", c=CH), zraw.rearrange("p (n c) -> p n c", c=CH),
                        bend.broadcast_to([128, 8, CH]), ALU.subtract, [R("zraw"), R("bc")], [R("zraw")])
                self.act(KDT, zraw, AF.Exp, [R("zraw"), "lnoml"], [R("KDT")], scale=-1.0, bias=lnoml)
                yield
                for blk in range(4):
                    bsl = slice(blk * 128, (blk + 1) * 128)
                    self.mm(self.ps[4][:, bsl], ("ps", 4), [(KK[:, bsl], QE[:, bsl])], [R("KK"), R("QE")])
                for blk in range(4):
                    bsl = slice(blk * 128, (blk + 1) * 128)
                    self.tt(DVE, attm[:, blk, :], self.ps[4][:, bsl], mask, ALU.mult, [("ps", 4), "cbf"], [R("attm")])

                def emitk(e):
                    for jj in range(4):
                        ins = e.transpose(out=self.psb[:, 512 + jj * 128:512 + (jj + 1) * 128], in_=KDT[:, jj * 128:(jj + 1) * 128], identity=self.identb)
                    return ins
                P.op(PE, emitk, reads=[R("KDT"), "cbf"], writes=["psb"])
                self.act(KDtok, self.psb[:, 512:1024].rearrange("p (n c) -> p n c", c=128), AF.Copy, ["psb"], [R("KDtok")])
                yield
                order = list(range(8)) if dr == 0 else list(range(7, -1, -1))
                for q, cpos in enumerate(order):
                    blk, c = cpos // 2, cpos % 2
                    pr = slice(c * CH, (c + 1) * CH)
                    bank = 5 + q % 2
                    self.mm(self.ps[bank][:, (q // 2) * 128:(q // 2 + 1) * 128], ("ps", bank), [(KDtok[pr, blk, :], Vtok[pr, blk, :])], [R("KDtok"), R("Vtok")])
                self.cp(DVE, S32[:, 0, :], S32[:, 8, :], [R("S32")], [R("S32")])
                self.cp(POOL, Sbf[:, 0, :], Sbf[:, 8, :], [R("Sbf")], [R("Sbf")])
                for q, cpos in enumerate(order):
                    bank = 5 + q % 2
                    self.stt(DVE, S32[:, q + 1, :], S32[:, q, :], DEC[:, cpos:cpos + 1], self.ps[bank][:, (q // 2) * 128:(q // 2 + 1) * 128],
                             ALU.mult, ALU.add, [R("S32"), R("DEC"), ("ps", bank)], [R("S32")])
                self.act(b["Sbf_flat"][:, 128:1152], b["S32_flat"][:, 128:1152], AF.Copy, [R("S32")], [R("Sbf")])
                self.pe_fill(FILL2, 3)
                yield
                for q, cpos in enumerate(order):
                    blk, c = cpos // 2, cpos % 2
                    cs = slice(cpos * CH, (cpos + 1) * CH)
                    pairs = [(Vtok[:, blk, :], attm[:, blk, c * CH:(c + 1) * CH]), (Sbf[:, q, :], QE[:, cs])]
                    self.mm(self.ps[3][:, cs], ("ps", 3), pairs, [R("Vtok"), R("attm"), R("Sbf"), R("QE")])
                tsl = slice(t * 512, (t + 1) * 512)
                self.act(O[:, tsl], self.ps[3][:, :], AF.Copy, [("ps", 3)], [("O", dr, t)])
                yield ("done", t)

        oi = [0]

        def finalize(h, t):
            tsl = slice(t * 512, (t + 1) * 512)
            self.proj(0, wsl[4], ("hw", hwset[0], 4), t)
            self.act(gsl, self.ps[0][:, :], AF.Exp, [("ps", 0)], ["gsl"], scale=-1.0)
            self.tt(DVE, og, CB[0]["O"][:, tsl], CB[1]["O"][:, tsl], ALU.add, [("O", 0, t), ("O", 1, t)], ["og"])
            self.act(osq, og, AF.Square, ["og"], ["osq"])
            self.mm(self.ps[1][:, :], ("ps", 1), [(self.onesb, osq)], ["cbf", "osq"])
            self.act(rs, self.ps[1][:, :], AF.Ln, [("ps", 1), "misc"], ["rs"], scale=1.0 / 128, bias=self.misc[:, 1:2])
            self.act(rs, rs, AF.Exp, ["rs"], ["rs"], scale=-0.5)
            self.act(gsl, gsl, AF.Ln, ["gsl", "misc"], ["gsl"], bias=one_c)
            self.act(gsl, gsl, AF.Exp, ["gsl"], ["gsl"], scale=-1.0)
            self.tt(DVE, og, og, rs, ALU.mult, ["og", "rs"], ["og"])
            self.tt(POOL, og, og, gsl, ALU.mult, ["og", "gsl"], ["og"])
            o = otl[oi[0] % 2]
            oreg = ("hotl", oi[0] % 2)
            oi[0] += 1
            self.stt(DVE, o, og, self.vec(L, 1, h), self.ps[0][:, :], ALU.mult, ALU.mult, ["og", ("ps", 0), "vecs"], [oreg])
            self.dma(self.OUTX[12 + h, :, tsl], o, [oreg], [("OUTX", 12 + h, t)], oreg)

        def load_hw(hh):
            st_ = hh % 2
            for wi_, off in enumerate((OFF_HQ, OFF_HI, OFF_HFF, OFF_HFB, OFF_HG)):
                s = self.wi % 2
                self.wi += 1
                st = self.wst[s][:, 0:8, :]
                self.dma(st, W[:, off + hh * 128:off + (hh + 1) * 128].rearrange("(k p) c -> p k c", p=128), [], [("wst", s)], ("wst", s))
                self.cp(POOL, wsl2[st_][wi_], st, [("wst", s)], [("hw", st_, wi_)])
        load_hw(0)
        for h in range(8):
            hwset[0] = h % 2
            wsl[:] = wsl2[h % 2]
            if h + 1 < 8:
                load_hw(h + 1)
            gens = [chain(h, 0), chain(h, 1)] if not os.environ.get('KDBG_ONECHAIN') else [chain(h, 0)]
            done = [set(), set()]
            alive = [True] * len(gens)
            for _ in range(int(os.environ.get('KSKEW', '3'))):
                next(gens[0])
            while any(alive):
                for gi, g in enumerate(gens):
                    if not alive[gi]:
                        continue
                    try:
                        r = next(g)
                    except StopIteration:
                        alive[gi] = False
                        continue
                    if r is not None:
                        t = r[1]
                        done[gi].add(t)
                        if len(gens) == 2 and t in done[1 - gi]:
                            finalize(h, t)
        P.barrier()

    def stage_merge(self, L, X):
        KX = (8, 4, 8)[X]
        c0 = (0, 8, 12)[X]
        Wb = self.ba(0, 8192).rearrange("p (k c) -> p k c", c=1024)
        Wg = self.ba(8192, 8192).rearrange("p (k c) -> p k c", c=1024)
        otl = [self.ba(16384 + i * 4096, 4096).rearrange("p (k t) -> p k t", t=512) for i in range(2)]
        gyt = [self.ba(24576 + i * 512, 512) for i in range(2)]
        sg = [self.fa(i * 512, 512) for i in range(2)]
        for k in range(KX):
            self.load_rows(self.w_br[X][L, k * 128:(k + 1) * 128, :], Wb[:, k, :], ("Wb", k))
        goff = OFF_GATE + X * 1024
        for k in range(8):
            self.load_rows(self.w_in[L, k * 128:(k + 1) * 128, goff:goff + 1024], Wg[:, k, :], ("Wg", k))
        gi = 0
        for t in range(8):
            o = t % 2
            tsl = slice(t * 512, (t + 1) * 512)
            self.dma(otl[o][:, 0:KX, :], self.OUTX[c0:c0 + KX, :, tsl].rearrange("c p t -> p c t"),
                     [("OUTX", c0 + k, t) for k in range(KX)], [("mo", o)], ("mo", o))
            for c in range(8):
                by, bg = (0, 1) if c % 2 == 0 else (2, 3)
                csl = slice(c * 128, (c + 1) * 128)
                self.mm(self.ps[by][:, :], ("ps", by), [(Wb[:, k, csl], otl[o][:, k, :]) for k in range(KX)], [("Wb", k) for k in range(KX)] + [("mo", o)])
                self.mm(self.ps[bg][:, :], ("ps", bg), [(Wg[:, k, csl], self.A[:, k, tsl]) for k in range(8)], [("Wg", k) for k in range(8)] + [("A", k, t) for k in range(8)])
                s = gi % 2
                gi += 1
                self.act(sg[s], self.ps[bg][:, :], AF.Sigmoid, [("ps", bg)], [("sg", s)])
                self.tt(DVE, gyt[s], self.ps[by][:, :], sg[s], ALU.mult, [("ps", by), ("sg", s)], [("gyt", s)])
                self.dma(self.GY[X, c, :, tsl], gyt[s], [("gyt", s)], [("GY", X, c, t)], ("gyt", s), eng=ACT)
        self.P.barrier()

    def ln_tile(self, L, which, xn, xnreg, xsq, stat, xr, xreg, tok0, ntok):
        gj, bj = (2, 3) if which == 1 else (4, 5)
        mean, msq, rstd = stat
        self.act(xsq, xn, AF.Square, [xnreg], ["xsq"])
        self.mm(self.ps[4][:, 0:ntok], ("ps", 4), [(self.onesD[:], xn[:, c, :]) for c in range(8)], ["onesD", xnreg])
        self.mm(self.ps[5][:, 0:ntok], ("ps", 5), [(self.onesD[:], xsq[:, c, :]) for c in range(8)], ["onesD", "xsq"])
        self.cp(DVE, mean, self.ps[4][:, 0:ntok], [("ps", 4)], ["mean"])
        self.tt(DVE, msq, mean, mean, ALU.mult, ["mean"], ["msq"])
        self.tt(DVE, msq, self.ps[5][:, 0:ntok], msq, ALU.subtract, [("ps", 5), "msq"], ["msq"])
        self.act(rstd, msq, AF.Ln, ["msq", "misc"], ["rstd"], bias=self.misc[:, 0:1])
        self.act(rstd, rstd, AF.Exp, ["rstd"], ["rstd"], scale=-0.5)
        mb = mean.rearrange("p (o t) -> p o t", o=1).broadcast_to([128, 8, ntok])
        rb = rstd.rearrange("p (o t) -> p o t", o=1).broadcast_to([128, 8, ntok])
        self.tt(DVE, xn, xn, mb, ALU.subtract, [xnreg, "mean"], [xnreg])
        self.tt(DVE, xn, xn, rb, ALU.mult, [xnreg, "rstd"], [xnreg])
        for c in range(8):
            self.act(xr[:, c, :], xn[:, c, :], AF.Identity, [xnreg, "vecs"], [xreg], scale=self.vec(L, gj, c), bias=self.vec(L, bj, c))
        t512 = tok0 // 512
        self.cp(DVE, self.A[:, :, tok0:tok0 + ntok], xr, [xreg], [("A", k, t512) for k in range(8)])

    def stage_mix(self, L):
        NTK = 256
        NT_ = S // NTK
        wo = self.ba(0, 8192).rearrange("p (k c) -> p k c", c=1024)
        gy3 = self.ba(8192, 3 * 2048).rearrange("p (x k t) -> p x k t", x=3, k=8)
        mg = self.ba(8192 + 6144, 2048).rearrange("p (k t) -> p k t", t=NTK)
        xr = [self.fa(i * 2048, 2048).rearrange("p (k t) -> p k t", t=NTK) for i in range(2)]
        xn = [self.fa(4096 + i * 2048, 2048).rearrange("p (k t) -> p k t", t=NTK) for i in range(2)]
        xsq = self.fa(8192, 2048).rearrange("p (k t) -> p k t", t=NTK)
        stat = [self.fa(10240 + i * NTK, NTK) for i in range(3)]
        tmpm = self.AF_[:, 0:2048].rearrange("p (k t) -> p k t", t=NTK)
        for k in range(8):
            self.load_rows(self.w_out[L, k * 128:(k + 1) * 128, :], wo[:, k, :], ("wo", k))

        def front(t):
            sl = t % 2
            tsl = slice(t * NTK, (t + 1) * NTK)
            t5 = (t * NTK) // 512
            for X in range(3):
                self.dma(gy3[:, X], self.GY[X, :, :, tsl].rearrange("c p t -> p c t"), [("GY", X, c, t5) for c in range(8)], [("gy3", X)], ("gy3", X))
            self.dma(xr[sl], self.XRES[:, :, tsl].rearrange("c p t -> p c t"), [("XRES", t5)], [("xr", sl)], ("xr_ld", sl))
            self.tt(POOL, tmpm, gy3[:, 0], gy3[:, 1], ALU.add, [("gy3", 0), ("gy3", 1)], ["tmpm"])
            self.tt(POOL, mg, tmpm, gy3[:, 2], ALU.add, ["tmpm", ("gy3", 2)], ["mg"])
            for c in range(8):
                bank = c % 4
                self.mm(self.ps[bank][:, 0:NTK], ("ps", bank), [(wo[:, k, c * 128:(c + 1) * 128], mg[:, k, :]) for k in range(8)], [("wo", k) for k in range(8)] + ["mg"])
                self.stt(DVE, xn[sl][:, c, :], xr[sl][:, c, :], ALPHA, self.ps[bank][:, 0:NTK], ALU.mult, ALU.add, [("xr", sl), ("ps", bank)], [("xn", sl)])

        def back(t):
            sl = t % 2
            tsl = slice(t * NTK, (t + 1) * NTK)
            t5 = (t * NTK) // 512
            self.ln_tile(L, 1, xn[sl], ("xn", sl), xsq, stat, xr[sl], ("xr", sl), t * NTK, NTK)
            self.dma(self.XRES[:, :, tsl].rearrange("c p t -> p c t"), xr[sl], [("xr", sl)], [("XRES", t5)], ("xr_st", sl), eng=ACT)

        front(0)
        for t in range(NT_):
            if t + 1 < NT_:
                front(t + 1)
            back(t)
        self.P.barrier()

    def stage_ffn_up(self, L):
        P = self.P
        ht = [self.ba(i * 512, 512) for i in range(2)]
        sgl = [self.fa(i * 512, 512) for i in range(2)]
        hi_ = 0
        nxt = (self.load_w(self.w_fg[L], 8, 0), self.load_w(self.w_fu[L], 8, 0))
        for f in range(NF):
            (wg, rg), (wu, ru) = nxt
            if f + 1 < NF:
                nxt = (self.load_w(self.w_fg[L], 8, (f + 1) * 128), self.load_w(self.w_fu[L], 8, (f + 1) * 128))
            for t in range(8):
                bg, bu = (0, 1) if t % 2 == 0 else (2, 3)
                self.proj(bg, wg, rg, t)
                self.proj(bu, wu, ru, t)
                s = hi_ % 2
                hi_ += 1
                self.act(sgl[s], self.ps[bg][:, :], AF.Silu, [("ps", bg)], [("sgl", s)])
                self.tt(DVE, ht[s], self.ps[bu][:, :], sgl[s], ALU.mult, [("ps", bu), ("sgl", s)], [("ht", s)])
                self.dma(self.HH[f, :, t * 512:(t + 1) * 512], ht[s], [("ht", s)], [("HH", f, t)], ("ht", s))
        PT = self.ba(1024, 8192).rearrange("p (k t) -> p k t", k=2)
        pin = [self.fa(1024 + i * 256, 256) for i in range(2)]
        for tt_ in range(32):
            s = tt_ % 2
            self.dma(pin[s], self.p[L, tt_ * 128:(tt_ + 1) * 128, :], [], [("pin", s)], ("pin", s))
            bank = 4 + s

            def emit(e, s=s, bank=bank):
                for j in range(2):
                    ins = e.transpose(out=self.ps[bank][:, j * 128:(j + 1) * 128], in_=pin[s][:, j * 128:(j + 1) * 128], identity=self.idf[:])
                return ins
            P.op(PE, emit, reads=[("pin", s), "idf"], writes=[("ps", bank)])
            self.cp(DVE, PT[:, :, tt_ * 128:(tt_ + 1) * 128], self.ps[bank][:, 0:256].rearrange("p (c t) -> p c t", t=128), [("ps", bank)], ["PT"])
        wppb2 = [self.ba(9216 + i * 256, 256).rearrange("p (k c) -> p k c", c=128) for i in range(2)]
        wpps2 = [self.fa(1536 + i * 256, 256).rearrange("p (k c) -> p k c", c=128) for i in range(2)]

        def load_ple(c):
            i = c % 2
            self.dma(wpps2[i], self.w_pp[L][:, c * 128:(c + 1) * 128].rearrange("(k p) c -> p k c", p=128), [], [("wpps", i)], ("wpps", i))
            self.cp(POOL, wppb2[i], wpps2[i], [("wpps", i)], [("wppb", i)])
            return (wppb2[i], ("wppb", i)), self.load_w(self.w_pg[L], 8, c * 128)
        nxt = load_ple(0)
        for c in range(8):
            (wppb, rpp), (wpg, rpg) = nxt
            if c + 1 < 8:
                nxt = load_ple(c + 1)
            for t in range(8):
                bp, bg = (0, 1) if t % 2 == 0 else (2, 3)
                tsl = slice(t * 512, (t + 1) * 512)
                self.mm(self.ps[bp][:, :], ("ps", bp), [(wppb[:, k, :], PT[:, k, tsl]) for k in range(2)], [rpp, "PT"])
                self.proj(bg, wpg, rpg, t)
                s = hi_ % 2
                hi_ += 1
                self.act(sgl[s], self.ps[bg][:, :], AF.Sigmoid, [("ps", bg)], [("sgl", s)])
                self.tt(DVE, ht[s], self.ps[bp][:, :], sgl[s], ALU.mult, [("ps", bp), ("sgl", s)], [("ht", s)])
                self.dma(self.PLE[c, :, tsl], ht[s], [("ht", s)], [("PLE", c, t)], ("ht", s))
        P.barrier()

    def stage_ffn_down(self, L, last):
        P = self.P
        NTK = 256
        NT_ = S // NTK
        bb = lambda off, n: self.AB_[:, 4096 + off:4096 + off + n]
        Wd = bb(0, NF * 1024).rearrange("p (k c) -> p k c", c=1024)
        ht = [bb(NF * 1024, NF * NTK).rearrange("p (k t) -> p k t", t=NTK)]
        plt = bb(NF * 1024 + NF * NTK, 8 * NTK).rearrange("p (k t) -> p k t", t=NTK)
        xr = [self.fa(i * 2048, 2048).rearrange("p (k t) -> p k t", t=NTK) for i in range(2)]
        xn = [self.fa(4096 + i * 2048, 2048).rearrange("p (k t) -> p k t", t=NTK) for i in range(2)]
        xsq = self.fa(8192, 2048).rearrange("p (k t) -> p k t", t=NTK)
        stat = [self.fa(10240 + i * NTK, NTK) for i in range(3)]
        otok = self.AF_[:, 0:2048]
        for k in range(NF):
            self.load_rows(self.w_fd[L, k * 128:(k + 1) * 128, :], Wd[:, k, :], ("Wd", k))
        if last:
            P.barrier()

        def front(t):
            sl = t % 2
            tsl = slice(t * NTK, (t + 1) * NTK)
            t5 = (t * NTK) // 512
            self.dma(ht[0], self.HH[:, :, tsl].rearrange("c p t -> p c t"), [("HH", f, t5) for f in range(NF)], ["htd"], "htd")
            self.dma(plt, self.PLE[:, :, tsl].rearrange("c p t -> p c t"), [("PLE", c, t5) for c in range(8)], ["plt"], "plt")
            self.dma(xr[sl], self.XRES[:, :, tsl].rearrange("c p t -> p c t"), [("XRES", t5)], [("xr", sl)], ("xr_ld", sl))
            for c in range(8):
                bank = c % 4
                self.mm(self.ps[bank][:, 0:NTK], ("ps", bank), [(Wd[:, k, c * 128:(c + 1) * 128], ht[0][:, k, :]) for k in range(NF)], [("Wd", k) for k in range(NF)] + ["htd"])
                self.stt(DVE, xn[sl][:, c, :], xr[sl][:, c, :], ALPHA, self.ps[bank][:, 0:NTK], ALU.mult, ALU.add, [("xr", sl), ("ps", bank)], [("xn", sl)])
            self.tt(POOL, xn[sl], xn[sl], plt, ALU.add, [("xn", sl), "plt"], [("xn", sl)])

        def back(t):
            sl = t % 2
            tsl = slice(t * NTK, (t + 1) * NTK)
            t5 = (t * NTK) // 512
            self.ln_tile(L, 2, xn[sl], ("xn", sl), xsq, stat, xr[sl], ("xr", sl), t * NTK, NTK)
            if not last:
                self.dma(self.XRES[:, :, tsl].rearrange("c p t -> p c t"), xr[sl], [("xr", sl)], [("XRES", t5)], ("xr_st", sl), eng=ACT)
            else:
                for tb in range(NTK // 128):
                    for hf in range(2):
                        bank = 6

                        def emit(e, tb=tb, hf=hf, bank=bank, sl=sl):
                            for j in range(4):
                                ins = e.transpose(out=self.ps[bank][:, j * 128:(j + 1) * 128], in_=xr[sl][:, hf * 4 + j, tb * 128:(tb + 1) * 128], identity=self.idf[:])
                            return ins
                        P.op(PE, emit, reads=[("xr", sl), "idf"], writes=[("ps", bank)])
                        self.act(otok[:, tb * 1024 + hf * 512:tb * 1024 + (hf + 1) * 512], self.ps[bank][:, :], AF.Copy, [("ps", bank)], [("otok", tb)])
                    self.dma(self.out[t * NTK + tb * 128:t * NTK + (tb + 1) * 128, :], otok[:, tb * 1024:(tb + 1) * 1024], [("otok", tb)], [("out", t, tb)], ("ost", tb), eng=ACT)

        front(0)
        for t in range(NT_):
            if t + 1 < NT_:
                front(t + 1)
            back(t)
        P.barrier()

    def dump(self, slot, src_dram_bf16_rows):
        pass

    def build(self, stages=None):
        self.declare()
        with self.stack:
            self.prologue()
            for L in range(self.n_layers):
                last = L == self.n_layers - 1
                on = lambda nm: stages is None or nm in stages
                if on("pool"):
                    self.stage_pool(L)
                if on("merge0"):
                    self.stage_merge(L, 0)
                if on("attn"):
                    self.stage_attn(L)
                if on("merge1"):
                    self.stage_merge(L, 1)
                if on("hgrn"):
                    self.stage_hgrn(L)
                if on("merge2"):
                    self.stage_merge(L, 2)
                if on("mix"):
                    self.stage_mix(L)
                if on("ffn_up"):
                    self.stage_ffn_up(L)
                if on("ffn_down"):
                    self.stage_ffn_down(L, last)
            if self.dbg and not os.environ.get('KDBG_NODUMP'):
                self.dbg_dump()
            self.P.finalize(self.stack)
        return self.nc

    def dbg_dump(self):
        tb = self.ba(0, 4096)
        tf = self.fa(0, 4096)
        for c in range(20):
            self.dma(tb, self.OUTX[c], [("OUTX", c, t) for t in range(8)], ["dtb"], "dtb")
            self.cp(DVE, tf, tb, ["dtb"], ["dtf"])
            self.dma(self.dbg_out[c], tf, ["dtf"], [("dbg", c)], "dtf")


_CACHE = {}


def kernel(**inputs):
    consts = make_consts()
    vecs = pack_vecs(inputs)
    if "nc" not in _CACHE:
        _CACHE["nc"] = Builder().build()
    nc = _CACHE["nc"]
    shared = {k: np.ascontiguousarray(np.asarray(inputs[k], np.float32)) for k in (
        "w_in", "pool_w", "w_branch_a", "w_branch_b", "w_branch_c", "w_out", "w_ffn_gate", "w_ffn_up",
        "w_ffn_down", "w_ple_proj", "w_ple_gate")}
    shared.update(consts)
    shared["vecs"] = vecs
    x = np.asarray(inputs["x"], np.float32)
    p = np.asarray(inputs["p"], np.float32)
    in_maps = []
    for c in range(NCORES):
        m = dict(shared)
        m["x"] = np.ascontiguousarray(x[c])
        m["p"] = np.ascontiguousarray(p[:, c])
        in_maps.append(m)
    res = run_bass_kernel_spmd(nc, in_maps, core_ids=list(range(NCORES)))
    return np.stack([np.asarray(r["out"], np.float32) for r in res.results], axis=0)
```
